# Optimizing a Trainium2 kernel written in Bass

```python
import math
import jax, jax.numpy as jnp
from jax import lax
import numpy as np

D_MODEL = 1024
BATCH = 8
SEQ = 4096
DEPTH = 1

CHUNK = 64
D_MIX = D_MODEL
S5_WIDTH = D_MIX // 2
S5_GROUP_CH = 16
S5_GROUPS = S5_WIDTH // S5_GROUP_CH
S5_STATE = 64
CONV_WIDTH = D_MIX - S5_WIDTH
CONV_HEAD_DIM = 64
CONV_HEADS = CONV_WIDTH // CONV_HEAD_DIM
CONV_K = 31
D_FF = 2816
IN_COLS = S5_WIDTH + 2 * CONV_WIDTH
EPS = 1e-6

kernel_name = "hybrid_s5_conformer_conv_macaron"


def rms_norm(x, g):
    xf = x.astype(jnp.float32)
    y = xf * lax.rsqrt(jnp.mean(xf * xf, axis=-1, keepdims=True) + EPS)
    return (y * g.astype(jnp.float32)).astype(x.dtype)


def swiglu_ffn(h, w_gate, w_up, w_down):
    return (jax.nn.silu(h @ w_gate) * (h @ w_up)) @ w_down


def _complex_affine_combine(left, right):
    a_re_i, a_im_i, b_re_i, b_im_i = left
    a_re_j, a_im_j, b_re_j, b_im_j = right
    a_re = a_re_j * a_re_i - a_im_j * a_im_i
    a_im = a_re_j * a_im_i + a_im_j * a_re_i
    b_re = a_re_j * b_re_i - a_im_j * b_im_i + b_re_j
    b_im = a_re_j * b_im_i + a_im_j * b_re_i + b_im_j
    return (a_re, a_im, b_re, b_im)


def s5_mixer(u, lam_re, lam_im, log_dt, b_re, b_im, c_re, c_im, d_skip, w_glu, b_glu):
    bsz, seq_len, _ = u.shape
    uf = u.astype(jnp.float32).reshape(bsz, seq_len, S5_GROUPS, S5_GROUP_CH)
    lr = lam_re.astype(jnp.float32)
    li = lam_im.astype(jnp.float32)
    dt = jnp.exp(log_dt.astype(jnp.float32))[:, None]
    mag = jnp.exp(lr * dt)
    abar_re = mag * jnp.cos(li * dt)
    abar_im = mag * jnp.sin(li * dt)
    den = lr * lr + li * li
    num_re = abar_re - 1.0
    num_im = abar_im
    f_re = ((num_re * lr + num_im * li) / den)[..., None]
    f_im = ((num_im * lr - num_re * li) / den)[..., None]
    br = b_re.astype(jnp.float32)
    bi = b_im.astype(jnp.float32)
    bbar_re = f_re * br - f_im * bi
    bbar_im = f_re * bi + f_im * br
    bu_re = jnp.einsum('blgc,gpc->blgp', uf, bbar_re)
    bu_im = jnp.einsum('blgc,gpc->blgp', uf, bbar_im)
    a_re = jnp.broadcast_to(abar_re[None, None], (1, seq_len, S5_GROUPS, S5_STATE))
    a_im = jnp.broadcast_to(abar_im[None, None], (1, seq_len, S5_GROUPS, S5_STATE))
    _, _, s_re, s_im = lax.associative_scan(_complex_affine_combine,
                                            (a_re, a_im, bu_re, bu_im), axis=1)
    y = (jnp.einsum('blgp,gcp->blgc', s_re, c_re.astype(jnp.float32))
         - jnp.einsum('blgp,gcp->blgc', s_im, c_im.astype(jnp.float32)))
    y = y + d_skip.astype(jnp.float32).reshape(S5_GROUPS, S5_GROUP_CH) * uf
    y = jax.nn.gelu(y.reshape(bsz, seq_len, S5_WIDTH)).astype(u.dtype)
    return y * jax.nn.sigmoid(y @ w_glu + b_glu)


def conv_module_mixer(v, w_dw, b_dw, ln_g, ln_b):
    bsz, seq_len, _ = v.shape
    z = v[..., :CONV_WIDTH] * jax.nn.sigmoid(v[..., CONV_WIDTH:])
    z = lax.conv_general_dilated(
        z, w_dw[:, None, :], window_strides=(1,), padding=[(CONV_K - 1, 0)],
        dimension_numbers=('NWC', 'WIO', 'NWC'), feature_group_count=CONV_WIDTH) + b_dw
    zf = z.astype(jnp.float32).reshape(bsz, seq_len, CONV_HEADS, CONV_HEAD_DIM)
    mu = jnp.mean(zf, axis=-1, keepdims=True)
    var = jnp.mean(jnp.square(zf - mu), axis=-1, keepdims=True)
    zn = ((zf - mu) * lax.rsqrt(var + EPS)).reshape(bsz, seq_len, CONV_WIDTH)
    zn = zn * ln_g.astype(jnp.float32) + ln_b.astype(jnp.float32)
    return jax.nn.silu(zn).astype(v.dtype)


def setup_inputs(seed: int = 0) -> dict:
    key = jax.random.key(seed)
    ks = jax.random.split(key, 32)
    f32 = jnp.float32

    def nrm(k, shape, scale):
        return jax.random.normal(k, shape, f32) * scale

    def gain(k, shape):
        return 1.0 + 0.02 * jax.random.normal(k, shape, f32)

    L = DEPTH
    n_idx = jnp.arange(S5_STATE, dtype=f32)
    lam_re = -0.5 * (1.0 + 0.05 * jax.random.normal(ks[10], (L, S5_GROUPS, S5_STATE), f32))
    lam_im = jnp.broadcast_to(math.pi * n_idx, (L, S5_GROUPS, S5_STATE)) \
        + 0.01 * jax.random.normal(ks[11], (L, S5_GROUPS, S5_STATE), f32)
    log_dt = jax.random.uniform(ks[12], (L, S5_GROUPS), f32, math.log(1e-3), math.log(1e-1))
    return {
        "x": jax.random.normal(ks[0], (BATCH, SEQ, D_MODEL), f32),
        "ffn1_norm": gain(ks[1], (L, D_MODEL)),
        "ffn1_w_gate": nrm(ks[2], (L, D_MODEL, D_FF), D_MODEL ** -0.5),
        "ffn1_w_up": nrm(ks[3], (L, D_MODEL, D_FF), D_MODEL ** -0.5),
        "ffn1_w_down": nrm(ks[4], (L, D_FF, D_MODEL), D_FF ** -0.5),
        "mix_norm": gain(ks[5], (L, D_MODEL)),
        "w_in": nrm(ks[6], (L, D_MODEL, IN_COLS), D_MODEL ** -0.5),
        "s5_lam_re": lam_re,
        "s5_lam_im": lam_im,
        "s5_log_dt": log_dt,
        "s5_b_re": nrm(ks[13], (L, S5_GROUPS, S5_STATE, S5_GROUP_CH), (2 * S5_GROUP_CH) ** -0.5),
        "s5_b_im": nrm(ks[14], (L, S5_GROUPS, S5_STATE, S5_GROUP_CH), (2 * S5_GROUP_CH) ** -0.5),
        "s5_c_re": nrm(ks[15], (L, S5_GROUPS, S5_GROUP_CH, S5_STATE), (2 * S5_STATE) ** -0.5),
        "s5_c_im": nrm(ks[16], (L, S5_GROUPS, S5_GROUP_CH, S5_STATE), (2 * S5_STATE) ** -0.5),
        "s5_d": gain(ks[17], (L, S5_WIDTH)),
        "s5_w_glu": nrm(ks[18], (L, S5_WIDTH, S5_WIDTH), S5_WIDTH ** -0.5),
        "s5_b_glu": nrm(ks[19], (L, S5_WIDTH), 0.02),
        "conv_w_dw": nrm(ks[20], (L, CONV_K, CONV_WIDTH), CONV_K ** -0.5),
        "conv_b_dw": nrm(ks[21], (L, CONV_WIDTH), 0.02),
        "conv_ln_g": gain(ks[22], (L, CONV_WIDTH)),
        "conv_ln_b": nrm(ks[23], (L, CONV_WIDTH), 0.02),
        "w_out": nrm(ks[24], (L, D_MIX, D_MODEL), D_MIX ** -0.5),
        "ffn2_norm": gain(ks[25], (L, D_MODEL)),
        "ffn2_w_gate": nrm(ks[26], (L, D_MODEL, D_FF), D_MODEL ** -0.5),
        "ffn2_w_up": nrm(ks[27], (L, D_MODEL, D_FF), D_MODEL ** -0.5),
        "ffn2_w_down": nrm(ks[28], (L, D_FF, D_MODEL), D_FF ** -0.5),
        "final_norm": gain(ks[29], (D_MODEL,)),
    }


def reference(x, ffn1_norm, ffn1_w_gate, ffn1_w_up, ffn1_w_down, mix_norm, w_in,
              s5_lam_re, s5_lam_im, s5_log_dt, s5_b_re, s5_b_im, s5_c_re, s5_c_im,
              s5_d, s5_w_glu, s5_b_glu, conv_w_dw, conv_b_dw, conv_ln_g, conv_ln_b,
              w_out, ffn2_norm, ffn2_w_gate, ffn2_w_up, ffn2_w_down, final_norm):
    for l in range(DEPTH):
        h = rms_norm(x, ffn1_norm[l])
        x = x + 0.5 * swiglu_ffn(h, ffn1_w_gate[l], ffn1_w_up[l], ffn1_w_down[l])
        h = rms_norm(x, mix_norm[l])
        u = h @ w_in[l]
        y_s5 = s5_mixer(u[..., :S5_WIDTH], s5_lam_re[l], s5_lam_im[l], s5_log_dt[l],
                        s5_b_re[l], s5_b_im[l], s5_c_re[l], s5_c_im[l], s5_d[l],
                        s5_w_glu[l], s5_b_glu[l])
        y_conv = conv_module_mixer(u[..., S5_WIDTH:], conv_w_dw[l], conv_b_dw[l],
                                   conv_ln_g[l], conv_ln_b[l])
        x = x + jnp.concatenate([y_s5, y_conv], axis=-1) @ w_out[l]
        h = rms_norm(x, ffn2_norm[l])
        x = x + 0.5 * swiglu_ffn(h, ffn2_w_gate[l], ffn2_w_up[l], ffn2_w_down[l])
    return rms_norm(x, final_norm)
```

```python
import math
from contextlib import ExitStack
import numpy as np
import ml_dtypes
import concourse.bass as bass
import concourse.mybir as mybir
from concourse.bass_utils import run_bass_kernel_spmd

F32 = mybir.dt.float32
BF16 = mybir.dt.bfloat16
AF = mybir.ActivationFunctionType
ALU = mybir.AluOpType

D = 1024
FF = 2816
NF = 22
KD = 8
L = 4096
TT = 512
NB = TT // 128
NTT = TT // 512
NST = L // TT
T = 4
NC = TT // T
EPS = 1e-6
NSLOT = 3
PI = math.pi
ZW = 32 + TT + 4
RW = TT + 32


class Res:
    __slots__ = ("name", "w", "rd")

    def __init__(self, name):
        self.name = name
        self.w = None
        self.rd = []


class DSem:
    def __init__(self, sem):
        self.sem = sem
        self.count = 0


class Prog:
    ENG = ("scalar", "vector", "gpsimd", "tensor", "sync")

    def __init__(self, nc, stack):
        self.nc = nc
        self.stack = stack
        self.q = {e: [] for e in self.ENG}
        self.esem = {e: stack.enter_context(nc.semaphore("pc_" + e)) for e in self.ENG}
        self.ecnt = {e: 0 for e in self.ENG}
        self.waited = {}
        self.pend_r = {e: [] for e in self.ENG}
        self.pend_w = {e: [] for e in self.ENG}

    def dsem(self, name):
        return DSem(self.stack.enter_context(self.nc.semaphore(name)))

    def _wait(self, eng, ev):
        if ev is None:
            return
        sem, val, src = ev[0], ev[1], ev[2]
        if src == "dma":
            val = max(val, ev[3].count)
        key = (eng, sem.num)
        if self.waited.get(key, 0) >= val:
            return
        self.waited[key] = val
        self.q[eng].append(lambda e, sem=sem, val=val: e.wait_ge(sem, val))

    def _deps(self, eng, reads, writes):
        for r in reads:
            self._wait(eng, r.w)
        for w in writes:
            self._wait(eng, w.w)
            for ev in w.rd:
                if ev is not None and ev[2] == eng:
                    continue
                self._wait(eng, ev)

    def op(self, eng, fn, reads=(), writes=(), inc=True):
        self._deps(eng, reads, writes)
        if inc:
            self.ecnt[eng] += 1
            val = self.ecnt[eng]
            sem = self.esem[eng]
            self.q[eng].append(lambda e, fn=fn, sem=sem: fn(e).then_inc(sem, 1))
            ev = (sem, val, eng)
            for r in self.pend_r[eng]:
                r.rd.append(ev)
            for w in self.pend_w[eng]:
                w.w = ev
                w.rd = []
            self.pend_r[eng] = []
            self.pend_w[eng] = []
            for r in reads:
                r.rd.append(ev)
            for w in writes:
                w.w = ev
                w.rd = []
        else:
            self.q[eng].append(lambda e, fn=fn: fn(e))
            self.pend_r[eng].extend(reads)
            self.pend_w[eng].extend(writes)

    def dma(self, eng, out, in_, ds, reads=(), writes=()):
        self._deps(eng, reads, writes)
        ds.count += 16
        val = ds.count
        sem = ds.sem
        self.q[eng].append(lambda e, out=out, in_=in_, sem=sem: e.dma_start(out=out, in_=in_).then_inc(sem, 16))
        ev = (sem, val, "dma", ds)
        for r in reads:
            r.rd.append(ev)
        for w in writes:
            w.w = ev
            w.rd = []
        return ev

    def mm(self, out, lhsT, rhs, start=True, stop=True, reads=(), writes=(), inc=False, tp=None):
        def fn(e):
            kw = {"skip_group_check": True}
            if tp is not None:
                kw["tile_position"] = tp
            return e.matmul(out, lhsT, rhs, start=start, stop=stop, **kw)
        self.op("tensor", fn, reads, writes, inc)

    def tr(self, out, in_, ident, reads=(), writes=(), inc=False):
        self.op("tensor", lambda e: e.transpose(out, in_, ident), reads, writes, inc)

    def act(self, out, in_, func, reads=(), writes=(), bias=None, scale=None, accum_out=None):
        def fn(e):
            kw = {}
            if bias is not None:
                kw["bias"] = bias
            if scale is not None:
                kw["scale"] = scale
            if accum_out is not None:
                kw["accum_out"] = accum_out
            return e.activation(out=out, in_=in_, func=func, **kw)
        self.op("scalar", fn, reads, writes)

    def tt(self, out, in0, in1, op, reads=(), writes=(), eng="vector"):
        self.op(eng, lambda e: e.tensor_tensor(out=out, in0=in0, in1=in1, op=op), reads, writes)

    def ts(self, out, in0, s1, s2, op0, op1=None, reads=(), writes=(), eng="vector"):
        def fn(e):
            if op1 is None:
                return e.tensor_scalar(out=out, in0=in0, scalar1=s1, scalar2=None, op0=op0)
            return e.tensor_scalar(out=out, in0=in0, scalar1=s1, scalar2=s2, op0=op0, op1=op1)
        self.op(eng, fn, reads, writes)

    def stt(self, out, in0, scalar, in1, op0, op1, reads=(), writes=(), eng="vector"):
        self.op(eng, lambda e: e.scalar_tensor_tensor(out=out, in0=in0, scalar=scalar, in1=in1, op0=op0, op1=op1),
                reads, writes)

    def cp(self, out, in_, reads=(), writes=(), eng="vector"):
        self.op(eng, lambda e: e.tensor_copy(out=out, in_=in_), reads, writes)

    def ms(self, ap, val, reads=(), writes=(), eng="vector"):
        self.op(eng, lambda e: e.memset(ap, val), reads, writes)


class Deferred:
    def __init__(self):
        self.ops = []
        self.pos = 0

    def __getattr__(self, name):
        def rec(*a, **k):
            self.ops.append((name, a, k))
        return rec

    def replay(self, P, n=None, only_dma=False):
        cnt = 0
        while self.pos < len(self.ops) and (n is None or cnt < n):
            name, a, k = self.ops[self.pos]
            if only_dma and name != "dma":
                break
            getattr(P, name)(*a, **k)
            self.pos += 1
            cnt += 1


def build(stage="full"):
    nc = bass.Bass("TRN2", target_bir_lowering=False)

    def din(name, shape, dt=F32):
        return nc.dram_tensor(name, list(shape), dt, kind="ExternalInput").ap()

    x = din("x", [L, D])
    y = nc.dram_tensor("y", [L, D], F32, kind="ExternalOutput").ap()
    g1 = din("ffn1_norm", [1, D]); gm = din("mix_norm", [1, D]); g2 = din("ffn2_norm", [1, D])
    gfin = din("final_norm", [D])
    w1g = din("ffn1_w_gate", [1, D, FF]); w1u = din("ffn1_w_up", [1, D, FF]); w1d = din("ffn1_w_down", [1, FF, D])
    w2g = din("ffn2_w_gate", [1, D, FF]); w2u = din("ffn2_w_up", [1, D, FF]); w2d = din("ffn2_w_down", [1, FF, D])
    w_in = din("w_in", [1, D, 1536]); w_out = din("w_out", [1, D, D])
    lam_re = din("s5_lam_re", [1, 32, 64]); lam_im = din("s5_lam_im", [1, 32, 64]); log_dt = din("s5_log_dt", [1, 32])
    b_re = din("s5_b_re", [1, 32, 64, 16]); b_im = din("s5_b_im", [1, 32, 64, 16])
    c_re = din("s5_c_re", [1, 32, 16, 64]); c_im = din("s5_c_im", [1, 32, 16, 64])
    s5_d = din("s5_d", [1, 512]); w_glu = din("s5_w_glu", [1, 512, 512]); b_glu = din("s5_b_glu", [1, 512])
    cw = din("conv_w_dw", [1, 31, 512]); cb = din("conv_b_dw", [1, 512])
    cg = din("conv_ln_g", [1, 512]); cbt = din("conv_ln_b", [1, 512])
    c_idb = din("c_idb", [128, 128], BF16)
    c_idf = din("c_idf", [128, 128])
    c_i32 = din("c_i32", [128, 32])
    c_par = din("c_par", [128, 4])
    c_ramp = din("c_ramp", [128, 144])
    c_m64 = din("c_m64", [128, 128])

    with ExitStack() as st:
        P = Prog(nc, st)
        st.enter_context(nc.allow_non_contiguous_dma(reason="small one-time parameter layouts"))

        def sb(name, shape, dt):
            return st.enter_context(nc.sbuf_tensor(name, list(shape), dt))

        xbuf = sb("xbuf", [128, 2, NB, D], F32)
        x_sb = xbuf[:, 0]
        hT = sb("hT", [128, KD, TT], BF16)
        hid = sb("hid", [128, NF * TT], BF16)
        wd_sb = sb("wd_sb", [128, NF * D], BF16)
        wgu = sb("wgu", [128, NSLOT, 2, KD, 256], BF16)
        hn = sb("hn", [128, 2, D], BF16)
        sg = sb("sg", [128, 2, 512], BF16)
        sgf = sb("sgf", [128, 2, 512], F32)
        gcol = sb("gcol", [128, 3, KD], F32)
        gfb = sb("gfb", [128, D], F32)
        stat = sb("stat", [128, 4 * NB], F32)
        idb = sb("idb", [128, 128], BF16)
        idf = sb("idf", [128, 128], F32)
        i32 = sb("i32", [128, 32], F32)
        par = sb("par", [128, 4], F32)
        ramp = sb("ramp", [128, 144], F32)
        m64 = sb("m64", [128, 128], F32)
        EC = sb("EC", [128, 16, NC + 1], F32)
        ES = sb("ES", [128, 16, NC + 1], F32)
        BZR = sb("BZR", [128, 4, T, 128], BF16)
        BZI = sb("BZI", [128, 4, T, 128], BF16)
        CZR = sb("CZR", [128, 16, T, 32], BF16)
        CZI = sb("CZI", [128, 16, T, 32], BF16)
        KDS = sb("KDS", [128, 4, T, 32], BF16)
        MAGT = sb("MAGT", [128, 16], F32)
        carR = sb("carR", [128, 16], F32)
        carI = sb("carI", [128, 16], F32)
        wglu_sb = sb("wglu_sb", [128, 4, 512], BF16)
        bglu = sb("bglu", [128, 4], F32)
        wcol = sb("wcol", [128, 16, 8], F32)
        wdiag = sb("wdiag", [128, 16, 8, 32], BF16)
        cvec = sb("cvec", [128, 3, 4], F32)
        dcol = sb("dcol", [128, 4], F32)

        pbig = [st.enter_context(nc.psum_tensor("pb%d" % i, [128, 1024], F32)) for i in range(4)]

        def bank(k):
            return pbig[k // 2][:, (k % 2) * 512:(k % 2) * 512 + 512]

        R = {}

        def res(name):
            if name not in R:
                R[name] = Res(name)
            return R[name]

        rb = [res("bank%d" % k) for k in range(8)]
        rxs = [[res("x%d_%d" % (p_, b)) for b in range(NB)] for p_ in range(2)]
        rx = rxs[0]
        rhT = [res("hT%d" % t) for t in range(NTT)]
        rhid = [res("hid%d" % t) for t in range(NTT)]
        rslot = [res("slot%d" % s) for s in range(NSLOT)]
        rwd = res("wd")
        rwdt = res("wdtail")
        rhn = [res("hn0"), res("hn1")]
        rsg = [res("sg0"), res("sg1")]
        rsgf = [res("sgf0"), res("sgf1")]
        rstat = res("stat")
        rconst = res("const")
        rA = res("arenaA")
        rB = res("arenaB")

        ds_xs = [P.dsem("ds_x0"), P.dsem("ds_x1")]
        ds_y = P.dsem("ds_y")
        ds_c = P.dsem("ds_c")
        ds_s5 = P.dsem("ds_s5")
        ds_cv = P.dsem("ds_cv")
        ds_slot = [P.dsem("ds_slot%d" % s) for s in range(NSLOT)]
        ds_wd = P.dsem("ds_wd")
        ds_rep = P.dsem("ds_rep")

        def load_x(si):
            par_ = si % 2
            for b in range(NB):
                P.dma("sync", xbuf[:, par_, b, :], x[si * TT + b * 128:si * TT + (b + 1) * 128, :], ds_xs[par_],
                      writes=[rxs[par_][b]] + ([rs5] if (si == 1 and rs5 is not None) else []))

        load_x(0)

        def cload(dst, src):
            P.dma("sync", dst, src, ds_c, writes=[rconst])

        cload(idb[:], c_idb[:]); cload(idf[:], c_idf[:]); cload(i32[:], c_i32[:])
        cload(par[:], c_par[:]); cload(ramp[:], c_ramp[:]); cload(m64[:], c_m64[:])
        for n, g in enumerate((g1, gm, g2)):
            cload(gcol[:, n, :], g[0].rearrange("(k p) -> p k", p=128))
        cload(gfb[:], gfin.partition_broadcast(128))
        cload(bglu[:], b_glu[0].rearrange("(q p) -> p q", p=128))
        cload(dcol[:], s5_d[0].rearrange("(q p) -> p q", p=128))
        for n, v in enumerate((cb, cg, cbt)):
            cload(cvec[:, n, :], v[0].rearrange("(q p) -> p q", p=128))
        ds_wglu = P.dsem("ds_wglu")
        rwglu = res("wglu")
        P.dma("gpsimd", wglu_sb[:], w_glu[0].rearrange("(k p) n -> p k n", p=128), ds_wglu, writes=[rwglu])

        hidF = hid[:].bitcast(F32)
        wdF = wd_sb[:].bitcast(F32)

        def carveF(base, off, shape):
            n = int(np.prod(shape))
            v = base[:, off:off + n]
            if len(shape) == 2:
                v = v.rearrange("p (a b) -> p a b", a=shape[0])
            elif len(shape) == 3:
                v = v.rearrange("p (a b c) -> p a b c", a=shape[0], b=shape[1])
            elif len(shape) == 4:
                v = v.rearrange("p (a b c d) -> p a b c d", a=shape[0], b=shape[1], c=shape[2])
            return v, off + n

        def setup_s5(P):
            rs = res("s5setup")
            o = 0
            LR, o = carveF(wdF, o, [16]); LI, o = carveF(wdF, o, [16]); LDT, o = carveF(wdF, o, [16])
            DT, o = carveF(wdF, o, [16]); LRD, o = carveF(wdF, o, [16]); LID, o = carveF(wdF, o, [16])
            THE, o = carveF(wdF, o, [16])
            BR, o = carveF(wdF, o, [16, 16]); BI, o = carveF(wdF, o, [16, 16])
            BBR, o = carveF(wdF, o, [16, 16]); BBI, o = carveF(wdF, o, [16, 16])
            TB1, o = carveF(wdF, o, [16, 16]); TB2, o = carveF(wdF, o, [16, 16])
            ARG, o = carveF(wdF, o, [16, 9]); MAG, o = carveF(wdF, o, [16, 9])
            COS, o = carveF(wdF, o, [16, 9]); SIN, o = carveF(wdF, o, [16, 9])
            PR, o = carveF(wdF, o, [16, 9]); PIm, o = carveF(wdF, o, [16, 9])
            T1, o = carveF(wdF, o, [16]); T2, o = carveF(wdF, o, [16]); T3, o = carveF(wdF, o, [16])
            FR, o = carveF(wdF, o, [16]); FI, o = carveF(wdF, o, [16])
            S8, o = carveF(wdF, o, [16]); SH, o = carveF(wdF, o, [16]); C8, o = carveF(wdF, o, [16]); TQ, o = carveF(wdF, o, [16])
            CN_R, o = carveF(wdF, o, [4, 64]); CN_I, o = carveF(wdF, o, [4, 64])
            CIN_R, o = carveF(wdF, o, [4, 128]); CIN_I, o = carveF(wdF, o, [4, 128])
            CTR, o = carveF(wdF, o, [4, 128]); CTI, o = carveF(wdF, o, [4, 128])
            xF = xbuf[:, 1].rearrange("p b d -> p (b d)")
            ox = 0
            ET1, ox = carveF(xF, ox, [16, NC // 2]); ET2, ox = carveF(xF, ox, [16, NC // 2])
            ET3, ox = carveF(xF, ox, [16, NC // 2]); ET4, ox = carveF(xF, ox, [16, NC // 2])
            assert ox <= NB * D
            o2 = o
            VBR, o2 = carveF(wdF, o2, [4, T, 128]); VBI, o2 = carveF(wdF, o2, [4, T, 128])
            TV1, o2 = carveF(wdF, o2, [T, 4, 16])
            TC1, o2 = carveF(wdF, o2, [4, T, 32]); TC2, o2 = carveF(wdF, o2, [4, T, 32])
            assert o2 <= 11264, o2

            def ld(dst, src):
                P.dma("sync", dst, src, ds_s5, writes=[rs])

            for h in range(2):
                hs = slice(64 * h, 64 * h + 64)
                ld(LR[hs, :], lam_re[0, h::2, :].rearrange("g p -> p g"))
                ld(LI[hs, :], lam_im[0, h::2, :].rearrange("g p -> p g"))
                ld(LDT[hs, :], log_dt[0:1, h::2].to_broadcast([64, 16]))
                ld(BR[hs, :, :], b_re[0, h::2, :, :].rearrange("g p c -> p g c"))
                ld(BI[hs, :, :], b_im[0, h::2, :, :].rearrange("g p c -> p g c"))
            ld(CN_R, c_re[0].rearrange("(q g) c p -> (g c) q p", q=4))
            ld(CN_I, c_im[0].rearrange("(q g) c p -> (g c) q p", q=4))

            rw = dict(reads=[rs, rconst], writes=[rs])
            V = "vector"
            P.act(DT, LDT, AF.Exp, **rw)
            P.tt(LRD, LR, DT, ALU.mult, **rw)
            P.tt(LID, LI, DT, ALU.mult, **rw)
            P.ts(THE, LID, float(T), None, ALU.mult, **rw)
            rmp9 = ramp[:, 0:9].unsqueeze(1).to_broadcast([128, 16, 9])
            P.tt(ARG, LRD.unsqueeze(2).to_broadcast([128, 16, 9]), rmp9, ALU.mult, **rw)
            P.act(MAG, ARG, AF.Exp, **rw)

            def cmul(oR, oI, aR, aI, bR, bI, t1, t2, t3, t4):
                P.tt(t1, aR, bR, ALU.mult, **rw)
                P.tt(t2, aI, bI, ALU.mult, **rw)
                P.tt(t3, aR, bI, ALU.mult, **rw)
                P.tt(t4, aI, bR, ALU.mult, **rw)
                P.tt(oR, t1, t2, ALU.subtract, **rw)
                P.tt(oI, t3, t4, ALU.add, **rw)

            P.act(S8, LID, AF.Sin, scale=1.0 / 8, **rw)
            P.act(SH, LID, AF.Sin, scale=1.0 / 16, **rw)
            P.tt(C8, SH, SH, ALU.mult, **rw)
            P.ts(C8, C8, -2.0, 1.0, ALU.mult, ALU.add, **rw)
            for _ in range(3):
                P.tt(T1, C8, C8, ALU.mult, **rw)
                P.tt(T2, S8, S8, ALU.mult, **rw)
                P.tt(T3, C8, S8, ALU.mult, **rw)
                P.tt(C8, T1, T2, ALU.subtract, **rw)
                P.ts(S8, T3, 2.0, None, ALU.mult, **rw)
            P.ms(COS[:, :, 0], 1.0, **rw)
            P.ms(SIN[:, :, 0], 0.0, **rw)
            P.ms(COS[:, :, T + 1:9], 1.0, **rw)
            P.ms(SIN[:, :, T + 1:9], 0.0, **rw)
            for d in range(1, T + 1):
                cmul(COS[:, :, d], SIN[:, :, d], COS[:, :, d - 1], SIN[:, :, d - 1], C8, S8, T1, T2, T3, TQ)
            P.tt(PR, MAG, COS, ALU.mult, **rw)
            P.tt(PIm, MAG, SIN, ALU.mult, **rw)
            P.cp(MAGT[:], MAG[:, :, T], **rw)
            P.ms(EC[:, :, 0], 1.0, **rw)
            P.ms(ES[:, :, 0], 0.0, **rw)
            P.cp(EC[:, :, 1], COS[:, :, T], **rw)
            P.cp(ES[:, :, 1], SIN[:, :, T], **rw)
            m = 1
            while m < NC:
                n = min(m, NC - m)
                bR = EC[:, :, m:m + 1].to_broadcast([128, 16, n])
                bI = ES[:, :, m:m + 1].to_broadcast([128, 16, n])
                cmul(EC[:, :, m + 1:m + 1 + n], ES[:, :, m + 1:m + 1 + n], EC[:, :, 1:1 + n], ES[:, :, 1:1 + n], bR, bI,
                     ET1[:, :, 0:n], ET2[:, :, 0:n], ET3[:, :, 0:n], ET4[:, :, 0:n])
                m += n
            P.ts(T1, PR[:, :, 1], -1.0, None, ALU.add, **rw)
            P.tt(T2, LR, LR, ALU.mult, **rw)
            P.tt(T3, LI, LI, ALU.mult, **rw)
            P.tt(T2, T2, T3, ALU.add, **rw)
            P.op(V, lambda e: e.reciprocal(out=T2, in_=T2), **rw)
            P.tt(FR, T1, LR, ALU.mult, **rw)
            P.tt(T3, PIm[:, :, 1], LI, ALU.mult, **rw)
            P.tt(FR, FR, T3, ALU.add, **rw)
            P.tt(FR, FR, T2, ALU.mult, **rw)
            P.tt(FI, PIm[:, :, 1], LR, ALU.mult, **rw)
            P.tt(T3, T1, LI, ALU.mult, **rw)
            P.tt(FI, FI, T3, ALU.subtract, **rw)
            P.tt(FI, FI, T2, ALU.mult, **rw)
            frb = FR.unsqueeze(2).to_broadcast([128, 16, 16])
            fib = FI.unsqueeze(2).to_broadcast([128, 16, 16])
            P.tt(BBR, BR, frb, ALU.mult, **rw)
            P.tt(TB1, BI, fib, ALU.mult, **rw)
            P.tt(BBR, BBR, TB1, ALU.subtract, **rw)
            P.tt(BBI, BI, frb, ALU.mult, **rw)
            P.tt(TB1, BR, fib, ALU.mult, **rw)
            P.tt(BBI, BBI, TB1, ALU.add, **rw)
            P.ms(VBR, 0.0, **rw)
            P.ms(VBI, 0.0, **rw)
            VBR5 = VBR.rearrange("p q d (g h c) -> p q d g h c", g=4, h=2)
            VBI5 = VBI.rearrange("p q d (g h c) -> p q d g h c", g=4, h=2)
            for q in range(4):
                for h in range(2):
                    hs = slice(64 * h, 64 * h + 64)
                    prb = PR[hs, 4 * q:4 * q + 4, 0:T].rearrange("p g d -> p d g").unsqueeze(3).to_broadcast([64, T, 4, 16])
                    pib = PIm[hs, 4 * q:4 * q + 4, 0:T].rearrange("p g d -> p d g").unsqueeze(3).to_broadcast([64, T, 4, 16])
                    bbr = BBR[hs, 4 * q:4 * q + 4, :].unsqueeze(1).to_broadcast([64, T, 4, 16])
                    bbi = BBI[hs, 4 * q:4 * q + 4, :].unsqueeze(1).to_broadcast([64, T, 4, 16])
                    oR = VBR5[hs, q, :, :, h, :]
                    oI = VBI5[hs, q, :, :, h, :]
                    t1 = TV1[hs]
                    P.tt(oR, prb, bbr, ALU.mult, **rw)
                    P.tt(t1, pib, bbi, ALU.mult, **rw)
                    P.tt(oR, oR, t1, ALU.subtract, **rw)
                    P.tt(oI, prb, bbi, ALU.mult, **rw)
                    P.tt(t1, pib, bbr, ALU.mult, **rw)
                    P.tt(oI, oI, t1, ALU.add, **rw)
            for (CN, CIN, pc) in ((CN_R, CIN_R, 0), (CN_I, CIN_I, 2)):
                P.ts(CIN[:, :, 0:64], CN, par[:, pc:pc + 1], None, ALU.mult, **rw)
                P.ts(CIN[:, :, 64:128], CN, par[:, pc + 1:pc + 2], None, ALU.mult, **rw)
            for (CIN, CT) in ((CIN_R, CTR), (CIN_I, CTI)):
                for q in range(4):
                    P.mm(bank(4)[:, q * 128:(q + 1) * 128], CIN[:, q, :], idf[:], True, True,
                         reads=[rs, rconst], writes=[rb[4]], inc=(q == 3))
                P.cp(CT, bank(4).rearrange("p (q c) -> p q c", q=4), reads=[rb[4]], writes=[rb[4], rs])
            for (VB, BZ) in ((VBR, BZR), (VBI, BZI)):
                for q in range(4):
                    for dh in range(T // 4):
                        bk = 5 + (dh % 2)
                        for dd in range(4):
                            d = dh * 4 + dd
                            P.mm(bank(bk)[:, dd * 128:(dd + 1) * 128], VB[:, q, d, :], idf[:], True, True,
                                 reads=[rs, rconst], writes=[rb[bk]], inc=(dd == 3))
                        for dd in range(4):
                            d = dh * 4 + dd
                            P.cp(BZ[:, q, T - 1 - d, :], bank(bk)[:, dd * 128:(dd + 1) * 128],
                                 reads=[rb[bk]], writes=[rb[bk], rs])
            for q in range(4):
                for d in range(T):
                    bk = 6 + (d // 4)
                    for g4 in range(4):
                        col = (d % 4) * 128 + g4 * 32
                        o_ = bank(bk)[32 * g4:32 * g4 + 32, col:col + 32]
                        last = (g4 == 3 and d % 4 == 3)
                        P.mm(o_, VBR[:, q, d, 32 * g4:32 * g4 + 32], CTR[:, q, 32 * g4:32 * g4 + 32], True, False,
                             reads=[rs], writes=[rb[bk]], tp=(0, 32 * g4))
                        P.mm(o_, VBI[:, q, d, 32 * g4:32 * g4 + 32], CTI[:, q, 32 * g4:32 * g4 + 32], False, True,
                             reads=[rs], writes=[rb[bk]], tp=(0, 32 * g4), inc=last)
                for dhh in range(T // 4):
                    bk = 6 + dhh
                    src = bank(bk).rearrange("p (d g c) -> p d g c", d=4, g=4)
                    for g4 in range(4):
                        ps_ = slice(32 * g4, 32 * g4 + 32)
                        if dhh == 0:
                            P.stt(KDS[ps_, q, 0, :], i32[ps_, :], dcol[ps_, q:q + 1], src[ps_, 0, g4, :],
                                  ALU.mult, ALU.add, reads=[rb[bk], rconst], writes=[rb[bk], rs])
                            P.cp(KDS[ps_, q, 1:4, :], src[ps_, 1:4, g4, :], reads=[rb[bk]], writes=[rb[bk], rs])
                        else:
                            P.cp(KDS[ps_, q, 4:8, :], src[ps_, :, g4, :], reads=[rb[bk]], writes=[rb[bk], rs])
            for q in range(4):
                ctr = CTR[:, q, :].rearrange("p (g c) -> p g c", g=4).unsqueeze(2).to_broadcast([128, 4, T, 32])
                cti = CTI[:, q, :].rearrange("p (g c) -> p g c", g=4).unsqueeze(2).to_broadcast([128, 4, T, 32])
                prb = PR[:, 4 * q:4 * q + 4, 1:T + 1].unsqueeze(3).to_broadcast([128, 4, T, 32])
                pib = PIm[:, 4 * q:4 * q + 4, 1:T + 1].unsqueeze(3).to_broadcast([128, 4, T, 32])
                P.tt(TC1, ctr, prb, ALU.mult, **rw)
                P.tt(TC2, cti, pib, ALU.mult, **rw)
                P.tt(CZR[:, 4 * q:4 * q + 4, :, :], TC1, TC2, ALU.add, **rw)
                P.tt(TC1, cti, prb, ALU.mult, **rw)
                P.tt(TC2, ctr, pib, ALU.mult, **rw)
                P.tt(CZI[:, 4 * q:4 * q + 4, :, :], TC1, TC2, ALU.subtract, **rw)
            P.ms(carR[:], 0.0, **rw)
            P.ms(carI[:], 0.0, **rw)
            return rs

        def setup_conv():
            rs = res("convsetup")
            P.ms(wcol[:], 0.0, reads=[], writes=[rs])
            for s in range(4):
                for r in range(8):
                    if 4 * r + s > 30:
                        continue
                    P.dma("sync", wcol[32 * s:32 * s + 32, :, r],
                          cw[0, 4 * r + s, :].rearrange("(g c) -> c g", c=32), ds_cv, reads=[], writes=[rs])
            return rs

        def setup_conv_late(rs):
            P.tt(wdiag[:], wcol[:].unsqueeze(3).to_broadcast([128, 16, 8, 32]),
                 i32[:].unsqueeze(1).unsqueeze(1).to_broadcast([128, 16, 8, 32]), ALU.mult,
                 reads=[rs, rconst], writes=[rs])

        stat2 = sb("stat2", [128, 2 * NB], F32)
        rstat2 = res("stat2")

        def norm_to_hT(nidx, xs=None, rxl=None, bank0=6, stt_=None, rst_=None):
            xs = x_sb if xs is None else xs
            rxl = rx if rxl is None else rxl
            stt_ = stat if stt_ is None else stt_
            rst_ = rstat if rst_ is None else rst_
            for b in range(NB):
                s = b % 2
                P.act(sgf[:, s, :].bitcast(BF16), xs[:, b, :], AF.Square, accum_out=stt_[:, b:b + 1],
                      reads=[rxl[b]], writes=[rsgf[s], rst_])
            rs_all = stt_[:, NB:2 * NB]
            P.ts(rs_all, stt_[:, 0:NB], 1.0 / D, EPS, ALU.mult, ALU.add, reads=[rst_], writes=[rst_])
            P.act(rs_all, rs_all, AF.Sqrt, reads=[rst_], writes=[rst_])
            P.op("vector", lambda e: e.reciprocal(out=rs_all, in_=rs_all), reads=[rst_], writes=[rst_])
            for b in range(NB):
                s = b % 2
                rstd = stt_[:, NB + b:NB + b + 1]
                P.ts(hn[:, s, :], xs[:, b, :], rstd, None, ALU.mult, reads=[rxl[b], rst_], writes=[rhn[s]])
                bk = bank0 + s
                pt = bank(bk).bitcast(BF16)
                for k in range(KD):
                    P.tr(pt[:, k * 128:(k + 1) * 128], hn[:, s, k * 128:(k + 1) * 128], idb[:],
                         reads=[rhn[s], rconst], writes=[rb[bk]], inc=(k == KD - 1))
                tt_ = b // 4
                P.tt(hT[:, :, b * 128:(b + 1) * 128], pt.rearrange("p (k t) -> p k t", k=KD),
                     gcol[:, nidx, :].unsqueeze(2).to_broadcast([128, KD, 128]), ALU.mult,
                     reads=[rb[bk], rconst], writes=[rb[bk], rhT[tt_]])

        scr_gu = [nc.dram_tensor("scr_gu%d" % i, [NF // 2, 128, 2 * KD * 256], BF16).ap() for i in range(2)]
        scr_wd = [nc.dram_tensor("scr_wd%d" % i, [128, NF * D], BF16).ap() for i in range(2)]
        rscr_gu = [[res("scrgu%d_%d" % (i, fp)) for fp in range(NF // 2)] for i in range(2)]
        rscr_wd = [res("scrwd%d" % i) for i in range(2)]
        ds_scr = P.dsem("ds_scr")

        ds_cvt = P.dsem("ds_cvt")

        def convert_ffn(fi_, wg, wu, wdn):
            wgv = wg[0].rearrange("(k p) n -> p k n", p=128)
            wuv = wu[0].rearrange("(k p) n -> p k n", p=128)
            wdv = wdn[0].rearrange("(f p) n -> p f n", p=128)
            for fp in range(NF // 2):
                dst = scr_gu[fi_][fp].rearrange("p (g k n) -> p g k n", g=2, k=KD)
                P.dma("gpsimd", dst[:, 0], wgv[:, :, fp * 256:(fp + 1) * 256], ds_cvt, writes=[rscr_gu[fi_][fp]])
                P.dma("gpsimd", dst[:, 1], wuv[:, :, fp * 256:(fp + 1) * 256], ds_cvt, writes=[rscr_gu[fi_][fp]])
            dstw = scr_wd[fi_].rearrange("p (f n) -> p f n", f=NF)
            P.dma("gpsimd", dstw[:, 0:11, :], wdv[:, 0:11, :], ds_cvt, writes=[rscr_wd[fi_]])
            P.dma("gpsimd", dstw[:, 11:22, :], wdv[:, 11:22, :], ds_cvt, writes=[rscr_wd[fi_]])

        def ffn(wg, wu, wdn, first_wd_dep, fi_, tile_i, mid_hook=None, hid_dep=(), interleave=None, wd_late=False):
            hid3 = hid[:].rearrange("p (f t) -> p f t", f=NF)
            wd3 = wd_sb[:].rearrange("p (f n) -> p f n", f=NF)
            wgv = wg[0].rearrange("(k p) n -> p k n", p=128)
            wuv = wu[0].rearrange("(k p) n -> p k n", p=128)
            wdv = wdn[0].rearrange("(f p) n -> p f n", p=128)

            NP_ = NF // 2

            def load_p(fp):
                s = fp % NSLOT
                flat = wgu[:, s].rearrange("p g k n -> p (g k n)")
                if tile_i == 0 and fi_ == 0:
                    P.dma("gpsimd", wgu[:, s, 0, :, :], wgv[:, :, fp * 256:(fp + 1) * 256], ds_slot[s], writes=[rslot[s]])
                    P.dma("gpsimd", wgu[:, s, 1, :, :], wuv[:, :, fp * 256:(fp + 1) * 256], ds_slot[s], writes=[rslot[s]])
                    P.dma("sync", scr_gu[fi_][fp], flat, ds_scr, reads=[rslot[s]], writes=[rscr_gu[fi_][fp]])
                else:
                    P.dma("gpsimd", flat, scr_gu[fi_][fp], ds_slot[s], reads=[rscr_gu[fi_][fp]], writes=[rslot[s]])

            for fp in range(min(NSLOT, NP_)):
                load_p(fp)
            def load_wd():
                if tile_i == 0 and fi_ == 0:
                    P.dma("gpsimd", wd3[:, 0:11, :], wdv[:, 0:11, :], ds_wd, writes=[rwd, rwdt] + list(first_wd_dep))
                    P.dma("gpsimd", wd3[:, 11:22, :], wdv[:, 11:22, :], ds_wd, writes=[rwd, rwdt])
                    P.dma("sync", scr_wd[fi_], wd_sb[:], ds_scr, reads=[rwd, rwdt], writes=[rscr_wd[fi_]])
                else:
                    P.dma("gpsimd", wd_sb[:], scr_wd[fi_], ds_wd, reads=[rscr_wd[fi_]],
                          writes=[rwd, rwdt] + list(first_wd_dep))

            if not wd_late:
                load_wd()
            it = 0
            for fp in range(NP_):
                s = fp % NSLOT
                for fi in range(2):
                    f = 2 * fp + fi
                    for t in range(NTT):
                        pa = (it % 2) * 2
                        it += 1
                        tsl = slice(t * 512, (t + 1) * 512)
                        for gu in range(2):
                            for k in range(KD):
                                P.mm(bank(pa + gu), wgu[:, s, gu, k, fi * 128:(fi + 1) * 128], hT[:, k, tsl], k == 0, k == KD - 1,
                                     reads=[rslot[s], rhT[t]], writes=[rb[pa + gu]], inc=(k == KD - 1))
                        ss = (it - 1) % 2
                        P.act(sg[:, ss, :], bank(pa), AF.Silu, reads=[rb[pa]], writes=[rb[pa], rsg[ss]])
                        P.tt(hid3[:, f, tsl], bank(pa + 1), sg[:, ss, :], ALU.mult,
                             reads=[rb[pa + 1], rsg[ss]], writes=[rb[pa + 1], rhid[t], rA] + list(hid_dep))
                    if interleave is not None:
                        interleave()
                if fp + NSLOT < NP_:
                    load_p(fp + NSLOT)
            if mid_hook is not None:
                mid_hook()
            if wd_late:
                load_wd()
            for b in range(NB):
                t = b // 4
                pa = 4 + (b % 2) * 2
                for dh in range(2):
                    for f in range(NF):
                        P.mm(bank(pa + dh), hid3[:, f, b * 128:(b + 1) * 128], wd3[:, f, dh * 512:(dh + 1) * 512],
                             f == 0, f == NF - 1, reads=[rhid[t], rwd, rwdt], writes=[rb[pa + dh]], inc=(f == NF - 1))
                for dh in range(2):
                    dsl = slice(dh * 512, (dh + 1) * 512)
                    P.stt(x_sb[:, b, dsl], bank(pa + dh), 0.5, x_sb[:, b, dsl], ALU.mult, ALU.add,
                          reads=[rb[pa + dh], rx[b]], writes=[rb[pa + dh], rx[b]])

        WOUT = wd_sb[:, 0:KD * D].rearrange("p (k n) -> p k n", k=KD)
        winv = w_in[0].rearrange("(k p) n -> p k n", p=128)
        oA = 0
        US5 = hid[:, oA:oA + 4 * TT].rearrange("p (q t) -> p q t", q=4); oA += 4 * TT
        ZB = hid[:, oA:oA + 4 * ZW].rearrange("p (q t) -> p q t", q=4); oA += 4 * ZW
        YG = hid[:, oA:oA + 4 * TT].rearrange("p (q t) -> p q t", q=4); oA += 4 * TT
        YCAT = hid[:, oA:oA + 8 * TT].rearrange("p (q t) -> p q t", q=8); oA += 8 * TT
        assert oA <= NF * TT, oA
        ZOFF = KD * D
        ZREP = wd_sb[:, ZOFF:ZOFF + 16 * RW].rearrange("p (g t) -> p g t", g=16)
        assert ZOFF + 16 * RW <= NF * D
        hTF = hT[:].rearrange("p k t -> p (k t)").bitcast(F32)
        oZ = 0
        WRe, oZ = carveF(hTF, oZ, [8, NC]); WIm, oZ = carveF(hTF, oZ, [8, NC])
        assert oZ <= KD * TT // 2, oZ
        ROFF = ZOFF + 16 * RW
        tailF = wd_sb[:, ROFF:NF * D].bitcast(F32)
        oZ = 0
        RRe, oZ = carveF(tailF, oZ, [8, NC + 1]); RIm, oZ = carveF(tailF, oZ, [8, NC + 1])
        assert oZ <= (NF * D - ROFF) // 2, oZ
        smallT = sb("smallT", [128, 4, 16], F32)
        SRb = sb("SRb", [128, 16, NC + 1], BF16)
        SIb = sb("SIb", [128, 16, NC + 1], BF16)
        zhist = sb("zhist", [128, 4, 32], BF16)
        rzh = res("zhist")
        czf = hn[:].rearrange("p s d -> p (s d)").bitcast(F32).rearrange("p (s d) -> p s d", s=2)
        rZB = res("zbuf"); rZBh = [res("zbuf_h0"), res("zbuf_h1")]; rZREP = res("zrep")
        rZREPh = [res("zrep_h0"), res("zrep_h1")]; ds_reph = [P.dsem("ds_rep0"), P.dsem("ds_rep1")]; rUS5 = res("us5"); rYG = res("yg"); rYC = [res("ycat%d" % t) for t in range(NTT)]
        rW = res("wrot"); rRR = res("rr"); rS = res("sfull"); rTM = res("tm"); rczf = rhn
        rcar = res("carry")

        def mixer(st_i, rs5, rcv):
            for fp in range(6):
                P.dma("gpsimd", wgu[:, fp % 3, fp // 3, :, :], winv[:, :, fp * 256:(fp + 1) * 256], ds_slot[fp % 3],
                      writes=[rslot[fp % 3]])
            P.dma("gpsimd", WOUT, w_out[0].rearrange("(k p) n -> p k n", p=128), ds_wd, writes=[rwd, rB])
            if stage == "full" and st_i == 0:
                convert_ffn(1, w2g, w2u, w2d)
            if st_i == 0:
                P.ms(zhist[:], 0.0, writes=[rzh])
            P.cp(ZB[:, :, 0:32], zhist[:], reads=[rzh], writes=[rZB, rZBh[0], rZBh[1], rA])
            P.ms(ZB[:, :, 32 + TT:ZW], 0.0, writes=[rZB, rZBh[0], rZBh[1], rA])
            it = 0
            for q in range(4):
                for t in range(NTT):
                    b1 = (it % 2) * 2
                    it += 1
                    tsl = slice(t * 512, (t + 1) * 512)
                    for hh, fp in enumerate((2 + q // 2, 4 + q // 2)):
                        sl, hf, fi = fp % 3, fp // 3, q % 2
                        for k in range(KD):
                            P.mm(bank(b1 + hh), wgu[:, sl, hf, k, fi * 128:(fi + 1) * 128], hT[:, k, tsl], k == 0, k == KD - 1,
                                 reads=[rslot[sl], rhT[t]], writes=[rb[b1 + hh]], inc=(k == KD - 1))
                    ss = it % 2
                    P.act(sg[:, ss, :], bank(b1 + 1), AF.Sigmoid, reads=[rb[b1 + 1]], writes=[rb[b1 + 1], rsg[ss]])
                    P.tt(ZB[:, q, 32 + t * 512:32 + (t + 1) * 512], bank(b1), sg[:, ss, :], ALU.mult,
                         reads=[rb[b1], rsg[ss]], writes=[rb[b1], rZBh[q // 2], rA])
                if q % 2 == 1:
                    hq = q // 2
                    for s in range(4):
                        for g4 in range(4):
                            g0 = 8 * hq + g4
                            P.dma("sync", ZREP[32 * s:32 * s + 32, g0:g0 + 5:4, :],
                                  ZB[32 * g4:32 * g4 + 32, 2 * hq:2 * hq + 2, s:s + RW], ds_reph[hq],
                                  reads=[rZBh[hq]], writes=[rZREPh[hq], rwdt])
            rUS5h = [res("us5_h0"), res("us5_h1")]

            def win_s5(cq):
                fp, fi = cq // 2, cq % 2
                sl, hf = fp % 3, fp // 3
                t = 0
                bk = 4 + cq
                tsl = slice(t * 512, (t + 1) * 512)
                for k in range(KD):
                    P.mm(bank(bk), wgu[:, sl, hf, k, fi * 128:(fi + 1) * 128], hT[:, k, tsl], k == 0, k == KD - 1,
                         reads=[rslot[sl], rhT[t]], writes=[rb[bk]], inc=(k == KD - 1))
                P.act(US5[:, cq, tsl], bank(bk), AF.Copy, reads=[rb[bk]], writes=[rb[bk], rUS5h[cq // 2], rUS5, rA])

            def z_half(h):
                for part, BZ in enumerate((BZR, BZI)):
                    for ql in range(2):
                        q = 2 * h + ql
                        c0 = part * 2 * NC + ql * NC
                        for i in range(T):
                            for g4 in range(4):
                                bk = 4 * h + g4
                                P.mm(bank(bk)[:, c0:c0 + NC], BZ[32 * g4:32 * g4 + 32, q, i, :],
                                     US5[32 * g4:32 * g4 + 32, q, i::T], i == 0, i == T - 1,
                                     reads=[rUS5h[h], rs5], writes=[rb[bk]], tp=(32 * g4, 0),
                                     inc=(i == T - 1 and part == 1 and ql == 1))

            win_s5(0)
            win_s5(1)
            z_half(0)
            win_s5(2)
            win_s5(3)
            z_half(1)
            rSh = [res("sfull_h0"), res("sfull_h1")]

            def s5_half(h):
                for g4 in range(4):
                    bk = 4 * h + g4
                    zr = bank(bk)[:, 0:2 * NC].rearrange("p (q c) -> p q c", q=2)
                    zi = bank(bk)[:, 2 * NC:4 * NC].rearrange("p (q c) -> p q c", q=2)
                    ec = EC[:, 8 * h + g4:8 * h + g4 + 5:4, 1:NC + 1]
                    es = ES[:, 8 * h + g4:8 * h + g4 + 5:4, 1:NC + 1]
                    wr = WRe[:, g4:8:4, :]
                    wi = WIm[:, g4:8:4, :]
                    t1 = RRe[:, g4:8:4, 1:NC + 1]
                    t2 = RIm[:, g4:8:4, 1:NC + 1]
                    dep = dict(reads=[rb[bk], rs5, rS], writes=[rb[bk], rW, rRR, rhT[0]])
                    P.tt(wr, zr, ec, ALU.mult, **dep)
                    P.tt(t1, zi, es, ALU.mult, **dep)
                    P.tt(wr, wr, t1, ALU.add, **dep)
                    P.tt(wi, zi, ec, ALU.mult, **dep)
                    P.tt(t2, zr, es, ALU.mult, **dep)
                    P.tt(wi, wi, t2, ALU.subtract, **dep)
                gs = slice(8 * h, 8 * h + 8)
                P.cp(RRe[:, :, 0], carR[:, gs], reads=[rcar, rW], writes=[rRR])
                P.cp(RIm[:, :, 0], carI[:, gs], reads=[rcar, rW], writes=[rRR])
                for gl in range(8):
                    gp = 8 * h + gl
                    mg = MAGT[:, gp:gp + 1].to_broadcast([128, NC])
                    for (RX, WX, CAR) in ((RRe, WRe, carR), (RIm, WIm, carI)):
                        P.op("vector", lambda e, RX=RX, WX=WX, CAR=CAR, gp=gp, gl=gl, mg=mg: e.tensor_tensor_scan(
                            out=RX[:, gl, 1:NC + 1], data0=mg, data1=WX[:, gl, :], initial=CAR[:, gp:gp + 1],
                            op0=ALU.mult, op1=ALU.add), reads=[rW, rRR, rcar, rs5, rhT[0]], writes=[rRR])
                depc = dict(reads=[rRR, rs5], writes=[rTM])
                P.tt(smallT[:, 0, gs], EC[:, gs, NC], RRe[:, :, NC], ALU.mult, **depc)
                P.tt(smallT[:, 1, gs], ES[:, gs, NC], RIm[:, :, NC], ALU.mult, **depc)
                P.tt(smallT[:, 2, gs], ES[:, gs, NC], RRe[:, :, NC], ALU.mult, **depc)
                P.tt(smallT[:, 3, gs], EC[:, gs, NC], RIm[:, :, NC], ALU.mult, **depc)
                P.tt(carR[:, gs], smallT[:, 0, gs], smallT[:, 1, gs], ALU.subtract, reads=[rTM], writes=[rcar])
                P.tt(carI[:, gs], smallT[:, 2, gs], smallT[:, 3, gs], ALU.add, reads=[rTM], writes=[rcar])
                dep = dict(reads=[rRR, rs5, rW, rhT[0]], writes=[rSh[h], rS, rW])
                TM1 = WRe
                TM2 = WIm
                P.tt(TM1, EC[:, gs, 0:NC], RRe[:, :, 0:NC], ALU.mult, **dep)
                P.tt(TM2, ES[:, gs, 0:NC], RIm[:, :, 0:NC], ALU.mult, **dep)
                P.tt(SRb[:, gs, 0:NC], TM1, TM2, ALU.subtract, **dep)
                P.tt(TM1, ES[:, gs, 0:NC], RRe[:, :, 0:NC], ALU.mult, **dep)
                P.tt(TM2, EC[:, gs, 0:NC], RIm[:, :, 0:NC], ALU.mult, **dep)
                P.tt(SIb[:, gs, 0:NC], TM1, TM2, ALU.add, **dep)

            def conv_it(q):
                t = 0
                bk = q % 2
                cs = q % 2
                bm, bv = 2, 3
                for r in range(8):
                    for g4 in range(4):
                        grp = 4 * q + g4
                        c0 = 2 + 4 * r + t * 512
                        P.mm(bank(bk)[32 * g4:32 * g4 + 32, :], wdiag[:, grp, r, :], ZREP[:, grp, c0:c0 + 512],
                             r == 0, r == 7, reads=[rZREPh[q // 2], rcv], writes=[rb[bk]], tp=(0, 32 * g4),
                             inc=(r == 7 and g4 == 3))
                tsl = slice(t * 512, (t + 1) * 512)
                P.act(czf[:, cs, :], bank(bk), AF.Identity, bias=cvec[:, 0, q:q + 1],
                      reads=[rb[bk], rconst], writes=[rb[bk], rczf[cs]])
                P.act(sgf[:, cs, :], czf[:, cs, :], AF.Square, reads=[rczf[cs]], writes=[rsgf[cs]])
                P.mm(bank(bm), m64[:], czf[:, cs, :], True, True, reads=[rczf[cs], rconst], writes=[rb[bm]], inc=True)
                P.mm(bank(bv), m64[:], sgf[:, cs, :], True, True, reads=[rsgf[cs], rconst], writes=[rb[bv]], inc=True)
                P.tt(czf[:, cs, :], czf[:, cs, :], bank(bm), ALU.subtract, reads=[rb[bm], rczf[cs]],
                     writes=[rb[bm], rczf[cs]])
                P.act(sgf[:, cs, :], bank(bm), AF.Square, reads=[rb[bm]], writes=[rb[bm], rsgf[cs]])
                P.tt(sgf[:, cs, :], bank(bv), sgf[:, cs, :], ALU.subtract, reads=[rb[bv], rsgf[cs]],
                     writes=[rb[bv], rsgf[cs]])
                P.ts(sgf[:, cs, :], sgf[:, cs, :], EPS, None, ALU.add, reads=[rsgf[cs]], writes=[rsgf[cs]])
                P.act(sgf[:, cs, :], sgf[:, cs, :], AF.Sqrt, reads=[rsgf[cs]], writes=[rsgf[cs]])
                P.op("vector", lambda e, cs=cs: e.reciprocal(out=sgf[:, cs, :], in_=sgf[:, cs, :]),
                     reads=[rsgf[cs]], writes=[rsgf[cs]])
                P.tt(czf[:, cs, :], czf[:, cs, :], sgf[:, cs, :], ALU.mult, reads=[rczf[cs], rsgf[cs]],
                     writes=[rczf[cs]])
                P.act(YCAT[:, 4 + q, tsl], czf[:, cs, :], AF.Silu, scale=cvec[:, 1, q:q + 1], bias=cvec[:, 2, q:q + 1],
                      reads=[rczf[cs], rconst], writes=[rYC[t], rA])

            def y_q(q):
                rSq = rSh[q // 2]
                for g4 in range(4):
                    gp = 4 * q + g4
                    bk = g4
                    o_ = bank(bk)[:, q * NC:(q + 1) * NC]
                    P.mm(o_, CZR[:, gp].rearrange("p j c -> p (j c)"), SRb[:, gp, 0:NC], True, False,
                         reads=[rSq, rs5], writes=[rb[bk]])
                    P.mm(o_, CZI[:, gp].rearrange("p j c -> p (j c)"), SIb[:, gp, 0:NC], False, False,
                         reads=[rSq, rs5], writes=[rb[bk]])
                for j in range(T):
                    for i in range(j + 1):
                        for g4 in range(4):
                            bk = g4
                            o_ = bank(bk)[32 * j:32 * j + 32, q * NC:(q + 1) * NC]
                            P.mm(o_, KDS[32 * g4:32 * g4 + 32, q, j - i, :], US5[32 * g4:32 * g4 + 32, q, i::T],
                                 False, (i == j), reads=[rUS5h[q // 2], rs5], writes=[rb[bk]], tp=(32 * g4, 32 * j),
                                 inc=(i == j and j == T - 1))

            def y_evac(h):
                for g4 in range(4):
                    for j in range(T):
                        src = bank(g4)[32 * j:32 * j + 32, 2 * h * NC:(2 * h + 2) * NC].rearrange("p (q c) -> p q c", q=2)
                        dst = YG[32 * g4:32 * g4 + 32, 2 * h:2 * h + 2, j::T]
                        P.act(dst, src, AF.Gelu_apprx_tanh, reads=[rb[g4]], writes=[rb[g4], rYG, rA])

            s5_half(0)
            conv_it(0)
            conv_it(1)
            s5_half(1)
            y_q(0)
            y_q(1)
            y_evac(0)
            conv_it(2)
            conv_it(3)
            y_q(2)
            y_q(3)
            y_evac(1)
            P.cp(zhist[:], ZB[:, :, TT:TT + 32], reads=[rZB, rZBh[0], rZBh[1]], writes=[rzh])
            it = 0
            for cq in range(4):
                for t in range(NTT):
                    bk = it % 2
                    ss = it % 2
                    it += 1
                    tsl = slice(t * 512, (t + 1) * 512)
                    for k in range(4):
                        P.mm(bank(bk), wglu_sb[:, k, cq * 128:(cq + 1) * 128], YG[:, k, tsl], k == 0, k == 3,
                             reads=[rYG, rconst, rwglu], writes=[rb[bk]], inc=(k == 3))
                    P.act(sg[:, ss, :], bank(bk), AF.Sigmoid, bias=bglu[:, cq:cq + 1],
                          reads=[rb[bk], rconst], writes=[rb[bk], rsg[ss]])
                    P.tt(YCAT[:, cq, tsl], YG[:, cq, tsl], sg[:, ss, :], ALU.mult, reads=[rYG, rsg[ss]],
                         writes=[rYC[t], rA])
            for b in range(NB):
                t = b // 4
                pa = 4 + (b % 2) * 2
                for dh in range(2):
                    for k in range(8):
                        P.mm(bank(pa + dh), YCAT[:, k, b * 128:(b + 1) * 128], WOUT[:, k, dh * 512:(dh + 1) * 512],
                             k == 0, k == 7, reads=[rYC[t], rwd], writes=[rb[pa + dh]], inc=(k == 7))
                for dh in range(2):
                    dsl = slice(dh * 512, (dh + 1) * 512)
                    P.tt(x_sb[:, b, dsl], bank(pa + dh), x_sb[:, b, dsl], ALU.add,
                         reads=[rb[pa + dh], rx[b]], writes=[rb[pa + dh], rx[b]])

        defer = Deferred()
        rs5 = setup_s5(defer) if stage != "ffn1" else None
        defer.replay(P, only_dma=True)
        rcv = setup_conv() if stage != "ffn1" else None
        for st_i in range(NST):
            t0 = st_i * TT
            x_sb = xbuf[:, st_i % 2]
            rx = rxs[st_i % 2]
            late_x = (st_i == 0 and rs5 is not None)
            if st_i + 1 < NST and not late_x:
                load_x(st_i + 1)
            if st_i == 0 or stage != "full":
                norm_to_hT(0)
            if st_i == 0 and rs5 is not None:
                nsl = (len(defer.ops) - defer.pos) // NF + 1
                ffn(w1g, w1u, w1d, [rA, rB, rs5], 0, st_i, hid_dep=[rA],
                    interleave=lambda: defer.replay(P, n=nsl),
                    mid_hook=lambda: defer.replay(P), wd_late=True)
                load_x(1)
            else:
                ffn(w1g, w1u, w1d, [rA, rB], 0, st_i, hid_dep=[rA, rB])
            if stage != "ffn1":
                if st_i == 0:
                    setup_conv_late(rcv)
                norm_to_hT(1)
                mixer(st_i, rs5, rcv)
            if stage == "full":
                norm_to_hT(2)
                hook = None
                if st_i + 1 < NST:
                    nxt = (st_i + 1) % 2
                    hook = (lambda nxt=nxt: norm_to_hT(0, xbuf[:, nxt], rxs[nxt], bank0=0, stt_=stat2, rst_=rstat2))
                ffn(w2g, w2u, w2d, [rA, rB], 1, st_i, mid_hook=hook, hid_dep=[rA, rB])
                for b in range(NB):
                    s = b % 2
                    P.act(sgf[:, s, :].bitcast(BF16), x_sb[:, b, :], AF.Square, accum_out=stat[:, 2 * NB + b:2 * NB + b + 1],
                          reads=[rx[b]], writes=[rsgf[s], rstat])
                rf_all = stat[:, 3 * NB:4 * NB]
                P.ts(rf_all, stat[:, 2 * NB:3 * NB], 1.0 / D, EPS, ALU.mult, ALU.add, reads=[rstat], writes=[rstat])
                P.act(rf_all, rf_all, AF.Sqrt, reads=[rstat], writes=[rstat])
                P.op("vector", lambda e: e.reciprocal(out=rf_all, in_=rf_all), reads=[rstat], writes=[rstat])
                for b in range(NB):
                    rstd = stat[:, 3 * NB + b:3 * NB + b + 1]
                    P.stt(x_sb[:, b, :], x_sb[:, b, :], rstd, gfb[:], ALU.mult, ALU.mult,
                          reads=[rx[b], rstat, rconst], writes=[rx[b]])
            for b in range(NB):
                P.dma("sync", y[t0 + b * 128:t0 + (b + 1) * 128, :], x_sb[:, b, :], ds_y, reads=[rx[b]])
        fin = (ds_y.sem, ds_y.count, "dma", ds_y)
        P._wait("sync", fin)

        with nc.Block() as block:
            @block.sync
            def _(e):
                for f in P.q["sync"]:
                    f(e)

            @block.scalar
            def _(e):
                for f in P.q["scalar"]:
                    f(e)

            @block.vector
            def _(e):
                for f in P.q["vector"]:
                    f(e)

            @block.gpsimd
            def _(e):
                for f in P.q["gpsimd"]:
                    f(e)

            @block.tensor
            def _(e):
                for f in P.q["tensor"]:
                    f(e)
    return nc


def make_consts():
    c = {}
    c["c_idb"] = np.eye(128, dtype=np.float32).astype(ml_dtypes.bfloat16)
    c["c_idf"] = np.eye(128, dtype=np.float32)
    p = np.arange(128)
    c["c_i32"] = (p[:, None] % 32 == np.arange(32)[None, :]).astype(np.float32)
    par = ((p // 16) % 2)
    c["c_par"] = np.stack([(par == 0), (par == 1), -1.0 * (par == 0), -1.0 * (par == 1)], 1).astype(np.float32)
    c["c_ramp"] = np.broadcast_to(np.arange(144, dtype=np.float32)[None, :], (128, 144)).copy()
    c["c_m64"] = ((p[:, None] // 64) == (p[None, :] // 64)).astype(np.float32) / 64.0
    return c


_NC_CACHE = {}


def kernel(**inputs):
    stage = "full"
    if stage not in _NC_CACHE:
        _NC_CACHE[stage] = build(stage)
    nc = _NC_CACHE[stage]
    consts = make_consts()
    shared = {k: np.ascontiguousarray(np.asarray(v)) for k, v in inputs.items() if k != "x"}
    x = np.asarray(inputs["x"])
    in_maps = []
    for b in range(8):
        m = dict(shared)
        m.update(consts)
        m["x"] = np.ascontiguousarray(x[b])
        in_maps.append(m)
    res = run_bass_kernel_spmd(nc, in_maps, core_ids=list(range(8)))
    return np.stack([r["y"] for r in res.results], 0).astype(np.float32)
```

```python
import math
from contextlib import ExitStack
import numpy as np
import ml_dtypes
import concourse.bass as bass
import concourse.mybir as mybir
from concourse.bass_utils import run_bass_kernel_spmd

F32 = mybir.dt.float32
BF16 = mybir.dt.bfloat16
AF = mybir.ActivationFunctionType
ALU = mybir.AluOpType

D = 1024
FF = 2816
NF = 22
KD = 8
L = 4096
TT = 512
NB = TT // 128
NTT = TT // 512
NST = L // TT
T = 4
NC = TT // T
EPS = 1e-6
NSLOT = 3
PI = math.pi
ZW = 32 + TT + 4
RW = TT + 32


class Res:
    __slots__ = ("name", "w", "rd")

    def __init__(self, name):
        self.name = name
        self.w = None
        self.rd = []


class DSem:
    def __init__(self, sem):
        self.sem = sem
        self.count = 0


class Prog:
    ENG = ("scalar", "vector", "gpsimd", "tensor", "sync")

    def __init__(self, nc, stack):
        self.nc = nc
        self.stack = stack
        self.q = {e: [] for e in self.ENG}
        self.esem = {e: stack.enter_context(nc.semaphore("pc_" + e)) for e in self.ENG}
        self.ecnt = {e: 0 for e in self.ENG}
        self.waited = {}
        self.pend_r = {e: [] for e in self.ENG}
        self.pend_w = {e: [] for e in self.ENG}

    def dsem(self, name):
        return DSem(self.stack.enter_context(self.nc.semaphore(name)))

    def _wait(self, eng, ev):
        if ev is None:
            return
        sem, val, src = ev[0], ev[1], ev[2]
        if src == "dma":
            val = max(val, ev[3].count)
        key = (eng, sem.num)
        if self.waited.get(key, 0) >= val:
            return
        self.waited[key] = val
        self.q[eng].append(lambda e, sem=sem, val=val: e.wait_ge(sem, val))

    def _deps(self, eng, reads, writes):
        for r in reads:
            self._wait(eng, r.w)
        for w in writes:
            self._wait(eng, w.w)
            for ev in w.rd:
                if ev is not None and ev[2] == eng:
                    continue
                self._wait(eng, ev)

    def op(self, eng, fn, reads=(), writes=(), inc=True):
        self._deps(eng, reads, writes)
        if inc:
            self.ecnt[eng] += 1
            val = self.ecnt[eng]
            sem = self.esem[eng]
            self.q[eng].append(lambda e, fn=fn, sem=sem: fn(e).then_inc(sem, 1))
            ev = (sem, val, eng)
            for r in self.pend_r[eng]:
                r.rd.append(ev)
            for w in self.pend_w[eng]:
                w.w = ev
                w.rd = []
            self.pend_r[eng] = []
            self.pend_w[eng] = []
            for r in reads:
                r.rd.append(ev)
            for w in writes:
                w.w = ev
                w.rd = []
        else:
            self.q[eng].append(lambda e, fn=fn: fn(e))
            self.pend_r[eng].extend(reads)
            self.pend_w[eng].extend(writes)

    def dma(self, eng, out, in_, ds, reads=(), writes=()):
        self._deps(eng, reads, writes)
        ds.count += 16
        val = ds.count
        sem = ds.sem
        self.q[eng].append(lambda e, out=out, in_=in_, sem=sem: e.dma_start(out=out, in_=in_).then_inc(sem, 16))
        ev = (sem, val, "dma", ds)
        for r in reads:
            r.rd.append(ev)
        for w in writes:
            w.w = ev
            w.rd = []
        return ev

    def mm(self, out, lhsT, rhs, start=True, stop=True, reads=(), writes=(), inc=False, tp=None):
        def fn(e):
            kw = {"skip_group_check": True}
            if tp is not None:
                kw["tile_position"] = tp
            return e.matmul(out, lhsT, rhs, start=start, stop=stop, **kw)
        self.op("tensor", fn, reads, writes, inc)

    def tr(self, out, in_, ident, reads=(), writes=(), inc=False):
        self.op("tensor", lambda e: e.transpose(out, in_, ident), reads, writes, inc)

    def act(self, out, in_, func, reads=(), writes=(), bias=None, scale=None, accum_out=None):
        def fn(e):
            kw = {}
            if bias is not None:
                kw["bias"] = bias
            if scale is not None:
                kw["scale"] = scale
            if accum_out is not None:
                kw["accum_out"] = accum_out
            return e.activation(out=out, in_=in_, func=func, **kw)
        self.op("scalar", fn, reads, writes)

    def tt(self, out, in0, in1, op, reads=(), writes=(), eng="vector"):
        self.op(eng, lambda e: e.tensor_tensor(out=out, in0=in0, in1=in1, op=op), reads, writes)

    def ts(self, out, in0, s1, s2, op0, op1=None, reads=(), writes=(), eng="vector"):
        def fn(e):
            if op1 is None:
                return e.tensor_scalar(out=out, in0=in0, scalar1=s1, scalar2=None, op0=op0)
            return e.tensor_scalar(out=out, in0=in0, scalar1=s1, scalar2=s2, op0=op0, op1=op1)
        self.op(eng, fn, reads, writes)

    def stt(self, out, in0, scalar, in1, op0, op1, reads=(), writes=(), eng="vector"):
        self.op(eng, lambda e: e.scalar_tensor_tensor(out=out, in0=in0, scalar=scalar, in1=in1, op0=op0, op1=op1),
                reads, writes)

    def cp(self, out, in_, reads=(), writes=(), eng="vector"):
        self.op(eng, lambda e: e.tensor_copy(out=out, in_=in_), reads, writes)

    def ms(self, ap, val, reads=(), writes=(), eng="vector"):
        self.op(eng, lambda e: e.memset(ap, val), reads, writes)


class Deferred:
    def __init__(self):
        self.ops = []
        self.pos = 0

    def __getattr__(self, name):
        def rec(*a, **k):
            self.ops.append((name, a, k))
        return rec

    def replay(self, P, n=None, only_dma=False):
        cnt = 0
        while self.pos < len(self.ops) and (n is None or cnt < n):
            name, a, k = self.ops[self.pos]
            if only_dma and name != "dma":
                break
            getattr(P, name)(*a, **k)
            self.pos += 1
            cnt += 1


def build(stage="full"):
    nc = bass.Bass("TRN2", target_bir_lowering=False)

    def din(name, shape, dt=F32):
        return nc.dram_tensor(name, list(shape), dt, kind="ExternalInput").ap()

    x = din("x", [L, D])
    y = nc.dram_tensor("y", [L, D], F32, kind="ExternalOutput").ap()
    g1 = din("ffn1_norm", [1, D]); gm = din("mix_norm", [1, D]); g2 = din("ffn2_norm", [1, D])
    gfin = din("final_norm", [D])
    w1g = din("ffn1_w_gate", [1, D, FF]); w1u = din("ffn1_w_up", [1, D, FF]); w1d = din("ffn1_w_down", [1, FF, D])
    w2g = din("ffn2_w_gate", [1, D, FF]); w2u = din("ffn2_w_up", [1, D, FF]); w2d = din("ffn2_w_down", [1, FF, D])
    w_in = din("w_in", [1, D, 1536]); w_out = din("w_out", [1, D, D])
    lam_re = din("s5_lam_re", [1, 32, 64]); lam_im = din("s5_lam_im", [1, 32, 64]); log_dt = din("s5_log_dt", [1, 32])
    b_re = din("s5_b_re", [1, 32, 64, 16]); b_im = din("s5_b_im", [1, 32, 64, 16])
    c_re = din("s5_c_re", [1, 32, 16, 64]); c_im = din("s5_c_im", [1, 32, 16, 64])
    s5_d = din("s5_d", [1, 512]); w_glu = din("s5_w_glu", [1, 512, 512]); b_glu = din("s5_b_glu", [1, 512])
    cw = din("conv_w_dw", [1, 31, 512]); cb = din("conv_b_dw", [1, 512])
    cg = din("conv_ln_g", [1, 512]); cbt = din("conv_ln_b", [1, 512])
    c_idb = din("c_idb", [128, 128], BF16)
    c_idf = din("c_idf", [128, 128])
    c_i32 = din("c_i32", [128, 32])
    c_par = din("c_par", [128, 4])
    c_ramp = din("c_ramp", [128, 144])
    c_m64 = din("c_m64", [128, 128])

    with ExitStack() as st:
        P = Prog(nc, st)
        st.enter_context(nc.allow_non_contiguous_dma(reason="small one-time parameter layouts"))

        def sb(name, shape, dt):
            return st.enter_context(nc.sbuf_tensor(name, list(shape), dt))

        xbuf = sb("xbuf", [128, 2, NB, D], F32)
        x_sb = xbuf[:, 0]
        hT = sb("hT", [128, KD, TT], BF16)
        hid = sb("hid", [128, NF * TT], BF16)
        wd_sb = sb("wd_sb", [128, NF * D], BF16)
        wgu = sb("wgu", [128, NSLOT, 2, KD, 256], BF16)
        hn = sb("hn", [128, 2, D], BF16)
        sg = sb("sg", [128, 2, 512], BF16)
        sgf = sb("sgf", [128, 2, 512], F32)
        gcol = sb("gcol", [128, 3, KD], F32)
        gfb = sb("gfb", [128, D], F32)
        stat = sb("stat", [128, 4 * NB], F32)
        idb = sb("idb", [128, 128], BF16)
        idf = sb("idf", [128, 128], F32)
        i32 = sb("i32", [128, 32], F32)
        par = sb("par", [128, 4], F32)
        ramp = sb("ramp", [128, 144], F32)
        m64 = sb("m64", [128, 128], F32)
        EC = sb("EC", [128, 16, NC + 1], F32)
        ES = sb("ES", [128, 16, NC + 1], F32)
        BZR = sb("BZR", [128, 4, T, 128], BF16)
        BZI = sb("BZI", [128, 4, T, 128], BF16)
        CZR = sb("CZR", [128, 16, T, 32], BF16)
        CZI = sb("CZI", [128, 16, T, 32], BF16)
        KDS = sb("KDS", [128, 4, T, 32], BF16)
        MAGT = sb("MAGT", [128, 16], F32)
        carR = sb("carR", [128, 16], F32)
        carI = sb("carI", [128, 16], F32)
        wglu_sb = sb("wglu_sb", [128, 4, 512], BF16)
        bglu = sb("bglu", [128, 4], F32)
        wcol = sb("wcol", [128, 16, 8], F32)
        wdiag = sb("wdiag", [128, 16, 8, 32], BF16)
        cvec = sb("cvec", [128, 3, 4], F32)
        dcol = sb("dcol", [128, 4], F32)

        pbig = [st.enter_context(nc.psum_tensor("pb%d" % i, [128, 1024], F32)) for i in range(4)]

        def bank(k):
            return pbig[k // 2][:, (k % 2) * 512:(k % 2) * 512 + 512]

        R = {}

        def res(name):
            if name not in R:
                R[name] = Res(name)
            return R[name]

        rb = [res("bank%d" % k) for k in range(8)]
        rxs = [[res("x%d_%d" % (p_, b)) for b in range(NB)] for p_ in range(2)]
        rx = rxs[0]
        rhT = [res("hT%d" % t) for t in range(NTT)]
        rhid = [res("hid%d" % t) for t in range(NTT)]
        rslot = [res("slot%d" % s) for s in range(NSLOT)]
        rwd = res("wd")
        rwdt = res("wdtail")
        rhn = [res("hn0"), res("hn1")]
        rsg = [res("sg0"), res("sg1")]
        rsgf = [res("sgf0"), res("sgf1")]
        rstat = res("stat")
        rconst = res("const")
        rA = res("arenaA")
        rB = res("arenaB")

        ds_xs = [P.dsem("ds_x0"), P.dsem("ds_x1")]
        ds_y = P.dsem("ds_y")
        ds_c = P.dsem("ds_c")
        ds_s5 = P.dsem("ds_s5")
        ds_cv = P.dsem("ds_cv")
        ds_slot = [P.dsem("ds_slot%d" % s) for s in range(NSLOT)]
        ds_wd = P.dsem("ds_wd")
        ds_rep = P.dsem("ds_rep")

        def load_x(si):
            par_ = si % 2
            for b in range(NB):
                P.dma("sync", xbuf[:, par_, b, :], x[si * TT + b * 128:si * TT + (b + 1) * 128, :], ds_xs[par_],
                      writes=[rxs[par_][b]] + ([rs5] if (si == 1 and rs5 is not None) else []))

        load_x(0)

        def cload(dst, src):
            P.dma("sync", dst, src, ds_c, writes=[rconst])

        cload(idb[:], c_idb[:]); cload(idf[:], c_idf[:]); cload(i32[:], c_i32[:])
        cload(par[:], c_par[:]); cload(ramp[:], c_ramp[:]); cload(m64[:], c_m64[:])
        for n, g in enumerate((g1, gm, g2)):
            cload(gcol[:, n, :], g[0].rearrange("(k p) -> p k", p=128))
        cload(gfb[:], gfin.partition_broadcast(128))
        cload(bglu[:], b_glu[0].rearrange("(q p) -> p q", p=128))
        cload(dcol[:], s5_d[0].rearrange("(q p) -> p q", p=128))
        for n, v in enumerate((cb, cg, cbt)):
            cload(cvec[:, n, :], v[0].rearrange("(q p) -> p q", p=128))
        ds_wglu = P.dsem("ds_wglu")
        rwglu = res("wglu")
        P.dma("gpsimd", wglu_sb[:], w_glu[0].rearrange("(k p) n -> p k n", p=128), ds_wglu, writes=[rwglu])

        hidF = hid[:].bitcast(F32)
        wdF = wd_sb[:].bitcast(F32)

        def carveF(base, off, shape):
            n = int(np.prod(shape))
            v = base[:, off:off + n]
            if len(shape) == 2:
                v = v.rearrange("p (a b) -> p a b", a=shape[0])
            elif len(shape) == 3:
                v = v.rearrange("p (a b c) -> p a b c", a=shape[0], b=shape[1])
            elif len(shape) == 4:
                v = v.rearrange("p (a b c d) -> p a b c d", a=shape[0], b=shape[1], c=shape[2])
            return v, off + n

        def setup_s5(P):
            rs = res("s5setup")
            o = 0
            LR, o = carveF(wdF, o, [16]); LI, o = carveF(wdF, o, [16]); LDT, o = carveF(wdF, o, [16])
            DT, o = carveF(wdF, o, [16]); LRD, o = carveF(wdF, o, [16]); LID, o = carveF(wdF, o, [16])
            THE, o = carveF(wdF, o, [16])
            BR, o = carveF(wdF, o, [16, 16]); BI, o = carveF(wdF, o, [16, 16])
            BBR, o = carveF(wdF, o, [16, 16]); BBI, o = carveF(wdF, o, [16, 16])
            TB1, o = carveF(wdF, o, [16, 16]); TB2, o = carveF(wdF, o, [16, 16])
            ARG, o = carveF(wdF, o, [16, 9]); MAG, o = carveF(wdF, o, [16, 9])
            COS, o = carveF(wdF, o, [16, 9]); SIN, o = carveF(wdF, o, [16, 9])
            PR, o = carveF(wdF, o, [16, 9]); PIm, o = carveF(wdF, o, [16, 9])
            T1, o = carveF(wdF, o, [16]); T2, o = carveF(wdF, o, [16]); T3, o = carveF(wdF, o, [16])
            FR, o = carveF(wdF, o, [16]); FI, o = carveF(wdF, o, [16])
            S8, o = carveF(wdF, o, [16]); SH, o = carveF(wdF, o, [16]); C8, o = carveF(wdF, o, [16]); TQ, o = carveF(wdF, o, [16])
            CN_R, o = carveF(wdF, o, [4, 64]); CN_I, o = carveF(wdF, o, [4, 64])
            CIN_R, o = carveF(wdF, o, [4, 128]); CIN_I, o = carveF(wdF, o, [4, 128])
            CTR, o = carveF(wdF, o, [4, 128]); CTI, o = carveF(wdF, o, [4, 128])
            xF = xbuf[:, 1].rearrange("p b d -> p (b d)")
            ox = 0
            ET1, ox = carveF(xF, ox, [16, NC // 2]); ET2, ox = carveF(xF, ox, [16, NC // 2])
            ET3, ox = carveF(xF, ox, [16, NC // 2]); ET4, ox = carveF(xF, ox, [16, NC // 2])
            assert ox <= NB * D
            o2 = o
            VBR, o2 = carveF(wdF, o2, [4, T, 128]); VBI, o2 = carveF(wdF, o2, [4, T, 128])
            TV1, o2 = carveF(wdF, o2, [T, 4, 16])
            TC1, o2 = carveF(wdF, o2, [4, T, 32]); TC2, o2 = carveF(wdF, o2, [4, T, 32])
            assert o2 <= 11264, o2

            def ld(dst, src):
                P.dma("sync", dst, src, ds_s5, writes=[rs])

            for h in range(2):
                hs = slice(64 * h, 64 * h + 64)
                ld(LR[hs, :], lam_re[0, h::2, :].rearrange("g p -> p g"))
                ld(LI[hs, :], lam_im[0, h::2, :].rearrange("g p -> p g"))
                ld(LDT[hs, :], log_dt[0:1, h::2].to_broadcast([64, 16]))
                ld(BR[hs, :, :], b_re[0, h::2, :, :].rearrange("g p c -> p g c"))
                ld(BI[hs, :, :], b_im[0, h::2, :, :].rearrange("g p c -> p g c"))
            ld(CN_R, c_re[0].rearrange("(q g) c p -> (g c) q p", q=4))
            ld(CN_I, c_im[0].rearrange("(q g) c p -> (g c) q p", q=4))

            rw = dict(reads=[rs, rconst], writes=[rs])
            V = "vector"
            P.act(DT, LDT, AF.Exp, **rw)
            P.tt(LRD, LR, DT, ALU.mult, **rw)
            P.tt(LID, LI, DT, ALU.mult, **rw)
            P.ts(THE, LID, float(T), None, ALU.mult, **rw)
            rmp9 = ramp[:, 0:9].unsqueeze(1).to_broadcast([128, 16, 9])
            P.tt(ARG, LRD.unsqueeze(2).to_broadcast([128, 16, 9]), rmp9, ALU.mult, **rw)
            P.act(MAG, ARG, AF.Exp, **rw)

            def cmul(oR, oI, aR, aI, bR, bI, t1, t2, t3, t4):
                P.tt(t1, aR, bR, ALU.mult, **rw)
                P.tt(t2, aI, bI, ALU.mult, **rw)
                P.tt(t3, aR, bI, ALU.mult, **rw)
                P.tt(t4, aI, bR, ALU.mult, **rw)
                P.tt(oR, t1, t2, ALU.subtract, **rw)
                P.tt(oI, t3, t4, ALU.add, **rw)

            P.act(S8, LID, AF.Sin, scale=1.0 / 8, **rw)
            P.act(SH, LID, AF.Sin, scale=1.0 / 16, **rw)
            P.tt(C8, SH, SH, ALU.mult, **rw)
            P.ts(C8, C8, -2.0, 1.0, ALU.mult, ALU.add, **rw)
            for _ in range(3):
                P.tt(T1, C8, C8, ALU.mult, **rw)
                P.tt(T2, S8, S8, ALU.mult, **rw)
                P.tt(T3, C8, S8, ALU.mult, **rw)
                P.tt(C8, T1, T2, ALU.subtract, **rw)
                P.ts(S8, T3, 2.0, None, ALU.mult, **rw)
            P.ms(COS[:, :, 0], 1.0, **rw)
            P.ms(SIN[:, :, 0], 0.0, **rw)
            P.ms(COS[:, :, T + 1:9], 1.0, **rw)
            P.ms(SIN[:, :, T + 1:9], 0.0, **rw)
            for d in range(1, T + 1):
                cmul(COS[:, :, d], SIN[:, :, d], COS[:, :, d - 1], SIN[:, :, d - 1], C8, S8, T1, T2, T3, TQ)
            P.tt(PR, MAG, COS, ALU.mult, **rw)
            P.tt(PIm, MAG, SIN, ALU.mult, **rw)
            P.cp(MAGT[:], MAG[:, :, T], **rw)
            P.ms(EC[:, :, 0], 1.0, **rw)
            P.ms(ES[:, :, 0], 0.0, **rw)
            P.cp(EC[:, :, 1], COS[:, :, T], **rw)
            P.cp(ES[:, :, 1], SIN[:, :, T], **rw)
            m = 1
            while m < NC:
                n = min(m, NC - m)
                bR = EC[:, :, m:m + 1].to_broadcast([128, 16, n])
                bI = ES[:, :, m:m + 1].to_broadcast([128, 16, n])
                cmul(EC[:, :, m + 1:m + 1 + n], ES[:, :, m + 1:m + 1 + n], EC[:, :, 1:1 + n], ES[:, :, 1:1 + n], bR, bI,
                     ET1[:, :, 0:n], ET2[:, :, 0:n], ET3[:, :, 0:n], ET4[:, :, 0:n])
                m += n
            P.ts(T1, PR[:, :, 1], -1.0, None, ALU.add, **rw)
            P.tt(T2, LR, LR, ALU.mult, **rw)
            P.tt(T3, LI, LI, ALU.mult, **rw)
            P.tt(T2, T2, T3, ALU.add, **rw)
            P.op(V, lambda e: e.reciprocal(out=T2, in_=T2), **rw)
            P.tt(FR, T1, LR, ALU.mult, **rw)
            P.tt(T3, PIm[:, :, 1], LI, ALU.mult, **rw)
            P.tt(FR, FR, T3, ALU.add, **rw)
            P.tt(FR, FR, T2, ALU.mult, **rw)
            P.tt(FI, PIm[:, :, 1], LR, ALU.mult, **rw)
            P.tt(T3, T1, LI, ALU.mult, **rw)
            P.tt(FI, FI, T3, ALU.subtract, **rw)
            P.tt(FI, FI, T2, ALU.mult, **rw)
            frb = FR.unsqueeze(2).to_broadcast([128, 16, 16])
            fib = FI.unsqueeze(2).to_broadcast([128, 16, 16])
            P.tt(BBR, BR, frb, ALU.mult, **rw)
            P.tt(TB1, BI, fib, ALU.mult, **rw)
            P.tt(BBR, BBR, TB1, ALU.subtract, **rw)
            P.tt(BBI, BI, frb, ALU.mult, **rw)
            P.tt(TB1, BR, fib, ALU.mult, **rw)
            P.tt(BBI, BBI, TB1, ALU.add, **rw)
            P.ms(VBR, 0.0, **rw)
            P.ms(VBI, 0.0, **rw)
            VBR5 = VBR.rearrange("p q d (g h c) -> p q d g h c", g=4, h=2)
            VBI5 = VBI.rearrange("p q d (g h c) -> p q d g h c", g=4, h=2)
            for q in range(4):
                for h in range(2):
                    hs = slice(64 * h, 64 * h + 64)
                    prb = PR[hs, 4 * q:4 * q + 4, 0:T].rearrange("p g d -> p d g").unsqueeze(3).to_broadcast([64, T, 4, 16])
                    pib = PIm[hs, 4 * q:4 * q + 4, 0:T].rearrange("p g d -> p d g").unsqueeze(3).to_broadcast([64, T, 4, 16])
                    bbr = BBR[hs, 4 * q:4 * q + 4, :].unsqueeze(1).to_broadcast([64, T, 4, 16])
                    bbi = BBI[hs, 4 * q:4 * q + 4, :].unsqueeze(1).to_broadcast([64, T, 4, 16])
                    oR = VBR5[hs, q, :, :, h, :]
                    oI = VBI5[hs, q, :, :, h, :]
                    t1 = TV1[hs]
                    P.tt(oR, prb, bbr, ALU.mult, **rw)
                    P.tt(t1, pib, bbi, ALU.mult, **rw)
                    P.tt(oR, oR, t1, ALU.subtract, **rw)
                    P.tt(oI, prb, bbi, ALU.mult, **rw)
                    P.tt(t1, pib, bbr, ALU.mult, **rw)
                    P.tt(oI, oI, t1, ALU.add, **rw)
            for (CN, CIN, pc) in ((CN_R, CIN_R, 0), (CN_I, CIN_I, 2)):
                P.ts(CIN[:, :, 0:64], CN, par[:, pc:pc + 1], None, ALU.mult, **rw)
                P.ts(CIN[:, :, 64:128], CN, par[:, pc + 1:pc + 2], None, ALU.mult, **rw)
            for (CIN, CT) in ((CIN_R, CTR), (CIN_I, CTI)):
                for q in range(4):
                    P.mm(bank(4)[:, q * 128:(q + 1) * 128], CIN[:, q, :], idf[:], True, True,
                         reads=[rs, rconst], writes=[rb[4]], inc=(q == 3))
                P.cp(CT, bank(4).rearrange("p (q c) -> p q c", q=4), reads=[rb[4]], writes=[rb[4], rs])
            for (VB, BZ) in ((VBR, BZR), (VBI, BZI)):
                for q in range(4):
                    for dh in range(T // 4):
                        bk = 5 + (dh % 2)
                        for dd in range(4):
                            d = dh * 4 + dd
                            P.mm(bank(bk)[:, dd * 128:(dd + 1) * 128], VB[:, q, d, :], idf[:], True, True,
                                 reads=[rs, rconst], writes=[rb[bk]], inc=(dd == 3))
                        for dd in range(4):
                            d = dh * 4 + dd
                            P.cp(BZ[:, q, T - 1 - d, :], bank(bk)[:, dd * 128:(dd + 1) * 128],
                                 reads=[rb[bk]], writes=[rb[bk], rs])
            for q in range(4):
                for d in range(T):
                    bk = 6 + (d // 4)
                    for g4 in range(4):
                        col = (d % 4) * 128 + g4 * 32
                        o_ = bank(bk)[32 * g4:32 * g4 + 32, col:col + 32]
                        last = (g4 == 3 and d % 4 == 3)
                        P.mm(o_, VBR[:, q, d, 32 * g4:32 * g4 + 32], CTR[:, q, 32 * g4:32 * g4 + 32], True, False,
                             reads=[rs], writes=[rb[bk]], tp=(0, 32 * g4))
                        P.mm(o_, VBI[:, q, d, 32 * g4:32 * g4 + 32], CTI[:, q, 32 * g4:32 * g4 + 32], False, True,
                             reads=[rs], writes=[rb[bk]], tp=(0, 32 * g4), inc=last)
                for dhh in range(T // 4):
                    bk = 6 + dhh
                    src = bank(bk).rearrange("p (d g c) -> p d g c", d=4, g=4)
                    for g4 in range(4):
                        ps_ = slice(32 * g4, 32 * g4 + 32)
                        if dhh == 0:
                            P.stt(KDS[ps_, q, 0, :], i32[ps_, :], dcol[ps_, q:q + 1], src[ps_, 0, g4, :],
                                  ALU.mult, ALU.add, reads=[rb[bk], rconst], writes=[rb[bk], rs])
                            P.cp(KDS[ps_, q, 1:4, :], src[ps_, 1:4, g4, :], reads=[rb[bk]], writes=[rb[bk], rs])
                        else:
                            P.cp(KDS[ps_, q, 4:8, :], src[ps_, :, g4, :], reads=[rb[bk]], writes=[rb[bk], rs])
            for q in range(4):
                ctr = CTR[:, q, :].rearrange("p (g c) -> p g c", g=4).unsqueeze(2).to_broadcast([128, 4, T, 32])
                cti = CTI[:, q, :].rearrange("p (g c) -> p g c", g=4).unsqueeze(2).to_broadcast([128, 4, T, 32])
                prb = PR[:, 4 * q:4 * q + 4, 1:T + 1].unsqueeze(3).to_broadcast([128, 4, T, 32])
                pib = PIm[:, 4 * q:4 * q + 4, 1:T + 1].unsqueeze(3).to_broadcast([128, 4, T, 32])
                P.tt(TC1, ctr, prb, ALU.mult, **rw)
                P.tt(TC2, cti, pib, ALU.mult, **rw)
                P.tt(CZR[:, 4 * q:4 * q + 4, :, :], TC1, TC2, ALU.add, **rw)
                P.tt(TC1, cti, prb, ALU.mult, **rw)
                P.tt(TC2, ctr, pib, ALU.mult, **rw)
                P.tt(CZI[:, 4 * q:4 * q + 4, :, :], TC1, TC2, ALU.subtract, **rw)
            P.ms(carR[:], 0.0, **rw)
            P.ms(carI[:], 0.0, **rw)
            return rs

        def setup_conv():
            rs = res("convsetup")
            P.ms(wcol[:], 0.0, reads=[], writes=[rs])
            for s in range(4):
                for r in range(8):
                    if 4 * r + s > 30:
                        continue
                    P.dma("sync", wcol[32 * s:32 * s + 32, :, r],
                          cw[0, 4 * r + s, :].rearrange("(g c) -> c g", c=32), ds_cv, reads=[], writes=[rs])
            return rs

        def setup_conv_late(rs):
            P.tt(wdiag[:], wcol[:].unsqueeze(3).to_broadcast([128, 16, 8, 32]),
                 i32[:].unsqueeze(1).unsqueeze(1).to_broadcast([128, 16, 8, 32]), ALU.mult,
                 reads=[rs, rconst], writes=[rs])

        stat2 = sb("stat2", [128, 2 * NB], F32)
        rstat2 = res("stat2")

        def norm_to_hT(nidx, xs=None, rxl=None, bank0=6, stt_=None, rst_=None):
            xs = x_sb if xs is None else xs
            rxl = rx if rxl is None else rxl
            stt_ = stat if stt_ is None else stt_
            rst_ = rstat if rst_ is None else rst_
            for b in range(NB):
                s = b % 2
                P.act(sgf[:, s, :].bitcast(BF16), xs[:, b, :], AF.Square, accum_out=stt_[:, b:b + 1],
                      reads=[rxl[b]], writes=[rsgf[s], rst_])
            rs_all = stt_[:, NB:2 * NB]
            P.ts(rs_all, stt_[:, 0:NB], 1.0 / D, EPS, ALU.mult, ALU.add, reads=[rst_], writes=[rst_])
            P.act(rs_all, rs_all, AF.Sqrt, reads=[rst_], writes=[rst_])
            P.op("vector", lambda e: e.reciprocal(out=rs_all, in_=rs_all), reads=[rst_], writes=[rst_])
            for b in range(NB):
                s = b % 2
                rstd = stt_[:, NB + b:NB + b + 1]
                P.ts(hn[:, s, :], xs[:, b, :], rstd, None, ALU.mult, reads=[rxl[b], rst_], writes=[rhn[s]])
                bk = bank0 + s
                pt = bank(bk).bitcast(BF16)
                for k in range(KD):
                    P.tr(pt[:, k * 128:(k + 1) * 128], hn[:, s, k * 128:(k + 1) * 128], idb[:],
                         reads=[rhn[s], rconst], writes=[rb[bk]], inc=(k == KD - 1))
                tt_ = b // 4
                P.tt(hT[:, :, b * 128:(b + 1) * 128], pt.rearrange("p (k t) -> p k t", k=KD),
                     gcol[:, nidx, :].unsqueeze(2).to_broadcast([128, KD, 128]), ALU.mult,
                     reads=[rb[bk], rconst], writes=[rb[bk], rhT[tt_]])

        scr_gu = [nc.dram_tensor("scr_gu%d" % i, [NF // 2, 128, 2 * KD * 256], BF16).ap() for i in range(2)]
        scr_wd = [nc.dram_tensor("scr_wd%d" % i, [128, NF * D], BF16).ap() for i in range(2)]
        rscr_gu = [[res("scrgu%d_%d" % (i, fp)) for fp in range(NF // 2)] for i in range(2)]
        rscr_wd = [res("scrwd%d" % i) for i in range(2)]
        ds_scr = P.dsem("ds_scr")

        ds_cvt = P.dsem("ds_cvt")

        def convert_ffn(fi_, wg, wu, wdn):
            wgv = wg[0].rearrange("(k p) n -> p k n", p=128)
            wuv = wu[0].rearrange("(k p) n -> p k n", p=128)
            wdv = wdn[0].rearrange("(f p) n -> p f n", p=128)
            for fp in range(NF // 2):
                dst = scr_gu[fi_][fp].rearrange("p (g k n) -> p g k n", g=2, k=KD)
                P.dma("gpsimd", dst[:, 0], wgv[:, :, fp * 256:(fp + 1) * 256], ds_cvt, writes=[rscr_gu[fi_][fp]])
                P.dma("gpsimd", dst[:, 1], wuv[:, :, fp * 256:(fp + 1) * 256], ds_cvt, writes=[rscr_gu[fi_][fp]])
            dstw = scr_wd[fi_].rearrange("p (f n) -> p f n", f=NF)
            P.dma("gpsimd", dstw[:, 0:11, :], wdv[:, 0:11, :], ds_cvt, writes=[rscr_wd[fi_]])
            P.dma("gpsimd", dstw[:, 11:22, :], wdv[:, 11:22, :], ds_cvt, writes=[rscr_wd[fi_]])

        def ffn(wg, wu, wdn, first_wd_dep, fi_, tile_i, mid_hook=None, hid_dep=(), interleave=None, wd_late=False):
            hid3 = hid[:].rearrange("p (f t) -> p f t", f=NF)
            wd3 = wd_sb[:].rearrange("p (f n) -> p f n", f=NF)
            wgv = wg[0].rearrange("(k p) n -> p k n", p=128)
            wuv = wu[0].rearrange("(k p) n -> p k n", p=128)
            wdv = wdn[0].rearrange("(f p) n -> p f n", p=128)

            NP_ = NF // 2

            def load_p(fp):
                s = fp % NSLOT
                flat = wgu[:, s].rearrange("p g k n -> p (g k n)")
                if tile_i == 0 and fi_ == 0:
                    P.dma("gpsimd", wgu[:, s, 0, :, :], wgv[:, :, fp * 256:(fp + 1) * 256], ds_slot[s], writes=[rslot[s]])
                    P.dma("gpsimd", wgu[:, s, 1, :, :], wuv[:, :, fp * 256:(fp + 1) * 256], ds_slot[s], writes=[rslot[s]])
                    P.dma("sync", scr_gu[fi_][fp], flat, ds_scr, reads=[rslot[s]], writes=[rscr_gu[fi_][fp]])
                else:
                    P.dma("gpsimd", flat, scr_gu[fi_][fp], ds_slot[s], reads=[rscr_gu[fi_][fp]], writes=[rslot[s]])

            for fp in range(min(NSLOT, NP_)):
                load_p(fp)
            def load_wd():
                if tile_i == 0 and fi_ == 0:
                    P.dma("gpsimd", wd3[:, 0:11, :], wdv[:, 0:11, :], ds_wd, writes=[rwd, rwdt] + list(first_wd_dep))
                    P.dma("gpsimd", wd3[:, 11:22, :], wdv[:, 11:22, :], ds_wd, writes=[rwd, rwdt])
                    P.dma("sync", scr_wd[fi_], wd_sb[:], ds_scr, reads=[rwd, rwdt], writes=[rscr_wd[fi_]])
                else:
                    P.dma("gpsimd", wd_sb[:], scr_wd[fi_], ds_wd, reads=[rscr_wd[fi_]],
                          writes=[rwd, rwdt] + list(first_wd_dep))

            if not wd_late:
                load_wd()
            it = 0
            for fp in range(NP_):
                s = fp % NSLOT
                for fi in range(2):
                    f = 2 * fp + fi
                    for t in range(NTT):
                        pa = (it % 2) * 2
                        it += 1
                        tsl = slice(t * 512, (t + 1) * 512)
                        for gu in range(2):
                            for k in range(KD):
                                P.mm(bank(pa + gu), wgu[:, s, gu, k, fi * 128:(fi + 1) * 128], hT[:, k, tsl], k == 0, k == KD - 1,
                                     reads=[rslot[s], rhT[t]], writes=[rb[pa + gu]], inc=(k == KD - 1))
                        ss = (it - 1) % 2
                        P.act(sg[:, ss, :], bank(pa), AF.Silu, reads=[rb[pa]], writes=[rb[pa], rsg[ss]])
                        P.tt(hid3[:, f, tsl], bank(pa + 1), sg[:, ss, :], ALU.mult,
                             reads=[rb[pa + 1], rsg[ss]], writes=[rb[pa + 1], rhid[t], rA] + list(hid_dep))
                    if interleave is not None:
                        interleave()
                if fp + NSLOT < NP_:
                    load_p(fp + NSLOT)
            if mid_hook is not None:
                mid_hook()
            if wd_late:
                load_wd()
            for b in range(NB):
                t = b // 4
                pa = 4 + (b % 2) * 2
                for dh in range(2):
                    for f in range(NF):
                        P.mm(bank(pa + dh), hid3[:, f, b * 128:(b + 1) * 128], wd3[:, f, dh * 512:(dh + 1) * 512],
                             f == 0, f == NF - 1, reads=[rhid[t], rwd, rwdt], writes=[rb[pa + dh]], inc=(f == NF - 1))
                for dh in range(2):
                    dsl = slice(dh * 512, (dh + 1) * 512)
                    P.stt(x_sb[:, b, dsl], bank(pa + dh), 0.5, x_sb[:, b, dsl], ALU.mult, ALU.add,
                          reads=[rb[pa + dh], rx[b]], writes=[rb[pa + dh], rx[b]])

        WOUT = wd_sb[:, 0:KD * D].rearrange("p (k n) -> p k n", k=KD)
        winv = w_in[0].rearrange("(k p) n -> p k n", p=128)
        oA = 0
        US5 = hid[:, oA:oA + 4 * TT].rearrange("p (q t) -> p q t", q=4); oA += 4 * TT
        ZB = hid[:, oA:oA + 4 * ZW].rearrange("p (q t) -> p q t", q=4); oA += 4 * ZW
        YG = hid[:, oA:oA + 4 * TT].rearrange("p (q t) -> p q t", q=4); oA += 4 * TT
        YCAT = hid[:, oA:oA + 8 * TT].rearrange("p (q t) -> p q t", q=8); oA += 8 * TT
        assert oA <= NF * TT, oA
        ZOFF = KD * D
        ZREP = wd_sb[:, ZOFF:ZOFF + 16 * RW].rearrange("p (g t) -> p g t", g=16)
        assert ZOFF + 16 * RW <= NF * D
        hTF = hT[:].rearrange("p k t -> p (k t)").bitcast(F32)
        oZ = 0
        WRe, oZ = carveF(hTF, oZ, [8, NC]); WIm, oZ = carveF(hTF, oZ, [8, NC])
        assert oZ <= KD * TT // 2, oZ
        ROFF = ZOFF + 16 * RW
        tailF = wd_sb[:, ROFF:NF * D].bitcast(F32)
        oZ = 0
        RRe, oZ = carveF(tailF, oZ, [8, NC + 1]); RIm, oZ = carveF(tailF, oZ, [8, NC + 1])
        assert oZ <= (NF * D - ROFF) // 2, oZ
        smallT = sb("smallT", [128, 4, 16], F32)
        SRb = sb("SRb", [128, 16, NC + 1], BF16)
        SIb = sb("SIb", [128, 16, NC + 1], BF16)
        zhist = sb("zhist", [128, 4, 32], BF16)
        rzh = res("zhist")
        czf = hn[:].rearrange("p s d -> p (s d)").bitcast(F32).rearrange("p (s d) -> p s d", s=2)
        rZB = res("zbuf"); rZBh = [res("zbuf_h0"), res("zbuf_h1")]; rZREP = res("zrep")
        rZREPh = [res("zrep_h0"), res("zrep_h1")]; ds_reph = [P.dsem("ds_rep0"), P.dsem("ds_rep1")]; rUS5 = res("us5"); rYG = res("yg"); rYC = [res("ycat%d" % t) for t in range(NTT)]
        rW = res("wrot"); rRR = res("rr"); rS = res("sfull"); rTM = res("tm"); rczf = rhn
        rcar = res("carry")

        def mixer(st_i, rs5, rcv):
            for fp in range(6):
                P.dma("gpsimd", wgu[:, fp % 3, fp // 3, :, :], winv[:, :, fp * 256:(fp + 1) * 256], ds_slot[fp % 3],
                      writes=[rslot[fp % 3]])
            P.dma("gpsimd", WOUT, w_out[0].rearrange("(k p) n -> p k n", p=128), ds_wd, writes=[rwd, rB])
            if stage == "full" and st_i == 0:
                convert_ffn(1, w2g, w2u, w2d)
            if st_i == 0:
                P.ms(zhist[:], 0.0, writes=[rzh])
            P.cp(ZB[:, :, 0:32], zhist[:], reads=[rzh], writes=[rZB, rZBh[0], rZBh[1], rA])
            P.ms(ZB[:, :, 32 + TT:ZW], 0.0, writes=[rZB, rZBh[0], rZBh[1], rA])
            it = 0
            for q in range(4):
                for t in range(NTT):
                    b1 = (it % 2) * 2
                    it += 1
                    tsl = slice(t * 512, (t + 1) * 512)
                    for hh, fp in enumerate((2 + q // 2, 4 + q // 2)):
                        sl, hf, fi = fp % 3, fp // 3, q % 2
                        for k in range(KD):
                            P.mm(bank(b1 + hh), wgu[:, sl, hf, k, fi * 128:(fi + 1) * 128], hT[:, k, tsl], k == 0, k == KD - 1,
                                 reads=[rslot[sl], rhT[t]], writes=[rb[b1 + hh]], inc=(k == KD - 1))
                    ss = it % 2
                    P.act(sg[:, ss, :], bank(b1 + 1), AF.Sigmoid, reads=[rb[b1 + 1]], writes=[rb[b1 + 1], rsg[ss]])
                    P.tt(ZB[:, q, 32 + t * 512:32 + (t + 1) * 512], bank(b1), sg[:, ss, :], ALU.mult,
                         reads=[rb[b1], rsg[ss]], writes=[rb[b1], rZBh[q // 2], rA])
                if q % 2 == 1:
                    hq = q // 2
                    for s in range(4):
                        for g4 in range(4):
                            g0 = 8 * hq + g4
                            P.dma("sync", ZREP[32 * s:32 * s + 32, g0:g0 + 5:4, :],
                                  ZB[32 * g4:32 * g4 + 32, 2 * hq:2 * hq + 2, s:s + RW], ds_reph[hq],
                                  reads=[rZBh[hq]], writes=[rZREPh[hq], rwdt])
            rUS5h = [res("us5_h0"), res("us5_h1")]

            def win_s5(cq):
                fp, fi = cq // 2, cq % 2
                sl, hf = fp % 3, fp // 3
                t = 0
                bk = 4 + cq
                tsl = slice(t * 512, (t + 1) * 512)
                for k in range(KD):
                    P.mm(bank(bk), wgu[:, sl, hf, k, fi * 128:(fi + 1) * 128], hT[:, k, tsl], k == 0, k == KD - 1,
                         reads=[rslot[sl], rhT[t]], writes=[rb[bk]], inc=(k == KD - 1))
                P.act(US5[:, cq, tsl], bank(bk), AF.Copy, reads=[rb[bk]], writes=[rb[bk], rUS5h[cq // 2], rUS5, rA])

            def z_half(h):
                for part, BZ in enumerate((BZR, BZI)):
                    for ql in range(2):
                        q = 2 * h + ql
                        c0 = part * 2 * NC + ql * NC
                        for i in range(T):
                            for g4 in range(4):
                                bk = 4 * h + g4
                                P.mm(bank(bk)[:, c0:c0 + NC], BZ[32 * g4:32 * g4 + 32, q, i, :],
                                     US5[32 * g4:32 * g4 + 32, q, i::T], i == 0, i == T - 1,
                                     reads=[rUS5h[h], rs5], writes=[rb[bk]], tp=(32 * g4, 0),
                                     inc=(i == T - 1 and part == 1 and ql == 1))

            win_s5(0)
            win_s5(1)
            z_half(0)
            win_s5(2)
            win_s5(3)
            z_half(1)
            rSh = [res("sfull_h0"), res("sfull_h1")]

            def s5_half(h):
                ECv = EC[:, 8 * h:8 * h + 8, :].rearrange("p (q g) k -> p g q k", g=4)
                ESv = ES[:, 8 * h:8 * h + 8, :].rearrange("p (q g) k -> p g q k", g=4)
                WRv = WRe.rearrange("p (q g) k -> p g q k", g=4)
                WIv = WIm.rearrange("p (q g) k -> p g q k", g=4)
                RRv = RRe.rearrange("p (q g) k -> p g q k", g=4)
                RIv = RIm.rearrange("p (q g) k -> p g q k", g=4)
                for a in range(2):
                    b0 = 4 * h + 2 * a
                    zz = pbig[b0 // 2][:, :].rearrange("p (b x) -> p b x", b=2)
                    zr = zz[:, :, 0:2 * NC].rearrange("p b (q c) -> p b q c", q=2)
                    zi = zz[:, :, 2 * NC:4 * NC].rearrange("p b (q c) -> p b q c", q=2)
                    ec = ECv[:, 2 * a:2 * a + 2, :, 1:NC + 1]
                    es = ESv[:, 2 * a:2 * a + 2, :, 1:NC + 1]
                    wr = WRv[:, 2 * a:2 * a + 2, :, :]
                    wi = WIv[:, 2 * a:2 * a + 2, :, :]
                    t1 = RRv[:, 2 * a:2 * a + 2, :, 1:NC + 1]
                    t2 = RIv[:, 2 * a:2 * a + 2, :, 1:NC + 1]
                    dep = dict(reads=[rb[b0], rb[b0 + 1], rs5, rS], writes=[rb[b0], rb[b0 + 1], rW, rRR, rhT[0]])
                    P.tt(wr, zr, ec, ALU.mult, **dep)
                    P.tt(t1, zi, es, ALU.mult, **dep)
                    P.tt(wr, wr, t1, ALU.add, **dep)
                    P.tt(wi, zi, ec, ALU.mult, **dep)
                    P.tt(t2, zr, es, ALU.mult, **dep)
                    P.tt(wi, wi, t2, ALU.subtract, **dep)
                gs = slice(8 * h, 8 * h + 8)
                P.cp(RRe[:, :, 0], carR[:, gs], reads=[rcar, rW], writes=[rRR])
                P.cp(RIm[:, :, 0], carI[:, gs], reads=[rcar, rW], writes=[rRR])
                for gl in range(8):
                    gp = 8 * h + gl
                    mg = MAGT[:, gp:gp + 1].to_broadcast([128, NC])
                    for (RX, WX, CAR) in ((RRe, WRe, carR), (RIm, WIm, carI)):
                        P.op("vector", lambda e, RX=RX, WX=WX, CAR=CAR, gp=gp, gl=gl, mg=mg: e.tensor_tensor_scan(
                            out=RX[:, gl, 1:NC + 1], data0=mg, data1=WX[:, gl, :], initial=CAR[:, gp:gp + 1],
                            op0=ALU.mult, op1=ALU.add), reads=[rW, rRR, rcar, rs5, rhT[0]], writes=[rRR])
                depc = dict(reads=[rRR, rs5], writes=[rTM])
                P.tt(smallT[:, 0, gs], EC[:, gs, NC], RRe[:, :, NC], ALU.mult, **depc)
                P.tt(smallT[:, 1, gs], ES[:, gs, NC], RIm[:, :, NC], ALU.mult, **depc)
                P.tt(smallT[:, 2, gs], ES[:, gs, NC], RRe[:, :, NC], ALU.mult, **depc)
                P.tt(smallT[:, 3, gs], EC[:, gs, NC], RIm[:, :, NC], ALU.mult, **depc)
                P.tt(carR[:, gs], smallT[:, 0, gs], smallT[:, 1, gs], ALU.subtract, reads=[rTM], writes=[rcar])
                P.tt(carI[:, gs], smallT[:, 2, gs], smallT[:, 3, gs], ALU.add, reads=[rTM], writes=[rcar])
                dep = dict(reads=[rRR, rs5, rW, rhT[0]], writes=[rSh[h], rS, rW])
                TM1 = WRe
                TM2 = WIm
                P.tt(TM1, EC[:, gs, 0:NC], RRe[:, :, 0:NC], ALU.mult, **dep)
                P.tt(TM2, ES[:, gs, 0:NC], RIm[:, :, 0:NC], ALU.mult, **dep)
                P.tt(SRb[:, gs, 0:NC], TM1, TM2, ALU.subtract, **dep)
                P.tt(TM1, ES[:, gs, 0:NC], RRe[:, :, 0:NC], ALU.mult, **dep)
                P.tt(TM2, EC[:, gs, 0:NC], RIm[:, :, 0:NC], ALU.mult, **dep)
                P.tt(SIb[:, gs, 0:NC], TM1, TM2, ALU.add, **dep)

            def conv_it(q):
                t = 0
                bk = q % 2
                cs = q % 2
                bm, bv = 2, 3
                for r in range(8):
                    for g4 in range(4):
                        grp = 4 * q + g4
                        c0 = 2 + 4 * r + t * 512
                        P.mm(bank(bk)[32 * g4:32 * g4 + 32, :], wdiag[:, grp, r, :], ZREP[:, grp, c0:c0 + 512],
                             r == 0, r == 7, reads=[rZREPh[q // 2], rcv], writes=[rb[bk]], tp=(0, 32 * g4),
                             inc=(r == 7 and g4 == 3))
                tsl = slice(t * 512, (t + 1) * 512)
                P.act(czf[:, cs, :], bank(bk), AF.Identity, bias=cvec[:, 0, q:q + 1],
                      reads=[rb[bk], rconst], writes=[rb[bk], rczf[cs]])
                P.act(sgf[:, cs, :], czf[:, cs, :], AF.Square, reads=[rczf[cs]], writes=[rsgf[cs]])
                P.mm(bank(bm), m64[:], czf[:, cs, :], True, True, reads=[rczf[cs], rconst], writes=[rb[bm]], inc=True)
                P.mm(bank(bv), m64[:], sgf[:, cs, :], True, True, reads=[rsgf[cs], rconst], writes=[rb[bv]], inc=True)
                P.tt(czf[:, cs, :], czf[:, cs, :], bank(bm), ALU.subtract, reads=[rb[bm], rczf[cs]],
                     writes=[rb[bm], rczf[cs]])
                P.act(sgf[:, cs, :], bank(bm), AF.Square, reads=[rb[bm]], writes=[rb[bm], rsgf[cs]])
                P.tt(sgf[:, cs, :], bank(bv), sgf[:, cs, :], ALU.subtract, reads=[rb[bv], rsgf[cs]],
                     writes=[rb[bv], rsgf[cs]])
                P.ts(sgf[:, cs, :], sgf[:, cs, :], EPS, None, ALU.add, reads=[rsgf[cs]], writes=[rsgf[cs]])
                P.act(sgf[:, cs, :], sgf[:, cs, :], AF.Sqrt, reads=[rsgf[cs]], writes=[rsgf[cs]])
                P.op("vector", lambda e, cs=cs: e.reciprocal(out=sgf[:, cs, :], in_=sgf[:, cs, :]),
                     reads=[rsgf[cs]], writes=[rsgf[cs]])
                P.tt(czf[:, cs, :], czf[:, cs, :], sgf[:, cs, :], ALU.mult, reads=[rczf[cs], rsgf[cs]],
                     writes=[rczf[cs]])
                P.act(YCAT[:, 4 + q, tsl], czf[:, cs, :], AF.Silu, scale=cvec[:, 1, q:q + 1], bias=cvec[:, 2, q:q + 1],
                      reads=[rczf[cs], rconst], writes=[rYC[t], rA])

            def y_q(q):
                rSq = rSh[q // 2]
                for g4 in range(4):
                    gp = 4 * q + g4
                    bk = g4
                    o_ = bank(bk)[:, q * NC:(q + 1) * NC]
                    P.mm(o_, CZR[:, gp].rearrange("p j c -> p (j c)"), SRb[:, gp, 0:NC], True, False,
                         reads=[rSq, rs5], writes=[rb[bk]])
                    P.mm(o_, CZI[:, gp].rearrange("p j c -> p (j c)"), SIb[:, gp, 0:NC], False, False,
                         reads=[rSq, rs5], writes=[rb[bk]])
                for j in range(T):
                    for i in range(j + 1):
                        for g4 in range(4):
                            bk = g4
                            o_ = bank(bk)[32 * j:32 * j + 32, q * NC:(q + 1) * NC]
                            P.mm(o_, KDS[32 * g4:32 * g4 + 32, q, j - i, :], US5[32 * g4:32 * g4 + 32, q, i::T],
                                 False, (i == j), reads=[rUS5h[q // 2], rs5], writes=[rb[bk]], tp=(32 * g4, 32 * j),
                                 inc=(i == j and j == T - 1))

            def y_evac(h):
                for g4 in range(4):
                    for j in range(T):
                        src = bank(g4)[32 * j:32 * j + 32, 2 * h * NC:(2 * h + 2) * NC].rearrange("p (q c) -> p q c", q=2)
                        dst = YG[32 * g4:32 * g4 + 32, 2 * h:2 * h + 2, j::T]
                        P.act(dst, src, AF.Gelu_apprx_tanh, reads=[rb[g4]], writes=[rb[g4], rYG, rA])

            s5_half(0)
            conv_it(0)
            conv_it(1)
            s5_half(1)
            y_q(0)
            y_q(1)
            y_evac(0)
            conv_it(2)
            conv_it(3)
            y_q(2)
            y_q(3)
            y_evac(1)
            P.cp(zhist[:], ZB[:, :, TT:TT + 32], reads=[rZB, rZBh[0], rZBh[1]], writes=[rzh])
            it = 0
            for cq in range(4):
                for t in range(NTT):
                    bk = it % 2
                    ss = it % 2
                    it += 1
                    tsl = slice(t * 512, (t + 1) * 512)
                    for k in range(4):
                        P.mm(bank(bk), wglu_sb[:, k, cq * 128:(cq + 1) * 128], YG[:, k, tsl], k == 0, k == 3,
                             reads=[rYG, rconst, rwglu], writes=[rb[bk]], inc=(k == 3))
                    P.act(sg[:, ss, :], bank(bk), AF.Sigmoid, bias=bglu[:, cq:cq + 1],
                          reads=[rb[bk], rconst], writes=[rb[bk], rsg[ss]])
                    P.tt(YCAT[:, cq, tsl], YG[:, cq, tsl], sg[:, ss, :], ALU.mult, reads=[rYG, rsg[ss]],
                         writes=[rYC[t], rA])
            for b in range(NB):
                t = b // 4
                pa = 4 + (b % 2) * 2
                for dh in range(2):
                    for k in range(8):
                        P.mm(bank(pa + dh), YCAT[:, k, b * 128:(b + 1) * 128], WOUT[:, k, dh * 512:(dh + 1) * 512],
                             k == 0, k == 7, reads=[rYC[t], rwd], writes=[rb[pa + dh]], inc=(k == 7))
                for dh in range(2):
                    dsl = slice(dh * 512, (dh + 1) * 512)
                    P.tt(x_sb[:, b, dsl], bank(pa + dh), x_sb[:, b, dsl], ALU.add,
                         reads=[rb[pa + dh], rx[b]], writes=[rb[pa + dh], rx[b]])

        defer = Deferred()
        rs5 = setup_s5(defer) if stage != "ffn1" else None
        defer.replay(P, only_dma=True)
        rcv = setup_conv() if stage != "ffn1" else None
        for st_i in range(NST):
            t0 = st_i * TT
            x_sb = xbuf[:, st_i % 2]
            rx = rxs[st_i % 2]
            late_x = (st_i == 0 and rs5 is not None)
            if st_i + 1 < NST and not late_x:
                load_x(st_i + 1)
            if st_i == 0 or stage != "full":
                norm_to_hT(0)
            if st_i == 0 and rs5 is not None:
                nsl = (len(defer.ops) - defer.pos) // NF + 1
                ffn(w1g, w1u, w1d, [rA, rB, rs5], 0, st_i, hid_dep=[rA],
                    interleave=lambda: defer.replay(P, n=nsl),
                    mid_hook=lambda: defer.replay(P), wd_late=True)
                load_x(1)
            else:
                ffn(w1g, w1u, w1d, [rA, rB], 0, st_i, hid_dep=[rA, rB])
            if stage != "ffn1":
                if st_i == 0:
                    setup_conv_late(rcv)
                norm_to_hT(1)
                mixer(st_i, rs5, rcv)
            if stage == "full":
                norm_to_hT(2)
                hook = None
                if st_i + 1 < NST:
                    nxt = (st_i + 1) % 2
                    hook = (lambda nxt=nxt: norm_to_hT(0, xbuf[:, nxt], rxs[nxt], bank0=0, stt_=stat2, rst_=rstat2))
                ffn(w2g, w2u, w2d, [rA, rB], 1, st_i, mid_hook=hook, hid_dep=[rA, rB])
                for b in range(NB):
                    s = b % 2
                    P.act(sgf[:, s, :].bitcast(BF16), x_sb[:, b, :], AF.Square, accum_out=stat[:, 2 * NB + b:2 * NB + b + 1],
                          reads=[rx[b]], writes=[rsgf[s], rstat])
                rf_all = stat[:, 3 * NB:4 * NB]
                P.ts(rf_all, stat[:, 2 * NB:3 * NB], 1.0 / D, EPS, ALU.mult, ALU.add, reads=[rstat], writes=[rstat])
                P.act(rf_all, rf_all, AF.Sqrt, reads=[rstat], writes=[rstat])
                P.op("vector", lambda e: e.reciprocal(out=rf_all, in_=rf_all), reads=[rstat], writes=[rstat])
                for b in range(NB):
                    rstd = stat[:, 3 * NB + b:3 * NB + b + 1]
                    P.stt(x_sb[:, b, :], x_sb[:, b, :], rstd, gfb[:], ALU.mult, ALU.mult,
                          reads=[rx[b], rstat, rconst], writes=[rx[b]])
            for b in range(NB):
                P.dma("sync", y[t0 + b * 128:t0 + (b + 1) * 128, :], x_sb[:, b, :], ds_y, reads=[rx[b]])
        fin = (ds_y.sem, ds_y.count, "dma", ds_y)
        P._wait("sync", fin)

        with nc.Block() as block:
            @block.sync
            def _(e):
                for f in P.q["sync"]:
                    f(e)

            @block.scalar
            def _(e):
                for f in P.q["scalar"]:
                    f(e)

            @block.vector
            def _(e):
                for f in P.q["vector"]:
                    f(e)

            @block.gpsimd
            def _(e):
                for f in P.q["gpsimd"]:
                    f(e)

            @block.tensor
            def _(e):
                for f in P.q["tensor"]:
                    f(e)
    return nc


def make_consts():
    c = {}
    c["c_idb"] = np.eye(128, dtype=np.float32).astype(ml_dtypes.bfloat16)
    c["c_idf"] = np.eye(128, dtype=np.float32)
    p = np.arange(128)
    c["c_i32"] = (p[:, None] % 32 == np.arange(32)[None, :]).astype(np.float32)
    par = ((p // 16) % 2)
    c["c_par"] = np.stack([(par == 0), (par == 1), -1.0 * (par == 0), -1.0 * (par == 1)], 1).astype(np.float32)
    c["c_ramp"] = np.broadcast_to(np.arange(144, dtype=np.float32)[None, :], (128, 144)).copy()
    c["c_m64"] = ((p[:, None] // 64) == (p[None, :] // 64)).astype(np.float32) / 64.0
    return c


_NC_CACHE = {}


def kernel(**inputs):
    stage = "full"
    if stage not in _NC_CACHE:
        _NC_CACHE[stage] = build(stage)
    nc = _NC_CACHE[stage]
    consts = make_consts()
    shared = {k: np.ascontiguousarray(np.asarray(v)) for k, v in inputs.items() if k != "x"}
    x = np.asarray(inputs["x"])
    in_maps = []
    for b in range(8):
        m = dict(shared)
        m.update(consts)
        m["x"] = np.ascontiguousarray(x[b])
        in_maps.append(m)
    res = run_bass_kernel_spmd(nc, in_maps, core_ids=list(range(8)))
    return np.stack([r["y"] for r in res.results], 0).astype(np.float32)
```

```python
import math
from contextlib import ExitStack
import numpy as np
import ml_dtypes
import concourse.bass as bass
import concourse.mybir as mybir
from concourse.bass_utils import run_bass_kernel_spmd

F32 = mybir.dt.float32
BF16 = mybir.dt.bfloat16
AF = mybir.ActivationFunctionType
ALU = mybir.AluOpType

D = 1024
FF = 2816
NF = 22
KD = 8
L = 4096
TT = 512
NB = TT // 128
NTT = TT // 512
NST = L // TT
T = 4
NC = TT // T
EPS = 1e-6
NSLOT = 3
PI = math.pi
ZW = 32 + TT + 4
RW = TT + 32


class Res:
    __slots__ = ("name", "w", "rd")

    def __init__(self, name):
        self.name = name
        self.w = None
        self.rd = []


class DSem:
    def __init__(self, sem):
        self.sem = sem
        self.count = 0


class Prog:
    ENG = ("scalar", "vector", "gpsimd", "tensor", "sync")

    def __init__(self, nc, stack):
        self.nc = nc
        self.stack = stack
        self.q = {e: [] for e in self.ENG}
        self.esem = {e: stack.enter_context(nc.semaphore("pc_" + e)) for e in self.ENG}
        self.ecnt = {e: 0 for e in self.ENG}
        self.waited = {}
        self.pend_r = {e: [] for e in self.ENG}
        self.pend_w = {e: [] for e in self.ENG}

    def dsem(self, name):
        return DSem(self.stack.enter_context(self.nc.semaphore(name)))

    def _wait(self, eng, ev):
        if ev is None:
            return
        sem, val, src = ev[0], ev[1], ev[2]
        if src == "dma":
            val = max(val, ev[3].count)
        key = (eng, sem.num)
        if self.waited.get(key, 0) >= val:
            return
        self.waited[key] = val
        self.q[eng].append(lambda e, sem=sem, val=val: e.wait_ge(sem, val))

    def _deps(self, eng, reads, writes):
        for r in reads:
            self._wait(eng, r.w)
        for w in writes:
            self._wait(eng, w.w)
            for ev in w.rd:
                if ev is not None and ev[2] == eng:
                    continue
                self._wait(eng, ev)

    def op(self, eng, fn, reads=(), writes=(), inc=True):
        self._deps(eng, reads, writes)
        if inc:
            self.ecnt[eng] += 1
            val = self.ecnt[eng]
            sem = self.esem[eng]
            self.q[eng].append(lambda e, fn=fn, sem=sem: fn(e).then_inc(sem, 1))
            ev = (sem, val, eng)
            for r in self.pend_r[eng]:
                r.rd.append(ev)
            for w in self.pend_w[eng]:
                w.w = ev
                w.rd = []
            self.pend_r[eng] = []
            self.pend_w[eng] = []
            for r in reads:
                r.rd.append(ev)
            for w in writes:
                w.w = ev
                w.rd = []
        else:
            self.q[eng].append(lambda e, fn=fn: fn(e))
            self.pend_r[eng].extend(reads)
            self.pend_w[eng].extend(writes)

    def dma(self, eng, out, in_, ds, reads=(), writes=()):
        self._deps(eng, reads, writes)
        ds.count += 16
        val = ds.count
        sem = ds.sem
        self.q[eng].append(lambda e, out=out, in_=in_, sem=sem: e.dma_start(out=out, in_=in_).then_inc(sem, 16))
        ev = (sem, val, "dma", ds)
        for r in reads:
            r.rd.append(ev)
        for w in writes:
            w.w = ev
            w.rd = []
        return ev

    def mm(self, out, lhsT, rhs, start=True, stop=True, reads=(), writes=(), inc=False, tp=None):
        def fn(e):
            kw = {"skip_group_check": True}
            if tp is not None:
                kw["tile_position"] = tp
            return e.matmul(out, lhsT, rhs, start=start, stop=stop, **kw)
        self.op("tensor", fn, reads, writes, inc)

    def tr(self, out, in_, ident, reads=(), writes=(), inc=False):
        self.op("tensor", lambda e: e.transpose(out, in_, ident), reads, writes, inc)

    def act(self, out, in_, func, reads=(), writes=(), bias=None, scale=None, accum_out=None):
        def fn(e):
            kw = {}
            if bias is not None:
                kw["bias"] = bias
            if scale is not None:
                kw["scale"] = scale
            if accum_out is not None:
                kw["accum_out"] = accum_out
            return e.activation(out=out, in_=in_, func=func, **kw)
        self.op("scalar", fn, reads, writes)

    def tt(self, out, in0, in1, op, reads=(), writes=(), eng="vector"):
        self.op(eng, lambda e: e.tensor_tensor(out=out, in0=in0, in1=in1, op=op), reads, writes)

    def ts(self, out, in0, s1, s2, op0, op1=None, reads=(), writes=(), eng="vector"):
        def fn(e):
            if op1 is None:
                return e.tensor_scalar(out=out, in0=in0, scalar1=s1, scalar2=None, op0=op0)
            return e.tensor_scalar(out=out, in0=in0, scalar1=s1, scalar2=s2, op0=op0, op1=op1)
        self.op(eng, fn, reads, writes)

    def stt(self, out, in0, scalar, in1, op0, op1, reads=(), writes=(), eng="vector"):
        self.op(eng, lambda e: e.scalar_tensor_tensor(out=out, in0=in0, scalar=scalar, in1=in1, op0=op0, op1=op1),
                reads, writes)

    def cp(self, out, in_, reads=(), writes=(), eng="vector"):
        self.op(eng, lambda e: e.tensor_copy(out=out, in_=in_), reads, writes)

    def ms(self, ap, val, reads=(), writes=(), eng="vector"):
        self.op(eng, lambda e: e.memset(ap, val), reads, writes)


class Deferred:
    def __init__(self):
        self.ops = []
        self.pos = 0

    def __getattr__(self, name):
        def rec(*a, **k):
            self.ops.append((name, a, k))
        return rec

    def replay(self, P, n=None, only_dma=False):
        cnt = 0
        while self.pos < len(self.ops) and (n is None or cnt < n):
            name, a, k = self.ops[self.pos]
            if only_dma and name != "dma":
                break
            getattr(P, name)(*a, **k)
            self.pos += 1
            cnt += 1


def build(stage="full"):
    nc = bass.Bass("TRN2", target_bir_lowering=False)

    def din(name, shape, dt=F32):
        return nc.dram_tensor(name, list(shape), dt, kind="ExternalInput").ap()

    x = din("x", [L, D])
    y = nc.dram_tensor("y", [L, D], F32, kind="ExternalOutput").ap()
    g1 = din("ffn1_norm", [1, D]); gm = din("mix_norm", [1, D]); g2 = din("ffn2_norm", [1, D])
    gfin = din("final_norm", [D])
    w1g = din("ffn1_w_gate", [1, D, FF]); w1u = din("ffn1_w_up", [1, D, FF]); w1d = din("ffn1_w_down", [1, FF, D])
    w2g = din("ffn2_w_gate", [1, D, FF]); w2u = din("ffn2_w_up", [1, D, FF]); w2d = din("ffn2_w_down", [1, FF, D])
    w_in = din("w_in", [1, D, 1536]); w_out = din("w_out", [1, D, D])
    lam_re = din("s5_lam_re", [1, 32, 64]); lam_im = din("s5_lam_im", [1, 32, 64]); log_dt = din("s5_log_dt", [1, 32])
    b_re = din("s5_b_re", [1, 32, 64, 16]); b_im = din("s5_b_im", [1, 32, 64, 16])
    c_re = din("s5_c_re", [1, 32, 16, 64]); c_im = din("s5_c_im", [1, 32, 16, 64])
    s5_d = din("s5_d", [1, 512]); w_glu = din("s5_w_glu", [1, 512, 512]); b_glu = din("s5_b_glu", [1, 512])
    cw = din("conv_w_dw", [1, 31, 512]); cb = din("conv_b_dw", [1, 512])
    cg = din("conv_ln_g", [1, 512]); cbt = din("conv_ln_b", [1, 512])
    c_idb = din("c_idb", [128, 128], BF16)
    c_idf = din("c_idf", [128, 128])
    c_i32 = din("c_i32", [128, 32])
    c_par = din("c_par", [128, 4])
    c_ramp = din("c_ramp", [128, 144])
    c_m64 = din("c_m64", [128, 128])

    with ExitStack() as st:
        P = Prog(nc, st)
        st.enter_context(nc.allow_non_contiguous_dma(reason="small one-time parameter layouts"))

        def sb(name, shape, dt):
            return st.enter_context(nc.sbuf_tensor(name, list(shape), dt))

        xbuf = sb("xbuf", [128, 2, NB, D], F32)
        x_sb = xbuf[:, 0]
        hT = sb("hT", [128, KD, TT], BF16)
        hid = sb("hid", [128, NF * TT], BF16)
        wd_sb = sb("wd_sb", [128, NF * D], BF16)
        wgu = sb("wgu", [128, NSLOT, 2, KD, 256], BF16)
        hn = sb("hn", [128, 2, D], BF16)
        sg = sb("sg", [128, 2, 512], BF16)
        sgf = sb("sgf", [128, 2, 512], F32)
        gcol = sb("gcol", [128, 3, KD], F32)
        gfb = sb("gfb", [128, D], F32)
        stat = sb("stat", [128, 4 * NB], F32)
        idb = sb("idb", [128, 128], BF16)
        idf = sb("idf", [128, 128], F32)
        i32 = sb("i32", [128, 32], F32)
        par = sb("par", [128, 4], F32)
        ramp = sb("ramp", [128, 144], F32)
        m64 = sb("m64", [128, 128], F32)
        EC = sb("EC", [128, 16, NC + 1], F32)
        ES = sb("ES", [128, 16, NC + 1], F32)
        BZR = sb("BZR", [128, 4, T, 128], BF16)
        BZI = sb("BZI", [128, 4, T, 128], BF16)
        CZR = sb("CZR", [128, 16, T, 32], BF16)
        CZI = sb("CZI", [128, 16, T, 32], BF16)
        KDS = sb("KDS", [128, 4, T, 32], BF16)
        MAGT = sb("MAGT", [128, 16], F32)
        carR = sb("carR", [128, 16], F32)
        carI = sb("carI", [128, 16], F32)
        wglu_sb = sb("wglu_sb", [128, 4, 512], BF16)
        bglu = sb("bglu", [128, 4], F32)
        wcol = sb("wcol", [128, 16, 8], F32)
        wdiag = sb("wdiag", [128, 16, 8, 32], BF16)
        cvec = sb("cvec", [128, 3, 4], F32)
        dcol = sb("dcol", [128, 4], F32)

        pall = st.enter_context(nc.psum_tensor("pall", [128, 8 * 512], F32))

        def bank(k):
            return pall[:, k * 512:(k + 1) * 512]

        R = {}

        def res(name):
            if name not in R:
                R[name] = Res(name)
            return R[name]

        rb = [res("bank%d" % k) for k in range(8)]
        rxs = [[res("x%d_%d" % (p_, b)) for b in range(NB)] for p_ in range(2)]
        rx = rxs[0]
        rhT = [res("hT%d" % t) for t in range(NTT)]
        rhid = [res("hid%d" % t) for t in range(NTT)]
        rslot = [res("slot%d" % s) for s in range(NSLOT)]
        rwd = res("wd")
        rwdt = res("wdtail")
        rhn = [res("hn0"), res("hn1")]
        rsg = [res("sg0"), res("sg1")]
        rsgf = [res("sgf0"), res("sgf1")]
        rstat = res("stat")
        rconst = res("const")
        rA = res("arenaA")
        rB = res("arenaB")

        ds_xs = [P.dsem("ds_x0"), P.dsem("ds_x1")]
        ds_y = P.dsem("ds_y")
        ds_c = P.dsem("ds_c")
        ds_s5 = P.dsem("ds_s5")
        ds_cv = P.dsem("ds_cv")
        ds_slot = [P.dsem("ds_slot%d" % s) for s in range(NSLOT)]
        ds_wd = P.dsem("ds_wd")
        ds_rep = P.dsem("ds_rep")

        def load_x(si):
            par_ = si % 2
            for b in range(NB):
                P.dma("sync", xbuf[:, par_, b, :], x[si * TT + b * 128:si * TT + (b + 1) * 128, :], ds_xs[par_],
                      writes=[rxs[par_][b]] + ([rs5] if (si == 1 and rs5 is not None) else []))

        load_x(0)

        def cload(dst, src):
            P.dma("sync", dst, src, ds_c, writes=[rconst])

        cload(idb[:], c_idb[:]); cload(idf[:], c_idf[:]); cload(i32[:], c_i32[:])
        cload(par[:], c_par[:]); cload(ramp[:], c_ramp[:]); cload(m64[:], c_m64[:])
        for n, g in enumerate((g1, gm, g2)):
            cload(gcol[:, n, :], g[0].rearrange("(k p) -> p k", p=128))
        cload(gfb[:], gfin.partition_broadcast(128))
        cload(bglu[:], b_glu[0].rearrange("(q p) -> p q", p=128))
        cload(dcol[:], s5_d[0].rearrange("(q p) -> p q", p=128))
        for n, v in enumerate((cb, cg, cbt)):
            cload(cvec[:, n, :], v[0].rearrange("(q p) -> p q", p=128))
        ds_wglu = P.dsem("ds_wglu")
        rwglu = res("wglu")
        P.dma("gpsimd", wglu_sb[:], w_glu[0].rearrange("(k p) n -> p k n", p=128), ds_wglu, writes=[rwglu])

        hidF = hid[:].bitcast(F32)
        wdF = wd_sb[:].bitcast(F32)

        def carveF(base, off, shape):
            n = int(np.prod(shape))
            v = base[:, off:off + n]
            if len(shape) == 2:
                v = v.rearrange("p (a b) -> p a b", a=shape[0])
            elif len(shape) == 3:
                v = v.rearrange("p (a b c) -> p a b c", a=shape[0], b=shape[1])
            elif len(shape) == 4:
                v = v.rearrange("p (a b c d) -> p a b c d", a=shape[0], b=shape[1], c=shape[2])
            return v, off + n

        def setup_s5(P):
            rs = res("s5setup")
            o = 0
            LR, o = carveF(wdF, o, [16]); LI, o = carveF(wdF, o, [16]); LDT, o = carveF(wdF, o, [16])
            DT, o = carveF(wdF, o, [16]); LRD, o = carveF(wdF, o, [16]); LID, o = carveF(wdF, o, [16])
            THE, o = carveF(wdF, o, [16])
            BR, o = carveF(wdF, o, [16, 16]); BI, o = carveF(wdF, o, [16, 16])
            BBR, o = carveF(wdF, o, [16, 16]); BBI, o = carveF(wdF, o, [16, 16])
            TB1, o = carveF(wdF, o, [16, 16]); TB2, o = carveF(wdF, o, [16, 16])
            ARG, o = carveF(wdF, o, [16, 9]); MAG, o = carveF(wdF, o, [16, 9])
            COS, o = carveF(wdF, o, [16, 9]); SIN, o = carveF(wdF, o, [16, 9])
            PR, o = carveF(wdF, o, [16, 9]); PIm, o = carveF(wdF, o, [16, 9])
            T1, o = carveF(wdF, o, [16]); T2, o = carveF(wdF, o, [16]); T3, o = carveF(wdF, o, [16])
            FR, o = carveF(wdF, o, [16]); FI, o = carveF(wdF, o, [16])
            S8, o = carveF(wdF, o, [16]); SH, o = carveF(wdF, o, [16]); C8, o = carveF(wdF, o, [16]); TQ, o = carveF(wdF, o, [16])
            CN_R, o = carveF(wdF, o, [4, 64]); CN_I, o = carveF(wdF, o, [4, 64])
            CIN_R, o = carveF(wdF, o, [4, 128]); CIN_I, o = carveF(wdF, o, [4, 128])
            CTR, o = carveF(wdF, o, [4, 128]); CTI, o = carveF(wdF, o, [4, 128])
            xF = xbuf[:, 1].rearrange("p b d -> p (b d)")
            ox = 0
            ET1, ox = carveF(xF, ox, [16, NC // 2]); ET2, ox = carveF(xF, ox, [16, NC // 2])
            ET3, ox = carveF(xF, ox, [16, NC // 2]); ET4, ox = carveF(xF, ox, [16, NC // 2])
            assert ox <= NB * D
            o2 = o
            VBR, o2 = carveF(wdF, o2, [4, T, 128]); VBI, o2 = carveF(wdF, o2, [4, T, 128])
            TV1, o2 = carveF(wdF, o2, [T, 4, 16])
            TC1, o2 = carveF(wdF, o2, [4, T, 32]); TC2, o2 = carveF(wdF, o2, [4, T, 32])
            assert o2 <= 11264, o2

            def ld(dst, src):
                P.dma("sync", dst, src, ds_s5, writes=[rs])

            for h in range(2):
                hs = slice(64 * h, 64 * h + 64)
                ld(LR[hs, :], lam_re[0, h::2, :].rearrange("g p -> p g"))
                ld(LI[hs, :], lam_im[0, h::2, :].rearrange("g p -> p g"))
                ld(LDT[hs, :], log_dt[0:1, h::2].to_broadcast([64, 16]))
                ld(BR[hs, :, :], b_re[0, h::2, :, :].rearrange("g p c -> p g c"))
                ld(BI[hs, :, :], b_im[0, h::2, :, :].rearrange("g p c -> p g c"))
            ld(CN_R, c_re[0].rearrange("(q g) c p -> (g c) q p", q=4))
            ld(CN_I, c_im[0].rearrange("(q g) c p -> (g c) q p", q=4))

            rw = dict(reads=[rs, rconst], writes=[rs])
            V = "vector"
            P.act(DT, LDT, AF.Exp, **rw)
            P.tt(LRD, LR, DT, ALU.mult, **rw)
            P.tt(LID, LI, DT, ALU.mult, **rw)
            P.ts(THE, LID, float(T), None, ALU.mult, **rw)
            rmp9 = ramp[:, 0:9].unsqueeze(1).to_broadcast([128, 16, 9])
            P.tt(ARG, LRD.unsqueeze(2).to_broadcast([128, 16, 9]), rmp9, ALU.mult, **rw)
            P.act(MAG, ARG, AF.Exp, **rw)

            def cmul(oR, oI, aR, aI, bR, bI, t1, t2, t3, t4):
                P.tt(t1, aR, bR, ALU.mult, **rw)
                P.tt(t2, aI, bI, ALU.mult, **rw)
                P.tt(t3, aR, bI, ALU.mult, **rw)
                P.tt(t4, aI, bR, ALU.mult, **rw)
                P.tt(oR, t1, t2, ALU.subtract, **rw)
                P.tt(oI, t3, t4, ALU.add, **rw)

            P.act(S8, LID, AF.Sin, scale=1.0 / 8, **rw)
            P.act(SH, LID, AF.Sin, scale=1.0 / 16, **rw)
            P.tt(C8, SH, SH, ALU.mult, **rw)
            P.ts(C8, C8, -2.0, 1.0, ALU.mult, ALU.add, **rw)
            for _ in range(3):
                P.tt(T1, C8, C8, ALU.mult, **rw)
                P.tt(T2, S8, S8, ALU.mult, **rw)
                P.tt(T3, C8, S8, ALU.mult, **rw)
                P.tt(C8, T1, T2, ALU.subtract, **rw)
                P.ts(S8, T3, 2.0, None, ALU.mult, **rw)
            P.ms(COS[:, :, 0], 1.0, **rw)
            P.ms(SIN[:, :, 0], 0.0, **rw)
            P.ms(COS[:, :, T + 1:9], 1.0, **rw)
            P.ms(SIN[:, :, T + 1:9], 0.0, **rw)
            for d in range(1, T + 1):
                cmul(COS[:, :, d], SIN[:, :, d], COS[:, :, d - 1], SIN[:, :, d - 1], C8, S8, T1, T2, T3, TQ)
            P.tt(PR, MAG, COS, ALU.mult, **rw)
            P.tt(PIm, MAG, SIN, ALU.mult, **rw)
            P.cp(MAGT[:], MAG[:, :, T], **rw)
            P.ms(EC[:, :, 0], 1.0, **rw)
            P.ms(ES[:, :, 0], 0.0, **rw)
            P.cp(EC[:, :, 1], COS[:, :, T], **rw)
            P.cp(ES[:, :, 1], SIN[:, :, T], **rw)
            m = 1
            while m < NC:
                n = min(m, NC - m)
                bR = EC[:, :, m:m + 1].to_broadcast([128, 16, n])
                bI = ES[:, :, m:m + 1].to_broadcast([128, 16, n])
                cmul(EC[:, :, m + 1:m + 1 + n], ES[:, :, m + 1:m + 1 + n], EC[:, :, 1:1 + n], ES[:, :, 1:1 + n], bR, bI,
                     ET1[:, :, 0:n], ET2[:, :, 0:n], ET3[:, :, 0:n], ET4[:, :, 0:n])
                m += n
            P.ts(T1, PR[:, :, 1], -1.0, None, ALU.add, **rw)
            P.tt(T2, LR, LR, ALU.mult, **rw)
            P.tt(T3, LI, LI, ALU.mult, **rw)
            P.tt(T2, T2, T3, ALU.add, **rw)
            P.op(V, lambda e: e.reciprocal(out=T2, in_=T2), **rw)
            P.tt(FR, T1, LR, ALU.mult, **rw)
            P.tt(T3, PIm[:, :, 1], LI, ALU.mult, **rw)
            P.tt(FR, FR, T3, ALU.add, **rw)
            P.tt(FR, FR, T2, ALU.mult, **rw)
            P.tt(FI, PIm[:, :, 1], LR, ALU.mult, **rw)
            P.tt(T3, T1, LI, ALU.mult, **rw)
            P.tt(FI, FI, T3, ALU.subtract, **rw)
            P.tt(FI, FI, T2, ALU.mult, **rw)
            frb = FR.unsqueeze(2).to_broadcast([128, 16, 16])
            fib = FI.unsqueeze(2).to_broadcast([128, 16, 16])
            P.tt(BBR, BR, frb, ALU.mult, **rw)
            P.tt(TB1, BI, fib, ALU.mult, **rw)
            P.tt(BBR, BBR, TB1, ALU.subtract, **rw)
            P.tt(BBI, BI, frb, ALU.mult, **rw)
            P.tt(TB1, BR, fib, ALU.mult, **rw)
            P.tt(BBI, BBI, TB1, ALU.add, **rw)
            P.ms(VBR, 0.0, **rw)
            P.ms(VBI, 0.0, **rw)
            VBR5 = VBR.rearrange("p q d (g h c) -> p q d g h c", g=4, h=2)
            VBI5 = VBI.rearrange("p q d (g h c) -> p q d g h c", g=4, h=2)
            for q in range(4):
                for h in range(2):
                    hs = slice(64 * h, 64 * h + 64)
                    prb = PR[hs, 4 * q:4 * q + 4, 0:T].rearrange("p g d -> p d g").unsqueeze(3).to_broadcast([64, T, 4, 16])
                    pib = PIm[hs, 4 * q:4 * q + 4, 0:T].rearrange("p g d -> p d g").unsqueeze(3).to_broadcast([64, T, 4, 16])
                    bbr = BBR[hs, 4 * q:4 * q + 4, :].unsqueeze(1).to_broadcast([64, T, 4, 16])
                    bbi = BBI[hs, 4 * q:4 * q + 4, :].unsqueeze(1).to_broadcast([64, T, 4, 16])
                    oR = VBR5[hs, q, :, :, h, :]
                    oI = VBI5[hs, q, :, :, h, :]
                    t1 = TV1[hs]
                    P.tt(oR, prb, bbr, ALU.mult, **rw)
                    P.tt(t1, pib, bbi, ALU.mult, **rw)
                    P.tt(oR, oR, t1, ALU.subtract, **rw)
                    P.tt(oI, prb, bbi, ALU.mult, **rw)
                    P.tt(t1, pib, bbr, ALU.mult, **rw)
                    P.tt(oI, oI, t1, ALU.add, **rw)
            for (CN, CIN, pc) in ((CN_R, CIN_R, 0), (CN_I, CIN_I, 2)):
                P.ts(CIN[:, :, 0:64], CN, par[:, pc:pc + 1], None, ALU.mult, **rw)
                P.ts(CIN[:, :, 64:128], CN, par[:, pc + 1:pc + 2], None, ALU.mult, **rw)
            for (CIN, CT) in ((CIN_R, CTR), (CIN_I, CTI)):
                for q in range(4):
                    P.mm(bank(4)[:, q * 128:(q + 1) * 128], CIN[:, q, :], idf[:], True, True,
                         reads=[rs, rconst], writes=[rb[4]], inc=(q == 3))
                P.cp(CT, bank(4).rearrange("p (q c) -> p q c", q=4), reads=[rb[4]], writes=[rb[4], rs])
            for (VB, BZ) in ((VBR, BZR), (VBI, BZI)):
                for q in range(4):
                    for dh in range(T // 4):
                        bk = 5 + (dh % 2)
                        for dd in range(4):
                            d = dh * 4 + dd
                            P.mm(bank(bk)[:, dd * 128:(dd + 1) * 128], VB[:, q, d, :], idf[:], True, True,
                                 reads=[rs, rconst], writes=[rb[bk]], inc=(dd == 3))
                        for dd in range(4):
                            d = dh * 4 + dd
                            P.cp(BZ[:, q, T - 1 - d, :], bank(bk)[:, dd * 128:(dd + 1) * 128],
                                 reads=[rb[bk]], writes=[rb[bk], rs])
            for q in range(4):
                for d in range(T):
                    bk = 6 + (d // 4)
                    for g4 in range(4):
                        col = (d % 4) * 128 + g4 * 32
                        o_ = bank(bk)[32 * g4:32 * g4 + 32, col:col + 32]
                        last = (g4 == 3 and d % 4 == 3)
                        P.mm(o_, VBR[:, q, d, 32 * g4:32 * g4 + 32], CTR[:, q, 32 * g4:32 * g4 + 32], True, False,
                             reads=[rs], writes=[rb[bk]], tp=(0, 32 * g4))
                        P.mm(o_, VBI[:, q, d, 32 * g4:32 * g4 + 32], CTI[:, q, 32 * g4:32 * g4 + 32], False, True,
                             reads=[rs], writes=[rb[bk]], tp=(0, 32 * g4), inc=last)
                for dhh in range(T // 4):
                    bk = 6 + dhh
                    src = bank(bk).rearrange("p (d g c) -> p d g c", d=4, g=4)
                    for g4 in range(4):
                        ps_ = slice(32 * g4, 32 * g4 + 32)
                        if dhh == 0:
                            P.stt(KDS[ps_, q, 0, :], i32[ps_, :], dcol[ps_, q:q + 1], src[ps_, 0, g4, :],
                                  ALU.mult, ALU.add, reads=[rb[bk], rconst], writes=[rb[bk], rs])
                            P.cp(KDS[ps_, q, 1:4, :], src[ps_, 1:4, g4, :], reads=[rb[bk]], writes=[rb[bk], rs])
                        else:
                            P.cp(KDS[ps_, q, 4:8, :], src[ps_, :, g4, :], reads=[rb[bk]], writes=[rb[bk], rs])
            for q in range(4):
                ctr = CTR[:, q, :].rearrange("p (g c) -> p g c", g=4).unsqueeze(2).to_broadcast([128, 4, T, 32])
                cti = CTI[:, q, :].rearrange("p (g c) -> p g c", g=4).unsqueeze(2).to_broadcast([128, 4, T, 32])
                prb = PR[:, 4 * q:4 * q + 4, 1:T + 1].unsqueeze(3).to_broadcast([128, 4, T, 32])
                pib = PIm[:, 4 * q:4 * q + 4, 1:T + 1].unsqueeze(3).to_broadcast([128, 4, T, 32])
                P.tt(TC1, ctr, prb, ALU.mult, **rw)
                P.tt(TC2, cti, pib, ALU.mult, **rw)
                P.tt(CZR[:, 4 * q:4 * q + 4, :, :], TC1, TC2, ALU.add, **rw)
                P.tt(TC1, cti, prb, ALU.mult, **rw)
                P.tt(TC2, ctr, pib, ALU.mult, **rw)
                P.tt(CZI[:, 4 * q:4 * q + 4, :, :], TC1, TC2, ALU.subtract, **rw)
            P.ms(carR[:], 0.0, **rw)
            P.ms(carI[:], 0.0, **rw)
            return rs

        def setup_conv():
            rs = res("convsetup")
            P.ms(wcol[:], 0.0, reads=[], writes=[rs])
            for s in range(4):
                for r in range(8):
                    if 4 * r + s > 30:
                        continue
                    P.dma("sync", wcol[32 * s:32 * s + 32, :, r],
                          cw[0, 4 * r + s, :].rearrange("(g c) -> c g", c=32), ds_cv, reads=[], writes=[rs])
            return rs

        def setup_conv_late(rs):
            P.tt(wdiag[:], wcol[:].unsqueeze(3).to_broadcast([128, 16, 8, 32]),
                 i32[:].unsqueeze(1).unsqueeze(1).to_broadcast([128, 16, 8, 32]), ALU.mult,
                 reads=[rs, rconst], writes=[rs])

        stat2 = sb("stat2", [128, 2 * NB], F32)
        rstat2 = res("stat2")

        def norm_to_hT(nidx, xs=None, rxl=None, bank0=6, stt_=None, rst_=None):
            xs = x_sb if xs is None else xs
            rxl = rx if rxl is None else rxl
            stt_ = stat if stt_ is None else stt_
            rst_ = rstat if rst_ is None else rst_
            for b in range(NB):
                s = b % 2
                P.act(sgf[:, s, :].bitcast(BF16), xs[:, b, :], AF.Square, accum_out=stt_[:, b:b + 1],
                      reads=[rxl[b]], writes=[rsgf[s], rst_])
            rs_all = stt_[:, NB:2 * NB]
            P.ts(rs_all, stt_[:, 0:NB], 1.0 / D, EPS, ALU.mult, ALU.add, reads=[rst_], writes=[rst_])
            P.act(rs_all, rs_all, AF.Sqrt, reads=[rst_], writes=[rst_])
            P.op("vector", lambda e: e.reciprocal(out=rs_all, in_=rs_all), reads=[rst_], writes=[rst_])
            for b in range(NB):
                s = b % 2
                rstd = stt_[:, NB + b:NB + b + 1]
                P.ts(hn[:, s, :], xs[:, b, :], rstd, None, ALU.mult, reads=[rxl[b], rst_], writes=[rhn[s]])
                bk = bank0 + s
                pt = bank(bk).bitcast(BF16)
                for k in range(KD):
                    P.tr(pt[:, k * 128:(k + 1) * 128], hn[:, s, k * 128:(k + 1) * 128], idb[:],
                         reads=[rhn[s], rconst], writes=[rb[bk]], inc=(k == KD - 1))
                tt_ = b // 4
                P.tt(hT[:, :, b * 128:(b + 1) * 128], pt.rearrange("p (k t) -> p k t", k=KD),
                     gcol[:, nidx, :].unsqueeze(2).to_broadcast([128, KD, 128]), ALU.mult,
                     reads=[rb[bk], rconst], writes=[rb[bk], rhT[tt_]])

        scr_gu = [nc.dram_tensor("scr_gu%d" % i, [NF // 2, 128, 2 * KD * 256], BF16).ap() for i in range(2)]
        scr_wd = [nc.dram_tensor("scr_wd%d" % i, [128, NF * D], BF16).ap() for i in range(2)]
        rscr_gu = [[res("scrgu%d_%d" % (i, fp)) for fp in range(NF // 2)] for i in range(2)]
        rscr_wd = [res("scrwd%d" % i) for i in range(2)]
        ds_scr = P.dsem("ds_scr")

        ds_cvt = P.dsem("ds_cvt")

        def convert_ffn(fi_, wg, wu, wdn):
            wgv = wg[0].rearrange("(k p) n -> p k n", p=128)
            wuv = wu[0].rearrange("(k p) n -> p k n", p=128)
            wdv = wdn[0].rearrange("(f p) n -> p f n", p=128)
            for fp in range(NF // 2):
                dst = scr_gu[fi_][fp].rearrange("p (g k n) -> p g k n", g=2, k=KD)
                P.dma("gpsimd", dst[:, 0], wgv[:, :, fp * 256:(fp + 1) * 256], ds_cvt, writes=[rscr_gu[fi_][fp]])
                P.dma("gpsimd", dst[:, 1], wuv[:, :, fp * 256:(fp + 1) * 256], ds_cvt, writes=[rscr_gu[fi_][fp]])
            dstw = scr_wd[fi_].rearrange("p (f n) -> p f n", f=NF)
            P.dma("gpsimd", dstw[:, 0:11, :], wdv[:, 0:11, :], ds_cvt, writes=[rscr_wd[fi_]])
            P.dma("gpsimd", dstw[:, 11:22, :], wdv[:, 11:22, :], ds_cvt, writes=[rscr_wd[fi_]])

        def ffn(wg, wu, wdn, first_wd_dep, fi_, tile_i, mid_hook=None, hid_dep=(), interleave=None, wd_late=False):
            hid3 = hid[:].rearrange("p (f t) -> p f t", f=NF)
            wd3 = wd_sb[:].rearrange("p (f n) -> p f n", f=NF)
            wgv = wg[0].rearrange("(k p) n -> p k n", p=128)
            wuv = wu[0].rearrange("(k p) n -> p k n", p=128)
            wdv = wdn[0].rearrange("(f p) n -> p f n", p=128)

            NP_ = NF // 2

            def load_p(fp):
                s = fp % NSLOT
                flat = wgu[:, s].rearrange("p g k n -> p (g k n)")
                if tile_i == 0 and fi_ == 0:
                    P.dma("gpsimd", wgu[:, s, 0, :, :], wgv[:, :, fp * 256:(fp + 1) * 256], ds_slot[s], writes=[rslot[s]])
                    P.dma("gpsimd", wgu[:, s, 1, :, :], wuv[:, :, fp * 256:(fp + 1) * 256], ds_slot[s], writes=[rslot[s]])
                    P.dma("sync", scr_gu[fi_][fp], flat, ds_scr, reads=[rslot[s]], writes=[rscr_gu[fi_][fp]])
                else:
                    P.dma("gpsimd", flat, scr_gu[fi_][fp], ds_slot[s], reads=[rscr_gu[fi_][fp]], writes=[rslot[s]])

            for fp in range(min(NSLOT, NP_)):
                load_p(fp)
            def load_wd():
                if tile_i == 0 and fi_ == 0:
                    P.dma("gpsimd", wd3[:, 0:11, :], wdv[:, 0:11, :], ds_wd, writes=[rwd, rwdt] + list(first_wd_dep))
                    P.dma("gpsimd", wd3[:, 11:22, :], wdv[:, 11:22, :], ds_wd, writes=[rwd, rwdt])
                    P.dma("sync", scr_wd[fi_], wd_sb[:], ds_scr, reads=[rwd, rwdt], writes=[rscr_wd[fi_]])
                else:
                    P.dma("gpsimd", wd_sb[:], scr_wd[fi_], ds_wd, reads=[rscr_wd[fi_]],
                          writes=[rwd, rwdt] + list(first_wd_dep))

            if not wd_late:
                load_wd()
            it = 0
            for fp in range(NP_):
                s = fp % NSLOT
                for fi in range(2):
                    f = 2 * fp + fi
                    for t in range(NTT):
                        pa = (it % 2) * 2
                        it += 1
                        tsl = slice(t * 512, (t + 1) * 512)
                        for gu in range(2):
                            for k in range(KD):
                                P.mm(bank(pa + gu), wgu[:, s, gu, k, fi * 128:(fi + 1) * 128], hT[:, k, tsl], k == 0, k == KD - 1,
                                     reads=[rslot[s], rhT[t]], writes=[rb[pa + gu]], inc=(k == KD - 1))
                        ss = (it - 1) % 2
                        P.act(sg[:, ss, :], bank(pa), AF.Silu, reads=[rb[pa]], writes=[rb[pa], rsg[ss]])
                        P.tt(hid3[:, f, tsl], bank(pa + 1), sg[:, ss, :], ALU.mult,
                             reads=[rb[pa + 1], rsg[ss]], writes=[rb[pa + 1], rhid[t], rA] + list(hid_dep))
                    if interleave is not None:
                        interleave()
                if fp + NSLOT < NP_:
                    load_p(fp + NSLOT)
            if mid_hook is not None:
                mid_hook()
            if wd_late:
                load_wd()
            for b in range(NB):
                t = b // 4
                pa = 4 + (b % 2) * 2
                for dh in range(2):
                    for f in range(NF):
                        P.mm(bank(pa + dh), hid3[:, f, b * 128:(b + 1) * 128], wd3[:, f, dh * 512:(dh + 1) * 512],
                             f == 0, f == NF - 1, reads=[rhid[t], rwd, rwdt], writes=[rb[pa + dh]], inc=(f == NF - 1))
                for dh in range(2):
                    dsl = slice(dh * 512, (dh + 1) * 512)
                    P.stt(x_sb[:, b, dsl], bank(pa + dh), 0.5, x_sb[:, b, dsl], ALU.mult, ALU.add,
                          reads=[rb[pa + dh], rx[b]], writes=[rb[pa + dh], rx[b]])

        WOUT = wd_sb[:, 0:KD * D].rearrange("p (k n) -> p k n", k=KD)
        winv = w_in[0].rearrange("(k p) n -> p k n", p=128)
        oA = 0
        US5 = hid[:, oA:oA + 4 * TT].rearrange("p (q t) -> p q t", q=4); oA += 4 * TT
        ZB = hid[:, oA:oA + 4 * ZW].rearrange("p (q t) -> p q t", q=4); oA += 4 * ZW
        YG = hid[:, oA:oA + 4 * TT].rearrange("p (q t) -> p q t", q=4); oA += 4 * TT
        YCAT = hid[:, oA:oA + 8 * TT].rearrange("p (q t) -> p q t", q=8); oA += 8 * TT
        assert oA <= NF * TT, oA
        ZOFF = KD * D
        ZREP = wd_sb[:, ZOFF:ZOFF + 16 * RW].rearrange("p (g t) -> p g t", g=16)
        assert ZOFF + 16 * RW <= NF * D
        hTF = hT[:].rearrange("p k t -> p (k t)").bitcast(F32)
        oZ = 0
        WRe, oZ = carveF(hTF, oZ, [8, NC]); WIm, oZ = carveF(hTF, oZ, [8, NC])
        assert oZ <= KD * TT // 2, oZ
        ROFF = ZOFF + 16 * RW
        tailF = wd_sb[:, ROFF:NF * D].bitcast(F32)
        oZ = 0
        RRe, oZ = carveF(tailF, oZ, [8, NC + 1]); RIm, oZ = carveF(tailF, oZ, [8, NC + 1])
        assert oZ <= (NF * D - ROFF) // 2, oZ
        smallT = sb("smallT", [128, 4, 16], F32)
        SRb = sb("SRb", [128, 16, NC + 1], BF16)
        SIb = sb("SIb", [128, 16, NC + 1], BF16)
        zhist = sb("zhist", [128, 4, 32], BF16)
        rzh = res("zhist")
        czf = hn[:].rearrange("p s d -> p (s d)").bitcast(F32).rearrange("p (s d) -> p s d", s=2)
        rZB = res("zbuf"); rZBh = [res("zbuf_h0"), res("zbuf_h1")]; rZREP = res("zrep")
        rZREPh = [res("zrep_h0"), res("zrep_h1")]; ds_reph = [P.dsem("ds_rep0"), P.dsem("ds_rep1")]; rUS5 = res("us5"); rYG = res("yg"); rYC = [res("ycat%d" % t) for t in range(NTT)]
        rW = res("wrot"); rRR = res("rr"); rS = res("sfull"); rTM = res("tm"); rczf = rhn
        rcar = res("carry")

        def mixer(st_i, rs5, rcv):
            for fp in range(6):
                P.dma("gpsimd", wgu[:, fp % 3, fp // 3, :, :], winv[:, :, fp * 256:(fp + 1) * 256], ds_slot[fp % 3],
                      writes=[rslot[fp % 3]])
            P.dma("gpsimd", WOUT, w_out[0].rearrange("(k p) n -> p k n", p=128), ds_wd, writes=[rwd, rB])
            if stage == "full" and st_i == 0:
                convert_ffn(1, w2g, w2u, w2d)
            if st_i == 0:
                P.ms(zhist[:], 0.0, writes=[rzh])
            P.cp(ZB[:, :, 0:32], zhist[:], reads=[rzh], writes=[rZB, rZBh[0], rZBh[1], rA])
            P.ms(ZB[:, :, 32 + TT:ZW], 0.0, writes=[rZB, rZBh[0], rZBh[1], rA])
            it = 0
            for q in range(4):
                for t in range(NTT):
                    b1 = (it % 2) * 2
                    it += 1
                    tsl = slice(t * 512, (t + 1) * 512)
                    for hh, fp in enumerate((2 + q // 2, 4 + q // 2)):
                        sl, hf, fi = fp % 3, fp // 3, q % 2
                        for k in range(KD):
                            P.mm(bank(b1 + hh), wgu[:, sl, hf, k, fi * 128:(fi + 1) * 128], hT[:, k, tsl], k == 0, k == KD - 1,
                                 reads=[rslot[sl], rhT[t]], writes=[rb[b1 + hh]], inc=(k == KD - 1))
                    ss = it % 2
                    P.act(sg[:, ss, :], bank(b1 + 1), AF.Sigmoid, reads=[rb[b1 + 1]], writes=[rb[b1 + 1], rsg[ss]])
                    P.tt(ZB[:, q, 32 + t * 512:32 + (t + 1) * 512], bank(b1), sg[:, ss, :], ALU.mult,
                         reads=[rb[b1], rsg[ss]], writes=[rb[b1], rZBh[q // 2], rA])
                if q % 2 == 1:
                    hq = q // 2
                    for s in range(4):
                        for g4 in range(4):
                            g0 = 8 * hq + g4
                            P.dma("sync", ZREP[32 * s:32 * s + 32, g0:g0 + 5:4, :],
                                  ZB[32 * g4:32 * g4 + 32, 2 * hq:2 * hq + 2, s:s + RW], ds_reph[hq],
                                  reads=[rZBh[hq]], writes=[rZREPh[hq], rwdt])
            rUS5h = [res("us5_h0"), res("us5_h1")]

            def win_s5(cq):
                fp, fi = cq // 2, cq % 2
                sl, hf = fp % 3, fp // 3
                t = 0
                bk = 4 + cq
                tsl = slice(t * 512, (t + 1) * 512)
                for k in range(KD):
                    P.mm(bank(bk), wgu[:, sl, hf, k, fi * 128:(fi + 1) * 128], hT[:, k, tsl], k == 0, k == KD - 1,
                         reads=[rslot[sl], rhT[t]], writes=[rb[bk]], inc=(k == KD - 1))
                P.act(US5[:, cq, tsl], bank(bk), AF.Copy, reads=[rb[bk]], writes=[rb[bk], rUS5h[cq // 2], rUS5, rA])

            def z_half(h):
                for part, BZ in enumerate((BZR, BZI)):
                    for ql in range(2):
                        q = 2 * h + ql
                        c0 = part * 2 * NC + ql * NC
                        for i in range(T):
                            for g4 in range(4):
                                bk = 4 * h + g4
                                P.mm(bank(bk)[:, c0:c0 + NC], BZ[32 * g4:32 * g4 + 32, q, i, :],
                                     US5[32 * g4:32 * g4 + 32, q, i::T], i == 0, i == T - 1,
                                     reads=[rUS5h[h], rs5], writes=[rb[bk]], tp=(32 * g4, 0),
                                     inc=(i == T - 1 and part == 1 and ql == 1))

            win_s5(0)
            win_s5(1)
            z_half(0)
            win_s5(2)
            win_s5(3)
            z_half(1)
            rSh = [res("sfull_h0"), res("sfull_h1")]

            def s5_half(h):
                ECv = EC[:, 8 * h:8 * h + 8, :].rearrange("p (q g) k -> p g q k", g=4)
                ESv = ES[:, 8 * h:8 * h + 8, :].rearrange("p (q g) k -> p g q k", g=4)
                WRv = WRe.rearrange("p (q g) k -> p g q k", g=4)
                WIv = WIm.rearrange("p (q g) k -> p g q k", g=4)
                RRv = RRe.rearrange("p (q g) k -> p g q k", g=4)
                RIv = RIm.rearrange("p (q g) k -> p g q k", g=4)
                zz = pall[:, 4 * h * 512:(4 * h + 4) * 512].rearrange("p (b x) -> p b x", b=4)
                zr = zz[:, :, 0:2 * NC].rearrange("p b (q c) -> p b q c", q=2)
                zi = zz[:, :, 2 * NC:4 * NC].rearrange("p b (q c) -> p b q c", q=2)
                ec = ECv[:, :, :, 1:NC + 1]
                es = ESv[:, :, :, 1:NC + 1]
                t1 = RRv[:, :, :, 1:NC + 1]
                t2 = RIv[:, :, :, 1:NC + 1]
                hb = [rb[4 * h + g] for g in range(4)]
                dep = dict(reads=hb + [rs5, rS], writes=hb + [rW, rRR, rhT[0]])
                P.tt(WRv, zr, ec, ALU.mult, **dep)
                P.tt(t1, zi, es, ALU.mult, **dep)
                P.tt(WRv, WRv, t1, ALU.add, **dep)
                P.tt(WIv, zi, ec, ALU.mult, **dep)
                P.tt(t2, zr, es, ALU.mult, **dep)
                P.tt(WIv, WIv, t2, ALU.subtract, **dep)
                gs = slice(8 * h, 8 * h + 8)
                P.cp(RRe[:, :, 0], carR[:, gs], reads=[rcar, rW], writes=[rRR])
                P.cp(RIm[:, :, 0], carI[:, gs], reads=[rcar, rW], writes=[rRR])
                for gl in range(8):
                    gp = 8 * h + gl
                    mg = MAGT[:, gp:gp + 1].to_broadcast([128, NC])
                    for (RX, WX, CAR) in ((RRe, WRe, carR), (RIm, WIm, carI)):
                        P.op("vector", lambda e, RX=RX, WX=WX, CAR=CAR, gp=gp, gl=gl, mg=mg: e.tensor_tensor_scan(
                            out=RX[:, gl, 1:NC + 1], data0=mg, data1=WX[:, gl, :], initial=CAR[:, gp:gp + 1],
                            op0=ALU.mult, op1=ALU.add), reads=[rW, rRR, rcar, rs5, rhT[0]], writes=[rRR])
                depc = dict(reads=[rRR, rs5], writes=[rTM])
                P.tt(smallT[:, 0, gs], EC[:, gs, NC], RRe[:, :, NC], ALU.mult, **depc)
                P.tt(smallT[:, 1, gs], ES[:, gs, NC], RIm[:, :, NC], ALU.mult, **depc)
                P.tt(smallT[:, 2, gs], ES[:, gs, NC], RRe[:, :, NC], ALU.mult, **depc)
                P.tt(smallT[:, 3, gs], EC[:, gs, NC], RIm[:, :, NC], ALU.mult, **depc)
                P.tt(carR[:, gs], smallT[:, 0, gs], smallT[:, 1, gs], ALU.subtract, reads=[rTM], writes=[rcar])
                P.tt(carI[:, gs], smallT[:, 2, gs], smallT[:, 3, gs], ALU.add, reads=[rTM], writes=[rcar])
                dep = dict(reads=[rRR, rs5, rW, rhT[0]], writes=[rSh[h], rS, rW])
                TM1 = WRe
                TM2 = WIm
                P.tt(TM1, EC[:, gs, 0:NC], RRe[:, :, 0:NC], ALU.mult, **dep)
                P.tt(TM2, ES[:, gs, 0:NC], RIm[:, :, 0:NC], ALU.mult, **dep)
                P.tt(SRb[:, gs, 0:NC], TM1, TM2, ALU.subtract, **dep)
                P.tt(TM1, ES[:, gs, 0:NC], RRe[:, :, 0:NC], ALU.mult, **dep)
                P.tt(TM2, EC[:, gs, 0:NC], RIm[:, :, 0:NC], ALU.mult, **dep)
                P.tt(SIb[:, gs, 0:NC], TM1, TM2, ALU.add, **dep)

            def conv_it(q):
                t = 0
                bk = q % 2
                cs = q % 2
                bm, bv = 2, 3
                for r in range(8):
                    for g4 in range(4):
                        grp = 4 * q + g4
                        c0 = 2 + 4 * r + t * 512
                        P.mm(bank(bk)[32 * g4:32 * g4 + 32, :], wdiag[:, grp, r, :], ZREP[:, grp, c0:c0 + 512],
                             r == 0, r == 7, reads=[rZREPh[q // 2], rcv], writes=[rb[bk]], tp=(0, 32 * g4),
                             inc=(r == 7 and g4 == 3))
                tsl = slice(t * 512, (t + 1) * 512)
                P.act(czf[:, cs, :], bank(bk), AF.Identity, bias=cvec[:, 0, q:q + 1],
                      reads=[rb[bk], rconst], writes=[rb[bk], rczf[cs]])
                P.act(sgf[:, cs, :], czf[:, cs, :], AF.Square, reads=[rczf[cs]], writes=[rsgf[cs]])
                P.mm(bank(bm), m64[:], czf[:, cs, :], True, True, reads=[rczf[cs], rconst], writes=[rb[bm]], inc=True)
                P.mm(bank(bv), m64[:], sgf[:, cs, :], True, True, reads=[rsgf[cs], rconst], writes=[rb[bv]], inc=True)
                P.tt(czf[:, cs, :], czf[:, cs, :], bank(bm), ALU.subtract, reads=[rb[bm], rczf[cs]],
                     writes=[rb[bm], rczf[cs]])
                P.act(sgf[:, cs, :], bank(bm), AF.Square, reads=[rb[bm]], writes=[rb[bm], rsgf[cs]])
                P.tt(sgf[:, cs, :], bank(bv), sgf[:, cs, :], ALU.subtract, reads=[rb[bv], rsgf[cs]],
                     writes=[rb[bv], rsgf[cs]])
                P.ts(sgf[:, cs, :], sgf[:, cs, :], EPS, None, ALU.add, reads=[rsgf[cs]], writes=[rsgf[cs]])
                P.act(sgf[:, cs, :], sgf[:, cs, :], AF.Sqrt, reads=[rsgf[cs]], writes=[rsgf[cs]])
                P.op("vector", lambda e, cs=cs: e.reciprocal(out=sgf[:, cs, :], in_=sgf[:, cs, :]),
                     reads=[rsgf[cs]], writes=[rsgf[cs]])
                P.tt(czf[:, cs, :], czf[:, cs, :], sgf[:, cs, :], ALU.mult, reads=[rczf[cs], rsgf[cs]],
                     writes=[rczf[cs]])
                P.act(YCAT[:, 4 + q, tsl], czf[:, cs, :], AF.Silu, scale=cvec[:, 1, q:q + 1], bias=cvec[:, 2, q:q + 1],
                      reads=[rczf[cs], rconst], writes=[rYC[t], rA])

            def y_q(q):
                rSq = rSh[q // 2]
                for g4 in range(4):
                    gp = 4 * q + g4
                    bk = g4
                    o_ = bank(bk)[:, q * NC:(q + 1) * NC]
                    P.mm(o_, CZR[:, gp].rearrange("p j c -> p (j c)"), SRb[:, gp, 0:NC], True, False,
                         reads=[rSq, rs5], writes=[rb[bk]])
                    P.mm(o_, CZI[:, gp].rearrange("p j c -> p (j c)"), SIb[:, gp, 0:NC], False, False,
                         reads=[rSq, rs5], writes=[rb[bk]])
                for j in range(T):
                    for i in range(j + 1):
                        for g4 in range(4):
                            bk = g4
                            o_ = bank(bk)[32 * j:32 * j + 32, q * NC:(q + 1) * NC]
                            P.mm(o_, KDS[32 * g4:32 * g4 + 32, q, j - i, :], US5[32 * g4:32 * g4 + 32, q, i::T],
                                 False, (i == j), reads=[rUS5h[q // 2], rs5], writes=[rb[bk]], tp=(32 * g4, 32 * j),
                                 inc=(i == j and j == T - 1))

            def y_evac(h):
                for g4 in range(4):
                    for j in range(T):
                        src = bank(g4)[32 * j:32 * j + 32, 2 * h * NC:(2 * h + 2) * NC].rearrange("p (q c) -> p q c", q=2)
                        dst = YG[32 * g4:32 * g4 + 32, 2 * h:2 * h + 2, j::T]
                        P.act(dst, src, AF.Gelu_apprx_tanh, reads=[rb[g4]], writes=[rb[g4], rYG, rA])

            s5_half(0)
            conv_it(0)
            conv_it(1)
            s5_half(1)
            y_q(0)
            y_q(1)
            y_evac(0)
            conv_it(2)
            conv_it(3)
            y_q(2)
            y_q(3)
            y_evac(1)
            P.cp(zhist[:], ZB[:, :, TT:TT + 32], reads=[rZB, rZBh[0], rZBh[1]], writes=[rzh])
            it = 0
            for cq in range(4):
                for t in range(NTT):
                    bk = it % 2
                    ss = it % 2
                    it += 1
                    tsl = slice(t * 512, (t + 1) * 512)
                    for k in range(4):
                        P.mm(bank(bk), wglu_sb[:, k, cq * 128:(cq + 1) * 128], YG[:, k, tsl], k == 0, k == 3,
                             reads=[rYG, rconst, rwglu], writes=[rb[bk]], inc=(k == 3))
                    P.act(sg[:, ss, :], bank(bk), AF.Sigmoid, bias=bglu[:, cq:cq + 1],
                          reads=[rb[bk], rconst], writes=[rb[bk], rsg[ss]])
                    P.tt(YCAT[:, cq, tsl], YG[:, cq, tsl], sg[:, ss, :], ALU.mult, reads=[rYG, rsg[ss]],
                         writes=[rYC[t], rA])
            for b in range(NB):
                t = b // 4
                pa = 4 + (b % 2) * 2
                for dh in range(2):
                    for k in range(8):
                        P.mm(bank(pa + dh), YCAT[:, k, b * 128:(b + 1) * 128], WOUT[:, k, dh * 512:(dh + 1) * 512],
                             k == 0, k == 7, reads=[rYC[t], rwd], writes=[rb[pa + dh]], inc=(k == 7))
                for dh in range(2):
                    dsl = slice(dh * 512, (dh + 1) * 512)
                    P.tt(x_sb[:, b, dsl], bank(pa + dh), x_sb[:, b, dsl], ALU.add,
                         reads=[rb[pa + dh], rx[b]], writes=[rb[pa + dh], rx[b]])

        defer = Deferred()
        rs5 = setup_s5(defer) if stage != "ffn1" else None
        defer.replay(P, only_dma=True)
        rcv = setup_conv() if stage != "ffn1" else None
        for st_i in range(NST):
            t0 = st_i * TT
            x_sb = xbuf[:, st_i % 2]
            rx = rxs[st_i % 2]
            late_x = (st_i == 0 and rs5 is not None)
            if st_i + 1 < NST and not late_x:
                load_x(st_i + 1)
            if st_i == 0 or stage != "full":
                norm_to_hT(0)
            if st_i == 0 and rs5 is not None:
                nsl = (len(defer.ops) - defer.pos) // NF + 1
                ffn(w1g, w1u, w1d, [rA, rB, rs5], 0, st_i, hid_dep=[rA],
                    interleave=lambda: defer.replay(P, n=nsl),
                    mid_hook=lambda: defer.replay(P), wd_late=True)
                load_x(1)
            else:
                ffn(w1g, w1u, w1d, [rA, rB], 0, st_i, hid_dep=[rA, rB])
            if stage != "ffn1":
                if st_i == 0:
                    setup_conv_late(rcv)
                norm_to_hT(1)
                mixer(st_i, rs5, rcv)
            if stage == "full":
                norm_to_hT(2)
                hook = None
                if st_i + 1 < NST:
                    nxt = (st_i + 1) % 2
                    hook = (lambda nxt=nxt: norm_to_hT(0, xbuf[:, nxt], rxs[nxt], bank0=0, stt_=stat2, rst_=rstat2))
                ffn(w2g, w2u, w2d, [rA, rB], 1, st_i, mid_hook=hook, hid_dep=[rA, rB])
                for b in range(NB):
                    s = b % 2
                    P.act(sgf[:, s, :].bitcast(BF16), x_sb[:, b, :], AF.Square, accum_out=stat[:, 2 * NB + b:2 * NB + b + 1],
                          reads=[rx[b]], writes=[rsgf[s], rstat])
                rf_all = stat[:, 3 * NB:4 * NB]
                P.ts(rf_all, stat[:, 2 * NB:3 * NB], 1.0 / D, EPS, ALU.mult, ALU.add, reads=[rstat], writes=[rstat])
                P.act(rf_all, rf_all, AF.Sqrt, reads=[rstat], writes=[rstat])
                P.op("vector", lambda e: e.reciprocal(out=rf_all, in_=rf_all), reads=[rstat], writes=[rstat])
                for b in range(NB):
                    rstd = stat[:, 3 * NB + b:3 * NB + b + 1]
                    P.stt(x_sb[:, b, :], x_sb[:, b, :], rstd, gfb[:], ALU.mult, ALU.mult,
                          reads=[rx[b], rstat, rconst], writes=[rx[b]])
            for b in range(NB):
                P.dma("sync", y[t0 + b * 128:t0 + (b + 1) * 128, :], x_sb[:, b, :], ds_y, reads=[rx[b]])
        fin = (ds_y.sem, ds_y.count, "dma", ds_y)
        P._wait("sync", fin)

        with nc.Block() as block:
            @block.sync
            def _(e):
                for f in P.q["sync"]:
                    f(e)

            @block.scalar
            def _(e):
                for f in P.q["scalar"]:
                    f(e)

            @block.vector
            def _(e):
                for f in P.q["vector"]:
                    f(e)

            @block.gpsimd
            def _(e):
                for f in P.q["gpsimd"]:
                    f(e)

            @block.tensor
            def _(e):
                for f in P.q["tensor"]:
                    f(e)
    return nc


def make_consts():
    c = {}
    c["c_idb"] = np.eye(128, dtype=np.float32).astype(ml_dtypes.bfloat16)
    c["c_idf"] = np.eye(128, dtype=np.float32)
    p = np.arange(128)
    c["c_i32"] = (p[:, None] % 32 == np.arange(32)[None, :]).astype(np.float32)
    par = ((p // 16) % 2)
    c["c_par"] = np.stack([(par == 0), (par == 1), -1.0 * (par == 0), -1.0 * (par == 1)], 1).astype(np.float32)
    c["c_ramp"] = np.broadcast_to(np.arange(144, dtype=np.float32)[None, :], (128, 144)).copy()
    c["c_m64"] = ((p[:, None] // 64) == (p[None, :] // 64)).astype(np.float32) / 64.0
    return c


_NC_CACHE = {}


def kernel(**inputs):
    stage = "full"
    if stage not in _NC_CACHE:
        _NC_CACHE[stage] = build(stage)
    nc = _NC_CACHE[stage]
    consts = make_consts()
    shared = {k: np.ascontiguousarray(np.asarray(v)) for k, v in inputs.items() if k != "x"}
    x = np.asarray(inputs["x"])
    in_maps = []
    for b in range(8):
        m = dict(shared)
        m.update(consts)
        m["x"] = np.ascontiguousarray(x[b])
        in_maps.append(m)
    res = run_bass_kernel_spmd(nc, in_maps, core_ids=list(range(8)))
    return np.stack([r["y"] for r in res.results], 0).astype(np.float32)
```

```python
import math
from contextlib import ExitStack
import numpy as np
import ml_dtypes
import concourse.bass as bass
import concourse.mybir as mybir
from concourse.bass_utils import run_bass_kernel_spmd

F32 = mybir.dt.float32
BF16 = mybir.dt.bfloat16
AF = mybir.ActivationFunctionType
ALU = mybir.AluOpType

D = 1024
FF = 2816
NF = 22
KD = 8
L = 4096
TT = 512
NB = TT // 128
NTT = TT // 512
NST = L // TT
T = 4
NC = TT // T
EPS = 1e-6
NSLOT = 3
PI = math.pi
ZW = 32 + TT + 4
RW = TT + 32


class Res:
    __slots__ = ("name", "w", "rd")

    def __init__(self, name):
        self.name = name
        self.w = None
        self.rd = []


class DSem:
    def __init__(self, sem):
        self.sem = sem
        self.count = 0


class Prog:
    ENG = ("scalar", "vector", "gpsimd", "tensor", "sync")

    def __init__(self, nc, stack):
        self.nc = nc
        self.stack = stack
        self.q = {e: [] for e in self.ENG}
        self.esem = {e: stack.enter_context(nc.semaphore("pc_" + e)) for e in self.ENG}
        self.ecnt = {e: 0 for e in self.ENG}
        self.waited = {}
        self.pend_r = {e: [] for e in self.ENG}
        self.pend_w = {e: [] for e in self.ENG}

    def dsem(self, name):
        return DSem(self.stack.enter_context(self.nc.semaphore(name)))

    def _wait(self, eng, ev):
        if ev is None:
            return
        sem, val, src = ev[0], ev[1], ev[2]
        if src == "dma":
            val = max(val, ev[3].count)
        key = (eng, sem.num)
        if self.waited.get(key, 0) >= val:
            return
        self.waited[key] = val
        self.q[eng].append(lambda e, sem=sem, val=val: e.wait_ge(sem, val))

    def _deps(self, eng, reads, writes):
        for r in reads:
            self._wait(eng, r.w)
        for w in writes:
            self._wait(eng, w.w)
            for ev in w.rd:
                if ev is not None and ev[2] == eng:
                    continue
                self._wait(eng, ev)

    def op(self, eng, fn, reads=(), writes=(), inc=True):
        self._deps(eng, reads, writes)
        if inc:
            self.ecnt[eng] += 1
            val = self.ecnt[eng]
            sem = self.esem[eng]
            self.q[eng].append(lambda e, fn=fn, sem=sem: fn(e).then_inc(sem, 1))
            ev = (sem, val, eng)
            for r in self.pend_r[eng]:
                r.rd.append(ev)
            for w in self.pend_w[eng]:
                w.w = ev
                w.rd = []
            self.pend_r[eng] = []
            self.pend_w[eng] = []
            for r in reads:
                r.rd.append(ev)
            for w in writes:
                w.w = ev
                w.rd = []
        else:
            self.q[eng].append(lambda e, fn=fn: fn(e))
            self.pend_r[eng].extend(reads)
            self.pend_w[eng].extend(writes)

    def dma(self, eng, out, in_, ds, reads=(), writes=()):
        self._deps(eng, reads, writes)
        ds.count += 16
        val = ds.count
        sem = ds.sem
        self.q[eng].append(lambda e, out=out, in_=in_, sem=sem: e.dma_start(out=out, in_=in_).then_inc(sem, 16))
        ev = (sem, val, "dma", ds)
        for r in reads:
            r.rd.append(ev)
        for w in writes:
            w.w = ev
            w.rd = []
        return ev

    def mm(self, out, lhsT, rhs, start=True, stop=True, reads=(), writes=(), inc=False, tp=None):
        def fn(e):
            kw = {"skip_group_check": True}
            if tp is not None:
                kw["tile_position"] = tp
            return e.matmul(out, lhsT, rhs, start=start, stop=stop, **kw)
        self.op("tensor", fn, reads, writes, inc)

    def tr(self, out, in_, ident, reads=(), writes=(), inc=False):
        self.op("tensor", lambda e: e.transpose(out, in_, ident), reads, writes, inc)

    def act(self, out, in_, func, reads=(), writes=(), bias=None, scale=None, accum_out=None):
        def fn(e):
            kw = {}
            if bias is not None:
                kw["bias"] = bias
            if scale is not None:
                kw["scale"] = scale
            if accum_out is not None:
                kw["accum_out"] = accum_out
            return e.activation(out=out, in_=in_, func=func, **kw)
        self.op("scalar", fn, reads, writes)

    def tt(self, out, in0, in1, op, reads=(), writes=(), eng="vector"):
        self.op(eng, lambda e: e.tensor_tensor(out=out, in0=in0, in1=in1, op=op), reads, writes)

    def ts(self, out, in0, s1, s2, op0, op1=None, reads=(), writes=(), eng="vector"):
        def fn(e):
            if op1 is None:
                return e.tensor_scalar(out=out, in0=in0, scalar1=s1, scalar2=None, op0=op0)
            return e.tensor_scalar(out=out, in0=in0, scalar1=s1, scalar2=s2, op0=op0, op1=op1)
        self.op(eng, fn, reads, writes)

    def stt(self, out, in0, scalar, in1, op0, op1, reads=(), writes=(), eng="vector"):
        self.op(eng, lambda e: e.scalar_tensor_tensor(out=out, in0=in0, scalar=scalar, in1=in1, op0=op0, op1=op1),
                reads, writes)

    def cp(self, out, in_, reads=(), writes=(), eng="vector"):
        self.op(eng, lambda e: e.tensor_copy(out=out, in_=in_), reads, writes)

    def ms(self, ap, val, reads=(), writes=(), eng="vector"):
        self.op(eng, lambda e: e.memset(ap, val), reads, writes)


class Deferred:
    def __init__(self):
        self.ops = []
        self.pos = 0

    def __getattr__(self, name):
        def rec(*a, **k):
            self.ops.append((name, a, k))
        return rec

    def replay(self, P, n=None, only_dma=False):
        cnt = 0
        while self.pos < len(self.ops) and (n is None or cnt < n):
            name, a, k = self.ops[self.pos]
            if only_dma and name != "dma":
                break
            getattr(P, name)(*a, **k)
            self.pos += 1
            cnt += 1


def build(stage="full"):
    nc = bass.Bass("TRN2", target_bir_lowering=False)

    def din(name, shape, dt=F32):
        return nc.dram_tensor(name, list(shape), dt, kind="ExternalInput").ap()

    x = din("x", [L, D])
    y = nc.dram_tensor("y", [L, D], F32, kind="ExternalOutput").ap()
    g1 = din("ffn1_norm", [1, D]); gm = din("mix_norm", [1, D]); g2 = din("ffn2_norm", [1, D])
    gfin = din("final_norm", [D])
    w1g = din("ffn1_w_gate", [1, D, FF]); w1u = din("ffn1_w_up", [1, D, FF]); w1d = din("ffn1_w_down", [1, FF, D])
    w2g = din("ffn2_w_gate", [1, D, FF]); w2u = din("ffn2_w_up", [1, D, FF]); w2d = din("ffn2_w_down", [1, FF, D])
    w_in = din("w_in", [1, D, 1536]); w_out = din("w_out", [1, D, D])
    lam_re = din("s5_lam_re", [1, 32, 64]); lam_im = din("s5_lam_im", [1, 32, 64]); log_dt = din("s5_log_dt", [1, 32])
    b_re = din("s5_b_re", [1, 32, 64, 16]); b_im = din("s5_b_im", [1, 32, 64, 16])
    c_re = din("s5_c_re", [1, 32, 16, 64]); c_im = din("s5_c_im", [1, 32, 16, 64])
    s5_d = din("s5_d", [1, 512]); w_glu = din("s5_w_glu", [1, 512, 512]); b_glu = din("s5_b_glu", [1, 512])
    cw = din("conv_w_dw", [1, 31, 512]); cb = din("conv_b_dw", [1, 512])
    cg = din("conv_ln_g", [1, 512]); cbt = din("conv_ln_b", [1, 512])
    c_idb = din("c_idb", [128, 128], BF16)
    c_idf = din("c_idf", [128, 128])
    c_i32 = din("c_i32", [128, 32])
    c_par = din("c_par", [128, 4])
    c_ramp = din("c_ramp", [128, 144])
    c_m64 = din("c_m64", [128, 128])

    with ExitStack() as st:
        P = Prog(nc, st)
        st.enter_context(nc.allow_non_contiguous_dma(reason="small one-time parameter layouts"))

        def sb(name, shape, dt):
            return st.enter_context(nc.sbuf_tensor(name, list(shape), dt))

        xbuf = sb("xbuf", [128, 2, NB, D], F32)
        x_sb = xbuf[:, 0]
        hT = sb("hT", [128, KD, TT], BF16)
        hid = sb("hid", [128, NF * TT], BF16)
        wd_sb = sb("wd_sb", [128, NF * D], BF16)
        wgu = sb("wgu", [128, NSLOT, 2, KD, 256], BF16)
        hn = sb("hn", [128, 2, D], BF16)
        sg = sb("sg", [128, 2, 512], BF16)
        sgf = sb("sgf", [128, 2, 512], F32)
        gcol = sb("gcol", [128, 3, KD], F32)
        gfb = sb("gfb", [128, D], F32)
        stat = sb("stat", [128, 4 * NB], F32)
        idb = sb("idb", [128, 128], BF16)
        idf = sb("idf", [128, 128], F32)
        i32 = sb("i32", [128, 32], F32)
        par = sb("par", [128, 4], F32)
        ramp = sb("ramp", [128, 144], F32)
        m64 = sb("m64", [128, 128], F32)
        EC = sb("EC", [128, 16, NC + 1], F32)
        ES = sb("ES", [128, 16, NC + 1], F32)
        BZR = sb("BZR", [128, 4, T, 128], BF16)
        BZI = sb("BZI", [128, 4, T, 128], BF16)
        CZR = sb("CZR", [128, 16, T, 32], BF16)
        CZI = sb("CZI", [128, 16, T, 32], BF16)
        KDS = sb("KDS", [128, 4, T, 32], BF16)
        MAGT = sb("MAGT", [128, 16], F32)
        carR = sb("carR", [128, 16], F32)
        carI = sb("carI", [128, 16], F32)
        wglu_sb = sb("wglu_sb", [128, 4, 512], BF16)
        bglu = sb("bglu", [128, 4], F32)
        wcol = sb("wcol", [128, 16, 8], F32)
        wdiag = sb("wdiag", [128, 16, 8, 32], BF16)
        cvec = sb("cvec", [128, 3, 4], F32)
        dcol = sb("dcol", [128, 4], F32)

        pbig = [st.enter_context(nc.psum_tensor("pb%d" % i, [128, 1024], F32)) for i in range(4)]

        def bank(k):
            return pbig[k // 2][:, (k % 2) * 512:(k % 2) * 512 + 512]

        R = {}

        def res(name):
            if name not in R:
                R[name] = Res(name)
            return R[name]

        rb = [res("bank%d" % k) for k in range(8)]
        rxs = [[res("x%d_%d" % (p_, b)) for b in range(NB)] for p_ in range(2)]
        rx = rxs[0]
        rhT = [res("hT%d" % t) for t in range(NTT)]
        rhid = [res("hid%d" % t) for t in range(NTT)]
        rslot = [res("slot%d" % s) for s in range(NSLOT)]
        rwd = res("wd")
        rwdt = res("wdtail")
        rhn = [res("hn0"), res("hn1")]
        rsg = [res("sg0"), res("sg1")]
        rsgf = [res("sgf0"), res("sgf1")]
        rstat = res("stat")
        rconst = res("const")
        rA = res("arenaA")
        rB = res("arenaB")

        ds_xs = [P.dsem("ds_x0"), P.dsem("ds_x1")]
        ds_y = P.dsem("ds_y")
        ds_c = P.dsem("ds_c")
        ds_s5 = P.dsem("ds_s5")
        ds_cv = P.dsem("ds_cv")
        ds_slot = [P.dsem("ds_slot%d" % s) for s in range(NSLOT)]
        ds_wd = P.dsem("ds_wd")
        ds_rep = P.dsem("ds_rep")

        def load_x(si):
            par_ = si % 2
            for b in range(NB):
                P.dma("sync", xbuf[:, par_, b, :], x[si * TT + b * 128:si * TT + (b + 1) * 128, :], ds_xs[par_],
                      writes=[rxs[par_][b]] + ([rs5] if (si == 1 and rs5 is not None) else []))

        load_x(0)

        def cload(dst, src):
            P.dma("sync", dst, src, ds_c, writes=[rconst])

        cload(idb[:], c_idb[:]); cload(idf[:], c_idf[:]); cload(i32[:], c_i32[:])
        cload(par[:], c_par[:]); cload(ramp[:], c_ramp[:]); cload(m64[:], c_m64[:])
        for n, g in enumerate((g1, gm, g2)):
            cload(gcol[:, n, :], g[0].rearrange("(k p) -> p k", p=128))
        cload(gfb[:], gfin.partition_broadcast(128))
        cload(bglu[:], b_glu[0].rearrange("(q p) -> p q", p=128))
        cload(dcol[:], s5_d[0].rearrange("(q p) -> p q", p=128))
        for n, v in enumerate((cb, cg, cbt)):
            cload(cvec[:, n, :], v[0].rearrange("(q p) -> p q", p=128))
        ds_wglu = P.dsem("ds_wglu")
        rwglu = res("wglu")
        P.dma("gpsimd", wglu_sb[:], w_glu[0].rearrange("(k p) n -> p k n", p=128), ds_wglu, writes=[rwglu])

        hidF = hid[:].bitcast(F32)
        wdF = wd_sb[:].bitcast(F32)

        def carveF(base, off, shape):
            n = int(np.prod(shape))
            v = base[:, off:off + n]
            if len(shape) == 2:
                v = v.rearrange("p (a b) -> p a b", a=shape[0])
            elif len(shape) == 3:
                v = v.rearrange("p (a b c) -> p a b c", a=shape[0], b=shape[1])
            elif len(shape) == 4:
                v = v.rearrange("p (a b c d) -> p a b c d", a=shape[0], b=shape[1], c=shape[2])
            return v, off + n

        def setup_s5(P):
            rs = res("s5setup")
            o = 0
            LR, o = carveF(wdF, o, [16]); LI, o = carveF(wdF, o, [16]); LDT, o = carveF(wdF, o, [16])
            DT, o = carveF(wdF, o, [16]); LRD, o = carveF(wdF, o, [16]); LID, o = carveF(wdF, o, [16])
            THE, o = carveF(wdF, o, [16])
            BR, o = carveF(wdF, o, [16, 16]); BI, o = carveF(wdF, o, [16, 16])
            BBR, o = carveF(wdF, o, [16, 16]); BBI, o = carveF(wdF, o, [16, 16])
            TB1, o = carveF(wdF, o, [16, 16]); TB2, o = carveF(wdF, o, [16, 16])
            ARG, o = carveF(wdF, o, [16, 9]); MAG, o = carveF(wdF, o, [16, 9])
            COS, o = carveF(wdF, o, [16, 9]); SIN, o = carveF(wdF, o, [16, 9])
            PR, o = carveF(wdF, o, [16, 9]); PIm, o = carveF(wdF, o, [16, 9])
            T1, o = carveF(wdF, o, [16]); T2, o = carveF(wdF, o, [16]); T3, o = carveF(wdF, o, [16])
            FR, o = carveF(wdF, o, [16]); FI, o = carveF(wdF, o, [16])
            S8, o = carveF(wdF, o, [16]); SH, o = carveF(wdF, o, [16]); C8, o = carveF(wdF, o, [16]); TQ, o = carveF(wdF, o, [16])
            CN_R, o = carveF(wdF, o, [4, 64]); CN_I, o = carveF(wdF, o, [4, 64])
            CIN_R, o = carveF(wdF, o, [4, 128]); CIN_I, o = carveF(wdF, o, [4, 128])
            CTR, o = carveF(wdF, o, [4, 128]); CTI, o = carveF(wdF, o, [4, 128])
            xF = xbuf[:, 1].rearrange("p b d -> p (b d)")
            ox = 0
            ET1, ox = carveF(xF, ox, [16, NC // 2]); ET2, ox = carveF(xF, ox, [16, NC // 2])
            ET3, ox = carveF(xF, ox, [16, NC // 2]); ET4, ox = carveF(xF, ox, [16, NC // 2])
            assert ox <= NB * D
            o2 = o
            VBR, o2 = carveF(wdF, o2, [4, T, 128]); VBI, o2 = carveF(wdF, o2, [4, T, 128])
            TV1, o2 = carveF(wdF, o2, [T, 4, 16])
            TC1, o2 = carveF(wdF, o2, [4, T, 32]); TC2, o2 = carveF(wdF, o2, [4, T, 32])
            assert o2 <= 11264, o2

            def ld(dst, src):
                P.dma("sync", dst, src, ds_s5, writes=[rs])

            for h in range(2):
                hs = slice(64 * h, 64 * h + 64)
                ld(LR[hs, :], lam_re[0, h::2, :].rearrange("g p -> p g"))
                ld(LI[hs, :], lam_im[0, h::2, :].rearrange("g p -> p g"))
                ld(LDT[hs, :], log_dt[0:1, h::2].to_broadcast([64, 16]))
                ld(BR[hs, :, :], b_re[0, h::2, :, :].rearrange("g p c -> p g c"))
                ld(BI[hs, :, :], b_im[0, h::2, :, :].rearrange("g p c -> p g c"))
            ld(CN_R, c_re[0].rearrange("(q g) c p -> (g c) q p", q=4))
            ld(CN_I, c_im[0].rearrange("(q g) c p -> (g c) q p", q=4))

            rw = dict(reads=[rs, rconst], writes=[rs])
            V = "vector"
            P.act(DT, LDT, AF.Exp, **rw)
            P.tt(LRD, LR, DT, ALU.mult, **rw)
            P.tt(LID, LI, DT, ALU.mult, **rw)
            P.ts(THE, LID, float(T), None, ALU.mult, **rw)
            rmp9 = ramp[:, 0:9].unsqueeze(1).to_broadcast([128, 16, 9])
            P.tt(ARG, LRD.unsqueeze(2).to_broadcast([128, 16, 9]), rmp9, ALU.mult, **rw)
            P.act(MAG, ARG, AF.Exp, **rw)

            def cmul(oR, oI, aR, aI, bR, bI, t1, t2, t3, t4):
                P.tt(t1, aR, bR, ALU.mult, **rw)
                P.tt(t2, aI, bI, ALU.mult, **rw)
                P.tt(t3, aR, bI, ALU.mult, **rw)
                P.tt(t4, aI, bR, ALU.mult, **rw)
                P.tt(oR, t1, t2, ALU.subtract, **rw)
                P.tt(oI, t3, t4, ALU.add, **rw)

            P.act(S8, LID, AF.Sin, scale=1.0 / 8, **rw)
            P.act(SH, LID, AF.Sin, scale=1.0 / 16, **rw)
            P.tt(C8, SH, SH, ALU.mult, **rw)
            P.ts(C8, C8, -2.0, 1.0, ALU.mult, ALU.add, **rw)
            for _ in range(3):
                P.tt(T1, C8, C8, ALU.mult, **rw)
                P.tt(T2, S8, S8, ALU.mult, **rw)
                P.tt(T3, C8, S8, ALU.mult, **rw)
                P.tt(C8, T1, T2, ALU.subtract, **rw)
                P.ts(S8, T3, 2.0, None, ALU.mult, **rw)
            P.ms(COS[:, :, 0], 1.0, **rw)
            P.ms(SIN[:, :, 0], 0.0, **rw)
            P.ms(COS[:, :, T + 1:9], 1.0, **rw)
            P.ms(SIN[:, :, T + 1:9], 0.0, **rw)
            for d in range(1, T + 1):
                cmul(COS[:, :, d], SIN[:, :, d], COS[:, :, d - 1], SIN[:, :, d - 1], C8, S8, T1, T2, T3, TQ)
            P.tt(PR, MAG, COS, ALU.mult, **rw)
            P.tt(PIm, MAG, SIN, ALU.mult, **rw)
            P.cp(MAGT[:], MAG[:, :, T], **rw)
            P.ms(EC[:, :, 0], 1.0, **rw)
            P.ms(ES[:, :, 0], 0.0, **rw)
            P.cp(EC[:, :, 1], COS[:, :, T], **rw)
            P.cp(ES[:, :, 1], SIN[:, :, T], **rw)
            m = 1
            while m < NC:
                n = min(m, NC - m)
                bR = EC[:, :, m:m + 1].to_broadcast([128, 16, n])
                bI = ES[:, :, m:m + 1].to_broadcast([128, 16, n])
                cmul(EC[:, :, m + 1:m + 1 + n], ES[:, :, m + 1:m + 1 + n], EC[:, :, 1:1 + n], ES[:, :, 1:1 + n], bR, bI,
                     ET1[:, :, 0:n], ET2[:, :, 0:n], ET3[:, :, 0:n], ET4[:, :, 0:n])
                m += n
            P.ts(T1, PR[:, :, 1], -1.0, None, ALU.add, **rw)
            P.tt(T2, LR, LR, ALU.mult, **rw)
            P.tt(T3, LI, LI, ALU.mult, **rw)
            P.tt(T2, T2, T3, ALU.add, **rw)
            P.op(V, lambda e: e.reciprocal(out=T2, in_=T2), **rw)
            P.tt(FR, T1, LR, ALU.mult, **rw)
            P.tt(T3, PIm[:, :, 1], LI, ALU.mult, **rw)
            P.tt(FR, FR, T3, ALU.add, **rw)
            P.tt(FR, FR, T2, ALU.mult, **rw)
            P.tt(FI, PIm[:, :, 1], LR, ALU.mult, **rw)
            P.tt(T3, T1, LI, ALU.mult, **rw)
            P.tt(FI, FI, T3, ALU.subtract, **rw)
            P.tt(FI, FI, T2, ALU.mult, **rw)
            frb = FR.unsqueeze(2).to_broadcast([128, 16, 16])
            fib = FI.unsqueeze(2).to_broadcast([128, 16, 16])
            P.tt(BBR, BR, frb, ALU.mult, **rw)
            P.tt(TB1, BI, fib, ALU.mult, **rw)
            P.tt(BBR, BBR, TB1, ALU.subtract, **rw)
            P.tt(BBI, BI, frb, ALU.mult, **rw)
            P.tt(TB1, BR, fib, ALU.mult, **rw)
            P.tt(BBI, BBI, TB1, ALU.add, **rw)
            P.ms(VBR, 0.0, **rw)
            P.ms(VBI, 0.0, **rw)
            VBR5 = VBR.rearrange("p q d (g h c) -> p q d g h c", g=4, h=2)
            VBI5 = VBI.rearrange("p q d (g h c) -> p q d g h c", g=4, h=2)
            for q in range(4):
                for h in range(2):
                    hs = slice(64 * h, 64 * h + 64)
                    prb = PR[hs, 4 * q:4 * q + 4, 0:T].rearrange("p g d -> p d g").unsqueeze(3).to_broadcast([64, T, 4, 16])
                    pib = PIm[hs, 4 * q:4 * q + 4, 0:T].rearrange("p g d -> p d g").unsqueeze(3).to_broadcast([64, T, 4, 16])
                    bbr = BBR[hs, 4 * q:4 * q + 4, :].unsqueeze(1).to_broadcast([64, T, 4, 16])
                    bbi = BBI[hs, 4 * q:4 * q + 4, :].unsqueeze(1).to_broadcast([64, T, 4, 16])
                    oR = VBR5[hs, q, :, :, h, :]
                    oI = VBI5[hs, q, :, :, h, :]
                    t1 = TV1[hs]
                    P.tt(oR, prb, bbr, ALU.mult, **rw)
                    P.tt(t1, pib, bbi, ALU.mult, **rw)
                    P.tt(oR, oR, t1, ALU.subtract, **rw)
                    P.tt(oI, prb, bbi, ALU.mult, **rw)
                    P.tt(t1, pib, bbr, ALU.mult, **rw)
                    P.tt(oI, oI, t1, ALU.add, **rw)
            for (CN, CIN, pc) in ((CN_R, CIN_R, 0), (CN_I, CIN_I, 2)):
                P.ts(CIN[:, :, 0:64], CN, par[:, pc:pc + 1], None, ALU.mult, **rw)
                P.ts(CIN[:, :, 64:128], CN, par[:, pc + 1:pc + 2], None, ALU.mult, **rw)
            for (CIN, CT) in ((CIN_R, CTR), (CIN_I, CTI)):
                for q in range(4):
                    P.mm(bank(4)[:, q * 128:(q + 1) * 128], CIN[:, q, :], idf[:], True, True,
                         reads=[rs, rconst], writes=[rb[4]], inc=(q == 3))
                P.cp(CT, bank(4).rearrange("p (q c) -> p q c", q=4), reads=[rb[4]], writes=[rb[4], rs])
            for (VB, BZ) in ((VBR, BZR), (VBI, BZI)):
                for q in range(4):
                    for dh in range(T // 4):
                        bk = 5 + (dh % 2)
                        for dd in range(4):
                            d = dh * 4 + dd
                            P.mm(bank(bk)[:, dd * 128:(dd + 1) * 128], VB[:, q, d, :], idf[:], True, True,
                                 reads=[rs, rconst], writes=[rb[bk]], inc=(dd == 3))
                        for dd in range(4):
                            d = dh * 4 + dd
                            P.cp(BZ[:, q, T - 1 - d, :], bank(bk)[:, dd * 128:(dd + 1) * 128],
                                 reads=[rb[bk]], writes=[rb[bk], rs])
            for q in range(4):
                for d in range(T):
                    bk = 6 + (d // 4)
                    for g4 in range(4):
                        col = (d % 4) * 128 + g4 * 32
                        o_ = bank(bk)[32 * g4:32 * g4 + 32, col:col + 32]
                        last = (g4 == 3 and d % 4 == 3)
                        P.mm(o_, VBR[:, q, d, 32 * g4:32 * g4 + 32], CTR[:, q, 32 * g4:32 * g4 + 32], True, False,
                             reads=[rs], writes=[rb[bk]], tp=(0, 32 * g4))
                        P.mm(o_, VBI[:, q, d, 32 * g4:32 * g4 + 32], CTI[:, q, 32 * g4:32 * g4 + 32], False, True,
                             reads=[rs], writes=[rb[bk]], tp=(0, 32 * g4), inc=last)
                for dhh in range(T // 4):
                    bk = 6 + dhh
                    src = bank(bk).rearrange("p (d g c) -> p d g c", d=4, g=4)
                    for g4 in range(4):
                        ps_ = slice(32 * g4, 32 * g4 + 32)
                        if dhh == 0:
                            P.stt(KDS[ps_, q, 0, :], i32[ps_, :], dcol[ps_, q:q + 1], src[ps_, 0, g4, :],
                                  ALU.mult, ALU.add, reads=[rb[bk], rconst], writes=[rb[bk], rs])
                            P.cp(KDS[ps_, q, 1:4, :], src[ps_, 1:4, g4, :], reads=[rb[bk]], writes=[rb[bk], rs])
                        else:
                            P.cp(KDS[ps_, q, 4:8, :], src[ps_, :, g4, :], reads=[rb[bk]], writes=[rb[bk], rs])
            for q in range(4):
                ctr = CTR[:, q, :].rearrange("p (g c) -> p g c", g=4).unsqueeze(2).to_broadcast([128, 4, T, 32])
                cti = CTI[:, q, :].rearrange("p (g c) -> p g c", g=4).unsqueeze(2).to_broadcast([128, 4, T, 32])
                prb = PR[:, 4 * q:4 * q + 4, 1:T + 1].unsqueeze(3).to_broadcast([128, 4, T, 32])
                pib = PIm[:, 4 * q:4 * q + 4, 1:T + 1].unsqueeze(3).to_broadcast([128, 4, T, 32])
                P.tt(TC1, ctr, prb, ALU.mult, **rw)
                P.tt(TC2, cti, pib, ALU.mult, **rw)
                P.tt(CZR[:, 4 * q:4 * q + 4, :, :], TC1, TC2, ALU.add, **rw)
                P.tt(TC1, cti, prb, ALU.mult, **rw)
                P.tt(TC2, ctr, pib, ALU.mult, **rw)
                P.tt(CZI[:, 4 * q:4 * q + 4, :, :], TC1, TC2, ALU.subtract, **rw)
            P.ms(carR[:], 0.0, **rw)
            P.ms(carI[:], 0.0, **rw)
            return rs

        def setup_conv():
            rs = res("convsetup")
            P.ms(wcol[:], 0.0, reads=[], writes=[rs])
            for s in range(4):
                for r in range(8):
                    if 4 * r + s > 30:
                        continue
                    P.dma("sync", wcol[32 * s:32 * s + 32, :, r],
                          cw[0, 4 * r + s, :].rearrange("(g c) -> c g", c=32), ds_cv, reads=[], writes=[rs])
            return rs

        def setup_conv_late(rs):
            P.tt(wdiag[:], wcol[:].unsqueeze(3).to_broadcast([128, 16, 8, 32]),
                 i32[:].unsqueeze(1).unsqueeze(1).to_broadcast([128, 16, 8, 32]), ALU.mult,
                 reads=[rs, rconst], writes=[rs])

        stat2 = sb("stat2", [128, 2 * NB], F32)
        rstat2 = res("stat2")

        def norm_to_hT(nidx, xs=None, rxl=None, bank0=6, stt_=None, rst_=None):
            xs = x_sb if xs is None else xs
            rxl = rx if rxl is None else rxl
            stt_ = stat if stt_ is None else stt_
            rst_ = rstat if rst_ is None else rst_
            for b in range(NB):
                s = b % 2
                P.act(sgf[:, s, :].bitcast(BF16), xs[:, b, :], AF.Square, accum_out=stt_[:, b:b + 1],
                      reads=[rxl[b]], writes=[rsgf[s], rst_])
            rs_all = stt_[:, NB:2 * NB]
            P.ts(rs_all, stt_[:, 0:NB], 1.0 / D, EPS, ALU.mult, ALU.add, reads=[rst_], writes=[rst_])
            P.act(rs_all, rs_all, AF.Sqrt, reads=[rst_], writes=[rst_])
            P.op("vector", lambda e: e.reciprocal(out=rs_all, in_=rs_all), reads=[rst_], writes=[rst_])
            for b in range(NB):
                s = b % 2
                rstd = stt_[:, NB + b:NB + b + 1]
                P.ts(hn[:, s, :], xs[:, b, :], rstd, None, ALU.mult, reads=[rxl[b], rst_], writes=[rhn[s]])
                bk = bank0 + s
                pt = bank(bk).bitcast(BF16)
                for k in range(KD):
                    P.tr(pt[:, k * 128:(k + 1) * 128], hn[:, s, k * 128:(k + 1) * 128], idb[:],
                         reads=[rhn[s], rconst], writes=[rb[bk]], inc=(k == KD - 1))
                tt_ = b // 4
                P.tt(hT[:, :, b * 128:(b + 1) * 128], pt.rearrange("p (k t) -> p k t", k=KD),
                     gcol[:, nidx, :].unsqueeze(2).to_broadcast([128, KD, 128]), ALU.mult,
                     reads=[rb[bk], rconst], writes=[rb[bk], rhT[tt_]])

        scr_gu = [nc.dram_tensor("scr_gu%d" % i, [NF // 2, 128, 2 * KD * 256], BF16).ap() for i in range(2)]
        scr_wd = [nc.dram_tensor("scr_wd%d" % i, [128, NF * D], BF16).ap() for i in range(2)]
        rscr_gu = [[res("scrgu%d_%d" % (i, fp)) for fp in range(NF // 2)] for i in range(2)]
        rscr_wd = [res("scrwd%d" % i) for i in range(2)]
        ds_scr = P.dsem("ds_scr")

        ds_cvt = P.dsem("ds_cvt")

        def convert_ffn(fi_, wg, wu, wdn):
            wgv = wg[0].rearrange("(k p) n -> p k n", p=128)
            wuv = wu[0].rearrange("(k p) n -> p k n", p=128)
            wdv = wdn[0].rearrange("(f p) n -> p f n", p=128)
            for fp in range(NF // 2):
                dst = scr_gu[fi_][fp].rearrange("p (g k n) -> p g k n", g=2, k=KD)
                P.dma("gpsimd", dst[:, 0], wgv[:, :, fp * 256:(fp + 1) * 256], ds_cvt, writes=[rscr_gu[fi_][fp]])
                P.dma("gpsimd", dst[:, 1], wuv[:, :, fp * 256:(fp + 1) * 256], ds_cvt, writes=[rscr_gu[fi_][fp]])
            dstw = scr_wd[fi_].rearrange("p (f n) -> p f n", f=NF)
            P.dma("gpsimd", dstw[:, 0:11, :], wdv[:, 0:11, :], ds_cvt, writes=[rscr_wd[fi_]])
            P.dma("gpsimd", dstw[:, 11:22, :], wdv[:, 11:22, :], ds_cvt, writes=[rscr_wd[fi_]])

        def ffn(wg, wu, wdn, first_wd_dep, fi_, tile_i, mid_hook=None, hid_dep=(), interleave=None, wd_late=False):
            hid3 = hid[:].rearrange("p (f t) -> p f t", f=NF)
            wd3 = wd_sb[:].rearrange("p (f n) -> p f n", f=NF)
            wgv = wg[0].rearrange("(k p) n -> p k n", p=128)
            wuv = wu[0].rearrange("(k p) n -> p k n", p=128)
            wdv = wdn[0].rearrange("(f p) n -> p f n", p=128)

            NP_ = NF // 2

            def load_p(fp):
                s = fp % NSLOT
                flat = wgu[:, s].rearrange("p g k n -> p (g k n)")
                if tile_i == 0 and fi_ == 0:
                    P.dma("gpsimd", wgu[:, s, 0, :, :], wgv[:, :, fp * 256:(fp + 1) * 256], ds_slot[s], writes=[rslot[s]])
                    P.dma("gpsimd", wgu[:, s, 1, :, :], wuv[:, :, fp * 256:(fp + 1) * 256], ds_slot[s], writes=[rslot[s]])
                    P.dma("sync", scr_gu[fi_][fp], flat, ds_scr, reads=[rslot[s]], writes=[rscr_gu[fi_][fp]])
                else:
                    P.dma("gpsimd", flat, scr_gu[fi_][fp], ds_slot[s], reads=[rscr_gu[fi_][fp]], writes=[rslot[s]])

            for fp in range(min(NSLOT, NP_)):
                load_p(fp)
            def load_wd():
                if tile_i == 0 and fi_ == 0:
                    P.dma("gpsimd", wd3[:, 0:11, :], wdv[:, 0:11, :], ds_wd, writes=[rwd, rwdt] + list(first_wd_dep))
                    P.dma("gpsimd", wd3[:, 11:22, :], wdv[:, 11:22, :], ds_wd, writes=[rwd, rwdt])
                    P.dma("sync", scr_wd[fi_], wd_sb[:], ds_scr, reads=[rwd, rwdt], writes=[rscr_wd[fi_]])
                else:
                    P.dma("gpsimd", wd_sb[:], scr_wd[fi_], ds_wd, reads=[rscr_wd[fi_]],
                          writes=[rwd, rwdt] + list(first_wd_dep))

            if not wd_late:
                load_wd()
            it = 0
            for fp in range(NP_):
                s = fp % NSLOT
                for fi in range(2):
                    f = 2 * fp + fi
                    for t in range(NTT):
                        pa = (it % 2) * 2
                        it += 1
                        tsl = slice(t * 512, (t + 1) * 512)
                        for gu in range(2):
                            for k in range(KD):
                                P.mm(bank(pa + gu), wgu[:, s, gu, k, fi * 128:(fi + 1) * 128], hT[:, k, tsl], k == 0, k == KD - 1,
                                     reads=[rslot[s], rhT[t]], writes=[rb[pa + gu]], inc=(k == KD - 1))
                        ss = (it - 1) % 2
                        P.act(sg[:, ss, :], bank(pa), AF.Silu, reads=[rb[pa]], writes=[rb[pa], rsg[ss]])
                        P.tt(hid3[:, f, tsl], bank(pa + 1), sg[:, ss, :], ALU.mult,
                             reads=[rb[pa + 1], rsg[ss]], writes=[rb[pa + 1], rhid[t], rA] + list(hid_dep))
                    if interleave is not None:
                        interleave()
                if fp + NSLOT < NP_:
                    load_p(fp + NSLOT)
            if mid_hook is not None:
                mid_hook()
            if wd_late:
                load_wd()
            for b in range(NB):
                t = b // 4
                pa = 4 + (b % 2) * 2
                for dh in range(2):
                    for f in range(NF):
                        P.mm(bank(pa + dh), hid3[:, f, b * 128:(b + 1) * 128], wd3[:, f, dh * 512:(dh + 1) * 512],
                             f == 0, f == NF - 1, reads=[rhid[t], rwd, rwdt], writes=[rb[pa + dh]], inc=(f == NF - 1))
                P.stt(x_sb[:, b, :], pbig[pa // 2][:, :], 0.5, x_sb[:, b, :], ALU.mult, ALU.add,
                      reads=[rb[pa], rb[pa + 1], rx[b]], writes=[rb[pa], rb[pa + 1], rx[b]])

        WOUT = wd_sb[:, 0:KD * D].rearrange("p (k n) -> p k n", k=KD)
        winv = w_in[0].rearrange("(k p) n -> p k n", p=128)
        oA = 0
        US5 = hid[:, oA:oA + 4 * TT].rearrange("p (q t) -> p q t", q=4); oA += 4 * TT
        ZB = hid[:, oA:oA + 4 * ZW].rearrange("p (q t) -> p q t", q=4); oA += 4 * ZW
        YG = hid[:, oA:oA + 4 * TT].rearrange("p (q t) -> p q t", q=4); oA += 4 * TT
        YCAT = hid[:, oA:oA + 8 * TT].rearrange("p (q t) -> p q t", q=8); oA += 8 * TT
        assert oA <= NF * TT, oA
        ZOFF = KD * D
        ZREP = wd_sb[:, ZOFF:ZOFF + 16 * RW].rearrange("p (g t) -> p g t", g=16)
        assert ZOFF + 16 * RW <= NF * D
        hTF = hT[:].rearrange("p k t -> p (k t)").bitcast(F32)
        oZ = 0
        WRe, oZ = carveF(hTF, oZ, [8, NC]); WIm, oZ = carveF(hTF, oZ, [8, NC])
        assert oZ <= KD * TT // 2, oZ
        ROFF = ZOFF + 16 * RW
        tailF = wd_sb[:, ROFF:NF * D].bitcast(F32)
        oZ = 0
        RRe, oZ = carveF(tailF, oZ, [8, NC + 1]); RIm, oZ = carveF(tailF, oZ, [8, NC + 1])
        assert oZ <= (NF * D - ROFF) // 2, oZ
        smallT = sb("smallT", [128, 4, 16], F32)
        SRb = sb("SRb", [128, 16, NC + 1], BF16)
        SIb = sb("SIb", [128, 16, NC + 1], BF16)
        zhist = sb("zhist", [128, 4, 32], BF16)
        rzh = res("zhist")
        czf = hn[:].rearrange("p s d -> p (s d)").bitcast(F32).rearrange("p (s d) -> p s d", s=2)
        rZB = res("zbuf"); rZBh = [res("zbuf_h0"), res("zbuf_h1")]; rZREP = res("zrep")
        rZREPh = [res("zrep_h0"), res("zrep_h1")]; ds_reph = [P.dsem("ds_rep0"), P.dsem("ds_rep1")]; rUS5 = res("us5"); rYG = res("yg"); rYC = [res("ycat%d" % t) for t in range(NTT)]
        rW = res("wrot"); rRR = res("rr"); rS = res("sfull"); rTM = res("tm"); rczf = rhn
        rcar = res("carry")

        def mixer(st_i, rs5, rcv):
            for fp in range(6):
                P.dma("gpsimd", wgu[:, fp % 3, fp // 3, :, :], winv[:, :, fp * 256:(fp + 1) * 256], ds_slot[fp % 3],
                      writes=[rslot[fp % 3]])
            P.dma("gpsimd", WOUT, w_out[0].rearrange("(k p) n -> p k n", p=128), ds_wd, writes=[rwd, rB])
            if stage == "full" and st_i == 0:
                convert_ffn(1, w2g, w2u, w2d)
            if st_i == 0:
                P.ms(zhist[:], 0.0, writes=[rzh])
            P.cp(ZB[:, :, 0:32], zhist[:], reads=[rzh], writes=[rZB, rZBh[0], rZBh[1], rA])
            P.ms(ZB[:, :, 32 + TT:ZW], 0.0, writes=[rZB, rZBh[0], rZBh[1], rA])
            it = 0
            for q in range(4):
                for t in range(NTT):
                    b1 = (it % 2) * 2
                    it += 1
                    tsl = slice(t * 512, (t + 1) * 512)
                    for hh, fp in enumerate((2 + q // 2, 4 + q // 2)):
                        sl, hf, fi = fp % 3, fp // 3, q % 2
                        for k in range(KD):
                            P.mm(bank(b1 + hh), wgu[:, sl, hf, k, fi * 128:(fi + 1) * 128], hT[:, k, tsl], k == 0, k == KD - 1,
                                 reads=[rslot[sl], rhT[t]], writes=[rb[b1 + hh]], inc=(k == KD - 1))
                    ss = it % 2
                    P.act(sg[:, ss, :], bank(b1 + 1), AF.Sigmoid, reads=[rb[b1 + 1]], writes=[rb[b1 + 1], rsg[ss]])
                    P.tt(ZB[:, q, 32 + t * 512:32 + (t + 1) * 512], bank(b1), sg[:, ss, :], ALU.mult,
                         reads=[rb[b1], rsg[ss]], writes=[rb[b1], rZBh[q // 2], rA])
                if q % 2 == 1:
                    hq = q // 2
                    for s in range(4):
                        for g4 in range(4):
                            g0 = 8 * hq + g4
                            P.dma("sync", ZREP[32 * s:32 * s + 32, g0:g0 + 5:4, :],
                                  ZB[32 * g4:32 * g4 + 32, 2 * hq:2 * hq + 2, s:s + RW], ds_reph[hq],
                                  reads=[rZBh[hq]], writes=[rZREPh[hq], rwdt])
            rUS5h = [res("us5_h0"), res("us5_h1")]

            def win_s5(cq):
                fp, fi = cq // 2, cq % 2
                sl, hf = fp % 3, fp // 3
                t = 0
                bk = 4 + cq
                tsl = slice(t * 512, (t + 1) * 512)
                for k in range(KD):
                    P.mm(bank(bk), wgu[:, sl, hf, k, fi * 128:(fi + 1) * 128], hT[:, k, tsl], k == 0, k == KD - 1,
                         reads=[rslot[sl], rhT[t]], writes=[rb[bk]], inc=(k == KD - 1))
                P.act(US5[:, cq, tsl], bank(bk), AF.Copy, reads=[rb[bk]], writes=[rb[bk], rUS5h[cq // 2], rUS5, rA])

            def z_half(h):
                for part, BZ in enumerate((BZR, BZI)):
                    for ql in range(2):
                        q = 2 * h + ql
                        c0 = part * 2 * NC + ql * NC
                        for i in range(T):
                            for g4 in range(4):
                                bk = 4 * h + g4
                                P.mm(bank(bk)[:, c0:c0 + NC], BZ[32 * g4:32 * g4 + 32, q, i, :],
                                     US5[32 * g4:32 * g4 + 32, q, i::T], i == 0, i == T - 1,
                                     reads=[rUS5h[h], rs5], writes=[rb[bk]], tp=(32 * g4, 0),
                                     inc=(i == T - 1 and part == 1 and ql == 1))

            win_s5(0)
            win_s5(1)
            z_half(0)
            win_s5(2)
            win_s5(3)
            z_half(1)
            rSh = [res("sfull_h0"), res("sfull_h1")]

            def s5_half(h):
                ECv = EC[:, 8 * h:8 * h + 8, :].rearrange("p (q g) k -> p g q k", g=4)
                ESv = ES[:, 8 * h:8 * h + 8, :].rearrange("p (q g) k -> p g q k", g=4)
                WRv = WRe.rearrange("p (q g) k -> p g q k", g=4)
                WIv = WIm.rearrange("p (q g) k -> p g q k", g=4)
                RRv = RRe.rearrange("p (q g) k -> p g q k", g=4)
                RIv = RIm.rearrange("p (q g) k -> p g q k", g=4)
                for a in range(2):
                    b0 = 4 * h + 2 * a
                    zz = pbig[b0 // 2][:, :].rearrange("p (b x) -> p b x", b=2)
                    zr = zz[:, :, 0:2 * NC].rearrange("p b (q c) -> p b q c", q=2)
                    zi = zz[:, :, 2 * NC:4 * NC].rearrange("p b (q c) -> p b q c", q=2)
                    ec = ECv[:, 2 * a:2 * a + 2, :, 1:NC + 1]
                    es = ESv[:, 2 * a:2 * a + 2, :, 1:NC + 1]
                    wr = WRv[:, 2 * a:2 * a + 2, :, :]
                    wi = WIv[:, 2 * a:2 * a + 2, :, :]
                    t1 = RRv[:, 2 * a:2 * a + 2, :, 1:NC + 1]
                    t2 = RIv[:, 2 * a:2 * a + 2, :, 1:NC + 1]
                    dep = dict(reads=[rb[b0], rb[b0 + 1], rs5, rS], writes=[rb[b0], rb[b0 + 1], rW, rRR, rhT[0]])
                    P.tt(wr, zr, ec, ALU.mult, **dep)
                    P.tt(t1, zi, es, ALU.mult, **dep)
                    P.tt(wr, wr, t1, ALU.add, **dep)
                    P.tt(wi, zi, ec, ALU.mult, **dep)
                    P.tt(t2, zr, es, ALU.mult, **dep)
                    P.tt(wi, wi, t2, ALU.subtract, **dep)
                gs = slice(8 * h, 8 * h + 8)
                P.cp(RRe[:, :, 0], carR[:, gs], reads=[rcar, rW], writes=[rRR])
                P.cp(RIm[:, :, 0], carI[:, gs], reads=[rcar, rW], writes=[rRR])
                for gl in range(8):
                    gp = 8 * h + gl
                    mg = MAGT[:, gp:gp + 1].to_broadcast([128, NC])
                    for (RX, WX, CAR) in ((RRe, WRe, carR), (RIm, WIm, carI)):
                        P.op("vector", lambda e, RX=RX, WX=WX, CAR=CAR, gp=gp, gl=gl, mg=mg: e.tensor_tensor_scan(
                            out=RX[:, gl, 1:NC + 1], data0=mg, data1=WX[:, gl, :], initial=CAR[:, gp:gp + 1],
                            op0=ALU.mult, op1=ALU.add), reads=[rW, rRR, rcar, rs5, rhT[0]], writes=[rRR])
                depc = dict(reads=[rRR, rs5], writes=[rTM])
                P.tt(smallT[:, 0, gs], EC[:, gs, NC], RRe[:, :, NC], ALU.mult, **depc)
                P.tt(smallT[:, 1, gs], ES[:, gs, NC], RIm[:, :, NC], ALU.mult, **depc)
                P.tt(smallT[:, 2, gs], ES[:, gs, NC], RRe[:, :, NC], ALU.mult, **depc)
                P.tt(smallT[:, 3, gs], EC[:, gs, NC], RIm[:, :, NC], ALU.mult, **depc)
                P.tt(carR[:, gs], smallT[:, 0, gs], smallT[:, 1, gs], ALU.subtract, reads=[rTM], writes=[rcar])
                P.tt(carI[:, gs], smallT[:, 2, gs], smallT[:, 3, gs], ALU.add, reads=[rTM], writes=[rcar])
                dep = dict(reads=[rRR, rs5, rW, rhT[0]], writes=[rSh[h], rS, rW])
                TM1 = WRe
                TM2 = WIm
                P.tt(TM1, EC[:, gs, 0:NC], RRe[:, :, 0:NC], ALU.mult, **dep)
                P.tt(TM2, ES[:, gs, 0:NC], RIm[:, :, 0:NC], ALU.mult, **dep)
                P.tt(SRb[:, gs, 0:NC], TM1, TM2, ALU.subtract, **dep)
                P.tt(TM1, ES[:, gs, 0:NC], RRe[:, :, 0:NC], ALU.mult, **dep)
                P.tt(TM2, EC[:, gs, 0:NC], RIm[:, :, 0:NC], ALU.mult, **dep)
                P.tt(SIb[:, gs, 0:NC], TM1, TM2, ALU.add, **dep)

            def conv_a(q):
                t = 0
                bk = q % 2
                cs = q % 2
                bm, bv = 2, 3
                for r in range(8):
                    for g4 in range(4):
                        grp = 4 * q + g4
                        c0 = 2 + 4 * r + t * 512
                        P.mm(bank(bk)[32 * g4:32 * g4 + 32, :], wdiag[:, grp, r, :], ZREP[:, grp, c0:c0 + 512],
                             r == 0, r == 7, reads=[rZREPh[q // 2], rcv], writes=[rb[bk]], tp=(0, 32 * g4),
                             inc=(r == 7 and g4 == 3))
                tsl = slice(t * 512, (t + 1) * 512)
                P.act(czf[:, cs, :], bank(bk), AF.Identity, bias=cvec[:, 0, q:q + 1],
                      reads=[rb[bk], rconst], writes=[rb[bk], rczf[cs]])
                P.act(sgf[:, cs, :], czf[:, cs, :], AF.Square, reads=[rczf[cs]], writes=[rsgf[cs]])
                P.mm(bank(bm), m64[:], czf[:, cs, :], True, True, reads=[rczf[cs], rconst], writes=[rb[bm]], inc=True)
                P.mm(bank(bv), m64[:], sgf[:, cs, :], True, True, reads=[rsgf[cs], rconst], writes=[rb[bv]], inc=True)
                P.tt(czf[:, cs, :], czf[:, cs, :], bank(bm), ALU.subtract, reads=[rb[bm], rczf[cs]],
                     writes=[rb[bm], rczf[cs]])
                P.act(sgf[:, cs, :], bank(bm), AF.Square, reads=[rb[bm]], writes=[rb[bm], rsgf[cs]])
                P.stt(sgf[:, cs, :], bank(bv), EPS, sgf[:, cs, :], ALU.add, ALU.subtract, reads=[rb[bv], rsgf[cs]],
                      writes=[rb[bv], rsgf[cs]])

            def conv_sqrt(q):
                cs = q % 2
                P.act(sgf[:, cs, :], sgf[:, cs, :], AF.Sqrt, reads=[rsgf[cs]], writes=[rsgf[cs]])

            def conv_norm(q):
                cs = q % 2
                P.op("vector", lambda e, cs=cs: e.reciprocal(out=sgf[:, cs, :], in_=sgf[:, cs, :]),
                     reads=[rsgf[cs]], writes=[rsgf[cs]])
                P.tt(czf[:, cs, :], czf[:, cs, :], sgf[:, cs, :], ALU.mult, reads=[rczf[cs], rsgf[cs]],
                     writes=[rczf[cs]])

            def conv_silu(q):
                cs = q % 2
                t = 0
                tsl = slice(t * 512, (t + 1) * 512)
                P.act(YCAT[:, 4 + q, tsl], czf[:, cs, :], AF.Silu, scale=cvec[:, 1, q:q + 1], bias=cvec[:, 2, q:q + 1],
                      reads=[rczf[cs], rconst], writes=[rYC[t], rA])

            def conv_pair(q0):
                conv_a(q0)
                conv_a(q0 + 1)
                conv_sqrt(q0)
                conv_sqrt(q0 + 1)
                conv_norm(q0)
                conv_norm(q0 + 1)
                conv_silu(q0)
                conv_silu(q0 + 1)

            def y_q(q):
                rSq = rSh[q // 2]
                for g4 in range(4):
                    gp = 4 * q + g4
                    bk = g4
                    o_ = bank(bk)[:, q * NC:(q + 1) * NC]
                    P.mm(o_, CZR[:, gp].rearrange("p j c -> p (j c)"), SRb[:, gp, 0:NC], True, False,
                         reads=[rSq, rs5], writes=[rb[bk]])
                    P.mm(o_, CZI[:, gp].rearrange("p j c -> p (j c)"), SIb[:, gp, 0:NC], False, False,
                         reads=[rSq, rs5], writes=[rb[bk]])
                for j in range(T):
                    for i in range(j + 1):
                        for g4 in range(4):
                            bk = g4
                            o_ = bank(bk)[32 * j:32 * j + 32, q * NC:(q + 1) * NC]
                            P.mm(o_, KDS[32 * g4:32 * g4 + 32, q, j - i, :], US5[32 * g4:32 * g4 + 32, q, i::T],
                                 False, (i == j), reads=[rUS5h[q // 2], rs5], writes=[rb[bk]], tp=(32 * g4, 32 * j),
                                 inc=(i == j and j == T - 1))

            def y_evac(h):
                for g4 in range(4):
                    for j in range(T):
                        src = bank(g4)[32 * j:32 * j + 32, 2 * h * NC:(2 * h + 2) * NC].rearrange("p (q c) -> p q c", q=2)
                        dst = YG[32 * g4:32 * g4 + 32, 2 * h:2 * h + 2, j::T]
                        P.act(dst, src, AF.Gelu_apprx_tanh, reads=[rb[g4]], writes=[rb[g4], rYG, rA])

            s5_half(0)
            conv_pair(0)
            s5_half(1)
            y_q(0)
            y_q(1)
            y_evac(0)
            conv_pair(2)
            y_q(2)
            y_q(3)
            y_evac(1)
            P.cp(zhist[:], ZB[:, :, TT:TT + 32], reads=[rZB, rZBh[0], rZBh[1]], writes=[rzh])
            it = 0
            for cq in range(4):
                for t in range(NTT):
                    bk = it % 2
                    ss = it % 2
                    it += 1
                    tsl = slice(t * 512, (t + 1) * 512)
                    for k in range(4):
                        P.mm(bank(bk), wglu_sb[:, k, cq * 128:(cq + 1) * 128], YG[:, k, tsl], k == 0, k == 3,
                             reads=[rYG, rconst, rwglu], writes=[rb[bk]], inc=(k == 3))
                    P.act(sg[:, ss, :], bank(bk), AF.Sigmoid, bias=bglu[:, cq:cq + 1],
                          reads=[rb[bk], rconst], writes=[rb[bk], rsg[ss]])
                    P.tt(YCAT[:, cq, tsl], YG[:, cq, tsl], sg[:, ss, :], ALU.mult, reads=[rYG, rsg[ss]],
                         writes=[rYC[t], rA])
            for b in range(NB):
                t = b // 4
                pa = 4 + (b % 2) * 2
                for dh in range(2):
                    for k in range(8):
                        P.mm(bank(pa + dh), YCAT[:, k, b * 128:(b + 1) * 128], WOUT[:, k, dh * 512:(dh + 1) * 512],
                             k == 0, k == 7, reads=[rYC[t], rwd], writes=[rb[pa + dh]], inc=(k == 7))
                P.tt(x_sb[:, b, :], pbig[pa // 2][:, :], x_sb[:, b, :], ALU.add,
                     reads=[rb[pa], rb[pa + 1], rx[b]], writes=[rb[pa], rb[pa + 1], rx[b]])

        defer = Deferred()
        rs5 = setup_s5(defer) if stage != "ffn1" else None
        defer.replay(P, only_dma=True)
        rcv = setup_conv() if stage != "ffn1" else None
        for st_i in range(NST):
            t0 = st_i * TT
            x_sb = xbuf[:, st_i % 2]
            rx = rxs[st_i % 2]
            late_x = (st_i == 0 and rs5 is not None)
            if st_i + 1 < NST and not late_x:
                load_x(st_i + 1)
            if st_i == 0 or stage != "full":
                norm_to_hT(0)
            if st_i == 0 and rs5 is not None:
                nsl = (len(defer.ops) - defer.pos) // NF + 1
                ffn(w1g, w1u, w1d, [rA, rB, rs5], 0, st_i, hid_dep=[rA],
                    interleave=lambda: defer.replay(P, n=nsl),
                    mid_hook=lambda: defer.replay(P), wd_late=True)
                load_x(1)
            else:
                ffn(w1g, w1u, w1d, [rA, rB], 0, st_i, hid_dep=[rA, rB])
            if stage != "ffn1":
                if st_i == 0:
                    setup_conv_late(rcv)
                norm_to_hT(1)
                mixer(st_i, rs5, rcv)
            if stage == "full":
                norm_to_hT(2)
                hook = None
                if st_i + 1 < NST:
                    nxt = (st_i + 1) % 2
                    hook = (lambda nxt=nxt: norm_to_hT(0, xbuf[:, nxt], rxs[nxt], bank0=0, stt_=stat2, rst_=rstat2))
                ffn(w2g, w2u, w2d, [rA, rB], 1, st_i, mid_hook=hook, hid_dep=[rA, rB])
                for b in range(NB):
                    s = b % 2
                    P.act(sgf[:, s, :].bitcast(BF16), x_sb[:, b, :], AF.Square, accum_out=stat[:, 2 * NB + b:2 * NB + b + 1],
                          reads=[rx[b]], writes=[rsgf[s], rstat])
                rf_all = stat[:, 3 * NB:4 * NB]
                P.ts(rf_all, stat[:, 2 * NB:3 * NB], 1.0 / D, EPS, ALU.mult, ALU.add, reads=[rstat], writes=[rstat])
                P.act(rf_all, rf_all, AF.Sqrt, reads=[rstat], writes=[rstat])
                P.op("vector", lambda e: e.reciprocal(out=rf_all, in_=rf_all), reads=[rstat], writes=[rstat])
                for b in range(NB):
                    rstd = stat[:, 3 * NB + b:3 * NB + b + 1]
                    P.stt(x_sb[:, b, :], x_sb[:, b, :], rstd, gfb[:], ALU.mult, ALU.mult,
                          reads=[rx[b], rstat, rconst], writes=[rx[b]])
            for b in range(NB):
                P.dma("sync", y[t0 + b * 128:t0 + (b + 1) * 128, :], x_sb[:, b, :], ds_y, reads=[rx[b]])
        fin = (ds_y.sem, ds_y.count, "dma", ds_y)
        P._wait("sync", fin)

        with nc.Block() as block:
            @block.sync
            def _(e):
                for f in P.q["sync"]:
                    f(e)

            @block.scalar
            def _(e):
                for f in P.q["scalar"]:
                    f(e)

            @block.vector
            def _(e):
                for f in P.q["vector"]:
                    f(e)

            @block.gpsimd
            def _(e):
                for f in P.q["gpsimd"]:
                    f(e)

            @block.tensor
            def _(e):
                for f in P.q["tensor"]:
                    f(e)
    return nc


def make_consts():
    c = {}
    c["c_idb"] = np.eye(128, dtype=np.float32).astype(ml_dtypes.bfloat16)
    c["c_idf"] = np.eye(128, dtype=np.float32)
    p = np.arange(128)
    c["c_i32"] = (p[:, None] % 32 == np.arange(32)[None, :]).astype(np.float32)
    par = ((p // 16) % 2)
    c["c_par"] = np.stack([(par == 0), (par == 1), -1.0 * (par == 0), -1.0 * (par == 1)], 1).astype(np.float32)
    c["c_ramp"] = np.broadcast_to(np.arange(144, dtype=np.float32)[None, :], (128, 144)).copy()
    c["c_m64"] = ((p[:, None] // 64) == (p[None, :] // 64)).astype(np.float32) / 64.0
    return c


_NC_CACHE = {}


def kernel(**inputs):
    stage = "full"
    if stage not in _NC_CACHE:
        _NC_CACHE[stage] = build(stage)
    nc = _NC_CACHE[stage]
    consts = make_consts()
    shared = {k: np.ascontiguousarray(np.asarray(v)) for k, v in inputs.items() if k != "x"}
    x = np.asarray(inputs["x"])
    in_maps = []
    for b in range(8):
        m = dict(shared)
        m.update(consts)
        m["x"] = np.ascontiguousarray(x[b])
        in_maps.append(m)
    res = run_bass_kernel_spmd(nc, in_maps, core_ids=list(range(8)))
    return np.stack([r["y"] for r in res.results], 0).astype(np.float32)
```

```python
import math
from contextlib import ExitStack
import numpy as np
import ml_dtypes
import concourse.bass as bass
import concourse.mybir as mybir
from concourse.bass_utils import run_bass_kernel_spmd

F32 = mybir.dt.float32
BF16 = mybir.dt.bfloat16
AF = mybir.ActivationFunctionType
ALU = mybir.AluOpType

D = 1024
FF = 2816
NF = 22
KD = 8
L = 4096
TT = 512
NB = TT // 128
NTT = TT // 512
NST = L // TT
T = 4
NC = TT // T
EPS = 1e-6
NSLOT = 3
PI = math.pi
ZW = 32 + TT + 4
RW = TT + 32


class Res:
    __slots__ = ("name", "w", "rd")

    def __init__(self, name):
        self.name = name
        self.w = None
        self.rd = []


class DSem:
    def __init__(self, sem):
        self.sem = sem
        self.count = 0


class Prog:
    ENG = ("scalar", "vector", "gpsimd", "tensor", "sync")

    def __init__(self, nc, stack):
        self.nc = nc
        self.stack = stack
        self.q = {e: [] for e in self.ENG}
        self.esem = {e: stack.enter_context(nc.semaphore("pc_" + e)) for e in self.ENG}
        self.ecnt = {e: 0 for e in self.ENG}
        self.waited = {}
        self.pend_r = {e: [] for e in self.ENG}
        self.pend_w = {e: [] for e in self.ENG}

    def dsem(self, name):
        return DSem(self.stack.enter_context(self.nc.semaphore(name)))

    def _wait(self, eng, ev):
        if ev is None:
            return
        sem, val, src = ev[0], ev[1], ev[2]
        if src == "dma":
            val = max(val, ev[3].count)
        key = (eng, sem.num)
        if self.waited.get(key, 0) >= val:
            return
        self.waited[key] = val
        self.q[eng].append(lambda e, sem=sem, val=val: e.wait_ge(sem, val))

    def _deps(self, eng, reads, writes):
        for r in reads:
            self._wait(eng, r.w)
        for w in writes:
            self._wait(eng, w.w)
            for ev in w.rd:
                if ev is not None and ev[2] == eng:
                    continue
                self._wait(eng, ev)

    def op(self, eng, fn, reads=(), writes=(), inc=True):
        self._deps(eng, reads, writes)
        if inc:
            self.ecnt[eng] += 1
            val = self.ecnt[eng]
            sem = self.esem[eng]
            self.q[eng].append(lambda e, fn=fn, sem=sem: fn(e).then_inc(sem, 1))
            ev = (sem, val, eng)
            for r in self.pend_r[eng]:
                r.rd.append(ev)
            for w in self.pend_w[eng]:
                w.w = ev
                w.rd = []
            self.pend_r[eng] = []
            self.pend_w[eng] = []
            for r in reads:
                r.rd.append(ev)
            for w in writes:
                w.w = ev
                w.rd = []
        else:
            self.q[eng].append(lambda e, fn=fn: fn(e))
            self.pend_r[eng].extend(reads)
            self.pend_w[eng].extend(writes)

    def dma(self, eng, out, in_, ds, reads=(), writes=()):
        self._deps(eng, reads, writes)
        ds.count += 16
        val = ds.count
        sem = ds.sem
        self.q[eng].append(lambda e, out=out, in_=in_, sem=sem: e.dma_start(out=out, in_=in_).then_inc(sem, 16))
        ev = (sem, val, "dma", ds)
        for r in reads:
            r.rd.append(ev)
        for w in writes:
            w.w = ev
            w.rd = []
        return ev

    def mm(self, out, lhsT, rhs, start=True, stop=True, reads=(), writes=(), inc=False, tp=None):
        def fn(e):
            kw = {"skip_group_check": True}
            if tp is not None:
                kw["tile_position"] = tp
            return e.matmul(out, lhsT, rhs, start=start, stop=stop, **kw)
        self.op("tensor", fn, reads, writes, inc)

    def tr(self, out, in_, ident, reads=(), writes=(), inc=False):
        self.op("tensor", lambda e: e.transpose(out, in_, ident), reads, writes, inc)

    def act(self, out, in_, func, reads=(), writes=(), bias=None, scale=None, accum_out=None):
        def fn(e):
            kw = {}
            if bias is not None:
                kw["bias"] = bias
            if scale is not None:
                kw["scale"] = scale
            if accum_out is not None:
                kw["accum_out"] = accum_out
            return e.activation(out=out, in_=in_, func=func, **kw)
        self.op("scalar", fn, reads, writes)

    def tt(self, out, in0, in1, op, reads=(), writes=(), eng="vector"):
        self.op(eng, lambda e: e.tensor_tensor(out=out, in0=in0, in1=in1, op=op), reads, writes)

    def ts(self, out, in0, s1, s2, op0, op1=None, reads=(), writes=(), eng="vector"):
        def fn(e):
            if op1 is None:
                return e.tensor_scalar(out=out, in0=in0, scalar1=s1, scalar2=None, op0=op0)
            return e.tensor_scalar(out=out, in0=in0, scalar1=s1, scalar2=s2, op0=op0, op1=op1)
        self.op(eng, fn, reads, writes)

    def stt(self, out, in0, scalar, in1, op0, op1, reads=(), writes=(), eng="vector"):
        self.op(eng, lambda e: e.scalar_tensor_tensor(out=out, in0=in0, scalar=scalar, in1=in1, op0=op0, op1=op1),
                reads, writes)

    def cp(self, out, in_, reads=(), writes=(), eng="vector"):
        self.op(eng, lambda e: e.tensor_copy(out=out, in_=in_), reads, writes)

    def ms(self, ap, val, reads=(), writes=(), eng="vector"):
        self.op(eng, lambda e: e.memset(ap, val), reads, writes)


class Deferred:
    def __init__(self):
        self.ops = []
        self.pos = 0

    def __getattr__(self, name):
        def rec(*a, **k):
            self.ops.append((name, a, k))
        return rec

    def replay(self, P, n=None, only_dma=False):
        cnt = 0
        while self.pos < len(self.ops) and (n is None or cnt < n):
            name, a, k = self.ops[self.pos]
            if only_dma and name != "dma":
                break
            getattr(P, name)(*a, **k)
            self.pos += 1
            cnt += 1


def build(stage="full"):
    nc = bass.Bass("TRN2", target_bir_lowering=False)

    def din(name, shape, dt=F32):
        return nc.dram_tensor(name, list(shape), dt, kind="ExternalInput").ap()

    x = din("x", [L, D])
    y = nc.dram_tensor("y", [L, D], F32, kind="ExternalOutput").ap()
    g1 = din("ffn1_norm", [1, D]); gm = din("mix_norm", [1, D]); g2 = din("ffn2_norm", [1, D])
    gfin = din("final_norm", [D])
    w1g = din("ffn1_w_gate", [1, D, FF]); w1u = din("ffn1_w_up", [1, D, FF]); w1d = din("ffn1_w_down", [1, FF, D])
    w2g = din("ffn2_w_gate", [1, D, FF]); w2u = din("ffn2_w_up", [1, D, FF]); w2d = din("ffn2_w_down", [1, FF, D])
    w_in = din("w_in", [1, D, 1536]); w_out = din("w_out", [1, D, D])
    lam_re = din("s5_lam_re", [1, 32, 64]); lam_im = din("s5_lam_im", [1, 32, 64]); log_dt = din("s5_log_dt", [1, 32])
    b_re = din("s5_b_re", [1, 32, 64, 16]); b_im = din("s5_b_im", [1, 32, 64, 16])
    c_re = din("s5_c_re", [1, 32, 16, 64]); c_im = din("s5_c_im", [1, 32, 16, 64])
    s5_d = din("s5_d", [1, 512]); w_glu = din("s5_w_glu", [1, 512, 512]); b_glu = din("s5_b_glu", [1, 512])
    cw = din("conv_w_dw", [1, 31, 512]); cb = din("conv_b_dw", [1, 512])
    cg = din("conv_ln_g", [1, 512]); cbt = din("conv_ln_b", [1, 512])
    c_idb = din("c_idb", [128, 128], BF16)
    c_idf = din("c_idf", [128, 128])
    c_i32 = din("c_i32", [128, 32])
    c_par = din("c_par", [128, 4])
    c_ramp = din("c_ramp", [128, 144])
    c_m64 = din("c_m64", [128, 128])

    with ExitStack() as st:
        P = Prog(nc, st)
        st.enter_context(nc.allow_non_contiguous_dma(reason="small one-time parameter layouts"))

        def sb(name, shape, dt):
            return st.enter_context(nc.sbuf_tensor(name, list(shape), dt))

        xbuf = sb("xbuf", [128, 2, NB, D], F32)
        x_sb = xbuf[:, 0]
        hT = sb("hT", [128, KD, TT], BF16)
        hid = sb("hid", [128, NF * TT], BF16)
        wd_sb = sb("wd_sb", [128, NF * D], BF16)
        wgu = sb("wgu", [128, NSLOT, 2, KD, 256], BF16)
        hn = sb("hn", [128, 2, D], BF16)
        sg = sb("sg", [128, 2, 512], BF16)
        sgf = sb("sgf", [128, 2, 512], F32)
        gcol = sb("gcol", [128, 3, KD], F32)
        gfb = sb("gfb", [128, D], F32)
        stat = sb("stat", [128, 4 * NB], F32)
        idb = sb("idb", [128, 128], BF16)
        idf = sb("idf", [128, 128], F32)
        i32 = sb("i32", [128, 32], F32)
        par = sb("par", [128, 4], F32)
        ramp = sb("ramp", [128, 144], F32)
        m64 = sb("m64", [128, 128], F32)
        EC = sb("EC", [128, 16, NC + 1], F32)
        ES = sb("ES", [128, 16, NC + 1], F32)
        BZR = sb("BZR", [128, 4, T, 128], BF16)
        BZI = sb("BZI", [128, 4, T, 128], BF16)
        CZR = sb("CZR", [128, 16, T, 32], BF16)
        CZI = sb("CZI", [128, 16, T, 32], BF16)
        KDS = sb("KDS", [128, 4, T, 32], BF16)
        MAGT = sb("MAGT", [128, 16], F32)
        carR = sb("carR", [128, 16], F32)
        carI = sb("carI", [128, 16], F32)
        wglu_sb = sb("wglu_sb", [128, 4, 512], BF16)
        bglu = sb("bglu", [128, 4], F32)
        wcol = sb("wcol", [128, 16, 8], F32)
        wdiag = sb("wdiag", [128, 16, 8, 32], BF16)
        cvec = sb("cvec", [128, 3, 4], F32)
        dcol = sb("dcol", [128, 4], F32)

        pbig = [st.enter_context(nc.psum_tensor("pb%d" % i, [128, 1024], F32)) for i in range(4)]

        def bank(k):
            return pbig[k // 2][:, (k % 2) * 512:(k % 2) * 512 + 512]

        R = {}

        def res(name):
            if name not in R:
                R[name] = Res(name)
            return R[name]

        rb = [res("bank%d" % k) for k in range(8)]
        rxs = [[res("x%d_%d" % (p_, b)) for b in range(NB)] for p_ in range(2)]
        rx = rxs[0]
        rhT = [res("hT%d" % t) for t in range(NTT)]
        rhid = [res("hid%d" % t) for t in range(NTT)]
        rslot = [res("slot%d" % s) for s in range(NSLOT)]
        rwd = res("wd")
        rwdt = res("wdtail")
        rhn = [res("hn0"), res("hn1")]
        rsg = [res("sg0"), res("sg1")]
        rsgf = [res("sgf0"), res("sgf1")]
        rstat = res("stat")
        rconst = res("const")
        rA = res("arenaA")
        rB = res("arenaB")

        ds_xs = [P.dsem("ds_x0"), P.dsem("ds_x1")]
        ds_y = P.dsem("ds_y")
        ds_c = P.dsem("ds_c")
        ds_s5 = P.dsem("ds_s5")
        ds_cv = P.dsem("ds_cv")
        ds_slot = [P.dsem("ds_slot%d" % s) for s in range(NSLOT)]
        ds_wd = P.dsem("ds_wd")
        ds_rep = P.dsem("ds_rep")

        def load_x(si):
            par_ = si % 2
            for b in range(NB):
                P.dma("sync", xbuf[:, par_, b, :], x[si * TT + b * 128:si * TT + (b + 1) * 128, :], ds_xs[par_],
                      writes=[rxs[par_][b]] + ([rs5] if (si == 1 and rs5 is not None) else []))

        load_x(0)

        def cload(dst, src):
            P.dma("sync", dst, src, ds_c, writes=[rconst])

        cload(idb[:], c_idb[:]); cload(idf[:], c_idf[:]); cload(i32[:], c_i32[:])
        cload(par[:], c_par[:]); cload(ramp[:], c_ramp[:]); cload(m64[:], c_m64[:])
        for n, g in enumerate((g1, gm, g2)):
            cload(gcol[:, n, :], g[0].rearrange("(k p) -> p k", p=128))
        cload(gfb[:], gfin.partition_broadcast(128))
        cload(bglu[:], b_glu[0].rearrange("(q p) -> p q", p=128))
        cload(dcol[:], s5_d[0].rearrange("(q p) -> p q", p=128))
        for n, v in enumerate((cb, cg, cbt)):
            cload(cvec[:, n, :], v[0].rearrange("(q p) -> p q", p=128))
        ds_wglu = P.dsem("ds_wglu")
        rwglu = res("wglu")
        P.dma("gpsimd", wglu_sb[:], w_glu[0].rearrange("(k p) n -> p k n", p=128), ds_wglu, writes=[rwglu])

        hidF = hid[:].bitcast(F32)
        wdF = wd_sb[:].bitcast(F32)

        def carveF(base, off, shape):
            n = int(np.prod(shape))
            v = base[:, off:off + n]
            if len(shape) == 2:
                v = v.rearrange("p (a b) -> p a b", a=shape[0])
            elif len(shape) == 3:
                v = v.rearrange("p (a b c) -> p a b c", a=shape[0], b=shape[1])
            elif len(shape) == 4:
                v = v.rearrange("p (a b c d) -> p a b c d", a=shape[0], b=shape[1], c=shape[2])
            return v, off + n

        def setup_s5(P):
            rs = res("s5setup")
            o = 0
            LR, o = carveF(wdF, o, [16]); LI, o = carveF(wdF, o, [16]); LDT, o = carveF(wdF, o, [16])
            DT, o = carveF(wdF, o, [16]); LRD, o = carveF(wdF, o, [16]); LID, o = carveF(wdF, o, [16])
            THE, o = carveF(wdF, o, [16])
            BR, o = carveF(wdF, o, [16, 16]); BI, o = carveF(wdF, o, [16, 16])
            BBR, o = carveF(wdF, o, [16, 16]); BBI, o = carveF(wdF, o, [16, 16])
            TB1, o = carveF(wdF, o, [16, 16]); TB2, o = carveF(wdF, o, [16, 16])
            ARG, o = carveF(wdF, o, [16, 9]); MAG, o = carveF(wdF, o, [16, 9])
            COS, o = carveF(wdF, o, [16, 9]); SIN, o = carveF(wdF, o, [16, 9])
            PR, o = carveF(wdF, o, [16, 9]); PIm, o = carveF(wdF, o, [16, 9])
            T1, o = carveF(wdF, o, [16]); T2, o = carveF(wdF, o, [16]); T3, o = carveF(wdF, o, [16])
            FR, o = carveF(wdF, o, [16]); FI, o = carveF(wdF, o, [16])
            S8, o = carveF(wdF, o, [16]); SH, o = carveF(wdF, o, [16]); C8, o = carveF(wdF, o, [16]); TQ, o = carveF(wdF, o, [16])
            CN_R, o = carveF(wdF, o, [4, 64]); CN_I, o = carveF(wdF, o, [4, 64])
            CIN_R, o = carveF(wdF, o, [4, 128]); CIN_I, o = carveF(wdF, o, [4, 128])
            CTR, o = carveF(wdF, o, [4, 128]); CTI, o = carveF(wdF, o, [4, 128])
            xF = xbuf[:, 1].rearrange("p b d -> p (b d)")
            ox = 0
            ET1, ox = carveF(xF, ox, [16, NC // 2]); ET2, ox = carveF(xF, ox, [16, NC // 2])
            ET3, ox = carveF(xF, ox, [16, NC // 2]); ET4, ox = carveF(xF, ox, [16, NC // 2])
            assert ox <= NB * D
            o2 = o
            VBR, o2 = carveF(wdF, o2, [4, T, 128]); VBI, o2 = carveF(wdF, o2, [4, T, 128])
            TV1, o2 = carveF(wdF, o2, [T, 4, 16])
            TC1, o2 = carveF(wdF, o2, [4, T, 32]); TC2, o2 = carveF(wdF, o2, [4, T, 32])
            assert o2 <= 11264, o2

            def ld(dst, src):
                P.dma("sync", dst, src, ds_s5, writes=[rs])

            for h in range(2):
                hs = slice(64 * h, 64 * h + 64)
                ld(LR[hs, :], lam_re[0, h::2, :].rearrange("g p -> p g"))
                ld(LI[hs, :], lam_im[0, h::2, :].rearrange("g p -> p g"))
                ld(LDT[hs, :], log_dt[0:1, h::2].to_broadcast([64, 16]))
                ld(BR[hs, :, :], b_re[0, h::2, :, :].rearrange("g p c -> p g c"))
                ld(BI[hs, :, :], b_im[0, h::2, :, :].rearrange("g p c -> p g c"))
            ld(CN_R, c_re[0].rearrange("(q g) c p -> (g c) q p", q=4))
            ld(CN_I, c_im[0].rearrange("(q g) c p -> (g c) q p", q=4))

            rw = dict(reads=[rs, rconst], writes=[rs])
            V = "vector"
            P.act(DT, LDT, AF.Exp, **rw)
            P.tt(LRD, LR, DT, ALU.mult, **rw)
            P.tt(LID, LI, DT, ALU.mult, **rw)
            P.ts(THE, LID, float(T), None, ALU.mult, **rw)
            rmp9 = ramp[:, 0:9].unsqueeze(1).to_broadcast([128, 16, 9])
            P.tt(ARG, LRD.unsqueeze(2).to_broadcast([128, 16, 9]), rmp9, ALU.mult, **rw)
            P.act(MAG, ARG, AF.Exp, **rw)

            def cmul(oR, oI, aR, aI, bR, bI, t1, t2, t3, t4):
                P.tt(t1, aR, bR, ALU.mult, **rw)
                P.tt(t2, aI, bI, ALU.mult, **rw)
                P.tt(t3, aR, bI, ALU.mult, **rw)
                P.tt(t4, aI, bR, ALU.mult, **rw)
                P.tt(oR, t1, t2, ALU.subtract, **rw)
                P.tt(oI, t3, t4, ALU.add, **rw)

            P.act(S8, LID, AF.Sin, scale=1.0 / 8, **rw)
            P.act(SH, LID, AF.Sin, scale=1.0 / 16, **rw)
            P.tt(C8, SH, SH, ALU.mult, **rw)
            P.ts(C8, C8, -2.0, 1.0, ALU.mult, ALU.add, **rw)
            for _ in range(3):
                P.tt(T1, C8, C8, ALU.mult, **rw)
                P.tt(T2, S8, S8, ALU.mult, **rw)
                P.tt(T3, C8, S8, ALU.mult, **rw)
                P.tt(C8, T1, T2, ALU.subtract, **rw)
                P.ts(S8, T3, 2.0, None, ALU.mult, **rw)
            P.ms(COS[:, :, 0], 1.0, **rw)
            P.ms(SIN[:, :, 0], 0.0, **rw)
            P.ms(COS[:, :, T + 1:9], 1.0, **rw)
            P.ms(SIN[:, :, T + 1:9], 0.0, **rw)
            for d in range(1, T + 1):
                cmul(COS[:, :, d], SIN[:, :, d], COS[:, :, d - 1], SIN[:, :, d - 1], C8, S8, T1, T2, T3, TQ)
            P.tt(PR, MAG, COS, ALU.mult, **rw)
            P.tt(PIm, MAG, SIN, ALU.mult, **rw)
            P.cp(MAGT[:], MAG[:, :, T], **rw)
            P.ms(EC[:, :, 0], 1.0, **rw)
            P.ms(ES[:, :, 0], 0.0, **rw)
            P.cp(EC[:, :, 1], COS[:, :, T], **rw)
            P.cp(ES[:, :, 1], SIN[:, :, T], **rw)
            m = 1
            while m < NC:
                n = min(m, NC - m)
                bR = EC[:, :, m:m + 1].to_broadcast([128, 16, n])
                bI = ES[:, :, m:m + 1].to_broadcast([128, 16, n])
                cmul(EC[:, :, m + 1:m + 1 + n], ES[:, :, m + 1:m + 1 + n], EC[:, :, 1:1 + n], ES[:, :, 1:1 + n], bR, bI,
                     ET1[:, :, 0:n], ET2[:, :, 0:n], ET3[:, :, 0:n], ET4[:, :, 0:n])
                m += n
            P.ts(T1, PR[:, :, 1], -1.0, None, ALU.add, **rw)
            P.tt(T2, LR, LR, ALU.mult, **rw)
            P.tt(T3, LI, LI, ALU.mult, **rw)
            P.tt(T2, T2, T3, ALU.add, **rw)
            P.op(V, lambda e: e.reciprocal(out=T2, in_=T2), **rw)
            P.tt(FR, T1, LR, ALU.mult, **rw)
            P.tt(T3, PIm[:, :, 1], LI, ALU.mult, **rw)
            P.tt(FR, FR, T3, ALU.add, **rw)
            P.tt(FR, FR, T2, ALU.mult, **rw)
            P.tt(FI, PIm[:, :, 1], LR, ALU.mult, **rw)
            P.tt(T3, T1, LI, ALU.mult, **rw)
            P.tt(FI, FI, T3, ALU.subtract, **rw)
            P.tt(FI, FI, T2, ALU.mult, **rw)
            frb = FR.unsqueeze(2).to_broadcast([128, 16, 16])
            fib = FI.unsqueeze(2).to_broadcast([128, 16, 16])
            P.tt(BBR, BR, frb, ALU.mult, **rw)
            P.tt(TB1, BI, fib, ALU.mult, **rw)
            P.tt(BBR, BBR, TB1, ALU.subtract, **rw)
            P.tt(BBI, BI, frb, ALU.mult, **rw)
            P.tt(TB1, BR, fib, ALU.mult, **rw)
            P.tt(BBI, BBI, TB1, ALU.add, **rw)
            P.ms(VBR, 0.0, **rw)
            P.ms(VBI, 0.0, **rw)
            VBR5 = VBR.rearrange("p q d (g h c) -> p q d g h c", g=4, h=2)
            VBI5 = VBI.rearrange("p q d (g h c) -> p q d g h c", g=4, h=2)
            for q in range(4):
                for h in range(2):
                    hs = slice(64 * h, 64 * h + 64)
                    prb = PR[hs, 4 * q:4 * q + 4, 0:T].rearrange("p g d -> p d g").unsqueeze(3).to_broadcast([64, T, 4, 16])
                    pib = PIm[hs, 4 * q:4 * q + 4, 0:T].rearrange("p g d -> p d g").unsqueeze(3).to_broadcast([64, T, 4, 16])
                    bbr = BBR[hs, 4 * q:4 * q + 4, :].unsqueeze(1).to_broadcast([64, T, 4, 16])
                    bbi = BBI[hs, 4 * q:4 * q + 4, :].unsqueeze(1).to_broadcast([64, T, 4, 16])
                    oR = VBR5[hs, q, :, :, h, :]
                    oI = VBI5[hs, q, :, :, h, :]
                    t1 = TV1[hs]
                    P.tt(oR, prb, bbr, ALU.mult, **rw)
                    P.tt(t1, pib, bbi, ALU.mult, **rw)
                    P.tt(oR, oR, t1, ALU.subtract, **rw)
                    P.tt(oI, prb, bbi, ALU.mult, **rw)
                    P.tt(t1, pib, bbr, ALU.mult, **rw)
                    P.tt(oI, oI, t1, ALU.add, **rw)
            for (CN, CIN, pc) in ((CN_R, CIN_R, 0), (CN_I, CIN_I, 2)):
                P.ts(CIN[:, :, 0:64], CN, par[:, pc:pc + 1], None, ALU.mult, **rw)
                P.ts(CIN[:, :, 64:128], CN, par[:, pc + 1:pc + 2], None, ALU.mult, **rw)
            for (CIN, CT) in ((CIN_R, CTR), (CIN_I, CTI)):
                for q in range(4):
                    P.mm(bank(4)[:, q * 128:(q + 1) * 128], CIN[:, q, :], idf[:], True, True,
                         reads=[rs, rconst], writes=[rb[4]], inc=(q == 3))
                P.cp(CT, bank(4).rearrange("p (q c) -> p q c", q=4), reads=[rb[4]], writes=[rb[4], rs])
            for (VB, BZ) in ((VBR, BZR), (VBI, BZI)):
                for q in range(4):
                    for dh in range(T // 4):
                        bk = 5 + (dh % 2)
                        for dd in range(4):
                            d = dh * 4 + dd
                            P.mm(bank(bk)[:, dd * 128:(dd + 1) * 128], VB[:, q, d, :], idf[:], True, True,
                                 reads=[rs, rconst], writes=[rb[bk]], inc=(dd == 3))
                        for dd in range(4):
                            d = dh * 4 + dd
                            P.cp(BZ[:, q, T - 1 - d, :], bank(bk)[:, dd * 128:(dd + 1) * 128],
                                 reads=[rb[bk]], writes=[rb[bk], rs])
            for q in range(4):
                for d in range(T):
                    bk = 6 + (d // 4)
                    for g4 in range(4):
                        col = (d % 4) * 128 + g4 * 32
                        o_ = bank(bk)[32 * g4:32 * g4 + 32, col:col + 32]
                        last = (g4 == 3 and d % 4 == 3)
                        P.mm(o_, VBR[:, q, d, 32 * g4:32 * g4 + 32], CTR[:, q, 32 * g4:32 * g4 + 32], True, False,
                             reads=[rs], writes=[rb[bk]], tp=(0, 32 * g4))
                        P.mm(o_, VBI[:, q, d, 32 * g4:32 * g4 + 32], CTI[:, q, 32 * g4:32 * g4 + 32], False, True,
                             reads=[rs], writes=[rb[bk]], tp=(0, 32 * g4), inc=last)
                for dhh in range(T // 4):
                    bk = 6 + dhh
                    src = bank(bk).rearrange("p (d g c) -> p d g c", d=4, g=4)
                    for g4 in range(4):
                        ps_ = slice(32 * g4, 32 * g4 + 32)
                        if dhh == 0:
                            P.stt(KDS[ps_, q, 0, :], i32[ps_, :], dcol[ps_, q:q + 1], src[ps_, 0, g4, :],
                                  ALU.mult, ALU.add, reads=[rb[bk], rconst], writes=[rb[bk], rs])
                            P.cp(KDS[ps_, q, 1:4, :], src[ps_, 1:4, g4, :], reads=[rb[bk]], writes=[rb[bk], rs])
                        else:
                            P.cp(KDS[ps_, q, 4:8, :], src[ps_, :, g4, :], reads=[rb[bk]], writes=[rb[bk], rs])
            for q in range(4):
                ctr = CTR[:, q, :].rearrange("p (g c) -> p g c", g=4).unsqueeze(2).to_broadcast([128, 4, T, 32])
                cti = CTI[:, q, :].rearrange("p (g c) -> p g c", g=4).unsqueeze(2).to_broadcast([128, 4, T, 32])
                prb = PR[:, 4 * q:4 * q + 4, 1:T + 1].unsqueeze(3).to_broadcast([128, 4, T, 32])
                pib = PIm[:, 4 * q:4 * q + 4, 1:T + 1].unsqueeze(3).to_broadcast([128, 4, T, 32])
                P.tt(TC1, ctr, prb, ALU.mult, **rw)
                P.tt(TC2, cti, pib, ALU.mult, **rw)
                P.tt(CZR[:, 4 * q:4 * q + 4, :, :], TC1, TC2, ALU.add, **rw)
                P.tt(TC1, cti, prb, ALU.mult, **rw)
                P.tt(TC2, ctr, pib, ALU.mult, **rw)
                P.tt(CZI[:, 4 * q:4 * q + 4, :, :], TC1, TC2, ALU.subtract, **rw)
            P.ms(carR[:], 0.0, **rw)
            P.ms(carI[:], 0.0, **rw)
            return rs

        def setup_conv():
            rs = res("convsetup")
            P.ms(wcol[:], 0.0, reads=[], writes=[rs])
            for s in range(4):
                for r in range(8):
                    if 4 * r + s > 30:
                        continue
                    P.dma("sync", wcol[32 * s:32 * s + 32, :, r],
                          cw[0, 4 * r + s, :].rearrange("(g c) -> c g", c=32), ds_cv, reads=[], writes=[rs])
            return rs

        def setup_conv_late(rs):
            P.tt(wdiag[:], wcol[:].unsqueeze(3).to_broadcast([128, 16, 8, 32]),
                 i32[:].unsqueeze(1).unsqueeze(1).to_broadcast([128, 16, 8, 32]), ALU.mult,
                 reads=[rs, rconst], writes=[rs])

        stat2 = sb("stat2", [128, 2 * NB], F32)
        rstat2 = res("stat2")

        def norm_to_hT(nidx, xs=None, rxl=None, bank0=6, stt_=None, rst_=None):
            xs = x_sb if xs is None else xs
            rxl = rx if rxl is None else rxl
            stt_ = stat if stt_ is None else stt_
            rst_ = rstat if rst_ is None else rst_
            for b in range(NB):
                s = b % 2
                P.act(sgf[:, s, :].bitcast(BF16), xs[:, b, :], AF.Square, accum_out=stt_[:, b:b + 1],
                      reads=[rxl[b]], writes=[rsgf[s], rst_])
            rs_all = stt_[:, NB:2 * NB]
            P.ts(rs_all, stt_[:, 0:NB], 1.0 / D, EPS, ALU.mult, ALU.add, reads=[rst_], writes=[rst_])
            P.act(rs_all, rs_all, AF.Sqrt, reads=[rst_], writes=[rst_])
            P.op("vector", lambda e: e.reciprocal(out=rs_all, in_=rs_all), reads=[rst_], writes=[rst_])
            for b in range(NB):
                s = b % 2
                rstd = stt_[:, NB + b:NB + b + 1]
                P.ts(hn[:, s, :], xs[:, b, :], rstd, None, ALU.mult, reads=[rxl[b], rst_], writes=[rhn[s]])
                bk = bank0 + s
                pt = bank(bk).bitcast(BF16)
                for k in range(KD):
                    P.tr(pt[:, k * 128:(k + 1) * 128], hn[:, s, k * 128:(k + 1) * 128], idb[:],
                         reads=[rhn[s], rconst], writes=[rb[bk]], inc=(k == KD - 1))
                tt_ = b // 4
                P.tt(hT[:, :, b * 128:(b + 1) * 128], pt.rearrange("p (k t) -> p k t", k=KD),
                     gcol[:, nidx, :].unsqueeze(2).to_broadcast([128, KD, 128]), ALU.mult,
                     reads=[rb[bk], rconst], writes=[rb[bk], rhT[tt_]])

        scr_gu = [nc.dram_tensor("scr_gu%d" % i, [NF // 2, 128, 2 * KD * 256], BF16).ap() for i in range(2)]
        scr_wd = [nc.dram_tensor("scr_wd%d" % i, [128, NF * D], BF16).ap() for i in range(2)]
        rscr_gu = [[res("scrgu%d_%d" % (i, fp)) for fp in range(NF // 2)] for i in range(2)]
        rscr_wd = [res("scrwd%d" % i) for i in range(2)]
        ds_scr = P.dsem("ds_scr")

        ds_cvt = P.dsem("ds_cvt")

        def convert_ffn(fi_, wg, wu, wdn):
            wgv = wg[0].rearrange("(k p) n -> p k n", p=128)
            wuv = wu[0].rearrange("(k p) n -> p k n", p=128)
            wdv = wdn[0].rearrange("(f p) n -> p f n", p=128)
            for fp in range(NF // 2):
                dst = scr_gu[fi_][fp].rearrange("p (g k n) -> p g k n", g=2, k=KD)
                P.dma("gpsimd", dst[:, 0], wgv[:, :, fp * 256:(fp + 1) * 256], ds_cvt, writes=[rscr_gu[fi_][fp]])
                P.dma("gpsimd", dst[:, 1], wuv[:, :, fp * 256:(fp + 1) * 256], ds_cvt, writes=[rscr_gu[fi_][fp]])
            dstw = scr_wd[fi_].rearrange("p (f n) -> p f n", f=NF)
            P.dma("gpsimd", dstw[:, 0:11, :], wdv[:, 0:11, :], ds_cvt, writes=[rscr_wd[fi_]])
            P.dma("gpsimd", dstw[:, 11:22, :], wdv[:, 11:22, :], ds_cvt, writes=[rscr_wd[fi_]])

        def ffn(wg, wu, wdn, first_wd_dep, fi_, tile_i, mid_hook=None, hid_dep=(), interleave=None, wd_late=False):
            hid3 = hid[:].rearrange("p (f t) -> p f t", f=NF)
            wd3 = wd_sb[:].rearrange("p (f n) -> p f n", f=NF)
            wgv = wg[0].rearrange("(k p) n -> p k n", p=128)
            wuv = wu[0].rearrange("(k p) n -> p k n", p=128)
            wdv = wdn[0].rearrange("(f p) n -> p f n", p=128)

            NP_ = NF // 2

            def load_p(fp):
                s = fp % NSLOT
                flat = wgu[:, s].rearrange("p g k n -> p (g k n)")
                if tile_i == 0 and fi_ == 0:
                    P.dma("gpsimd", wgu[:, s, 0, :, :], wgv[:, :, fp * 256:(fp + 1) * 256], ds_slot[s], writes=[rslot[s]])
                    P.dma("gpsimd", wgu[:, s, 1, :, :], wuv[:, :, fp * 256:(fp + 1) * 256], ds_slot[s], writes=[rslot[s]])
                    P.dma("sync", scr_gu[fi_][fp], flat, ds_scr, reads=[rslot[s]], writes=[rscr_gu[fi_][fp]])
                else:
                    P.dma("gpsimd", flat, scr_gu[fi_][fp], ds_slot[s], reads=[rscr_gu[fi_][fp]], writes=[rslot[s]])

            for fp in range(min(NSLOT, NP_)):
                load_p(fp)
            def load_wd():
                if tile_i == 0 and fi_ == 0:
                    P.dma("gpsimd", wd3[:, 0:11, :], wdv[:, 0:11, :], ds_wd, writes=[rwd, rwdt] + list(first_wd_dep))
                    P.dma("gpsimd", wd3[:, 11:22, :], wdv[:, 11:22, :], ds_wd, writes=[rwd, rwdt])
                    P.dma("sync", scr_wd[fi_], wd_sb[:], ds_scr, reads=[rwd, rwdt], writes=[rscr_wd[fi_]])
                else:
                    P.dma("gpsimd", wd_sb[:], scr_wd[fi_], ds_wd, reads=[rscr_wd[fi_]],
                          writes=[rwd, rwdt] + list(first_wd_dep))

            if not wd_late:
                load_wd()
            it = 0
            for fp in range(NP_):
                s = fp % NSLOT
                for fi in range(2):
                    f = 2 * fp + fi
                    for t in range(NTT):
                        pa = (it % 2) * 2
                        it += 1
                        tsl = slice(t * 512, (t + 1) * 512)
                        for gu in range(2):
                            for k in range(KD):
                                P.mm(bank(pa + gu), wgu[:, s, gu, k, fi * 128:(fi + 1) * 128], hT[:, k, tsl], k == 0, k == KD - 1,
                                     reads=[rslot[s], rhT[t]], writes=[rb[pa + gu]], inc=(k == KD - 1))
                        ss = (it - 1) % 2
                        P.act(sg[:, ss, :], bank(pa), AF.Silu, reads=[rb[pa]], writes=[rb[pa], rsg[ss]])
                        P.tt(hid3[:, f, tsl], bank(pa + 1), sg[:, ss, :], ALU.mult,
                             reads=[rb[pa + 1], rsg[ss]], writes=[rb[pa + 1], rhid[t], rA] + list(hid_dep))
                    if interleave is not None:
                        interleave()
                if fp + NSLOT < NP_:
                    load_p(fp + NSLOT)
            if mid_hook is not None:
                mid_hook()
            if wd_late:
                load_wd()
            for b in range(NB):
                t = b // 4
                pa = 4 + (b % 2) * 2
                for dh in range(2):
                    for f in range(NF):
                        P.mm(bank(pa + dh), hid3[:, f, b * 128:(b + 1) * 128], wd3[:, f, dh * 512:(dh + 1) * 512],
                             f == 0, f == NF - 1, reads=[rhid[t], rwd, rwdt], writes=[rb[pa + dh]], inc=(f == NF - 1))
                P.stt(x_sb[:, b, :], pbig[pa // 2][:, :], 0.5, x_sb[:, b, :], ALU.mult, ALU.add,
                      reads=[rb[pa], rb[pa + 1], rx[b]], writes=[rb[pa], rb[pa + 1], rx[b]])

        WOUT = wd_sb[:, 0:KD * D].rearrange("p (k n) -> p k n", k=KD)
        winv = w_in[0].rearrange("(k p) n -> p k n", p=128)
        oA = 0
        US5 = hid[:, oA:oA + 4 * TT].rearrange("p (q t) -> p q t", q=4); oA += 4 * TT
        ZB = hid[:, oA:oA + 4 * ZW].rearrange("p (q t) -> p q t", q=4); oA += 4 * ZW
        YG = hid[:, oA:oA + 4 * TT].rearrange("p (q t) -> p q t", q=4); oA += 4 * TT
        YCAT = hid[:, oA:oA + 8 * TT].rearrange("p (q t) -> p q t", q=8); oA += 8 * TT
        assert oA <= NF * TT, oA
        ZOFF = KD * D
        ZREP = wd_sb[:, ZOFF:ZOFF + 16 * RW].rearrange("p (g t) -> p g t", g=16)
        assert ZOFF + 16 * RW <= NF * D
        hTF = hT[:].rearrange("p k t -> p (k t)").bitcast(F32)
        oZ = 0
        WRe, oZ = carveF(hTF, oZ, [8, NC]); WIm, oZ = carveF(hTF, oZ, [8, NC])
        assert oZ <= KD * TT // 2, oZ
        ROFF = ZOFF + 16 * RW
        tailF = wd_sb[:, ROFF:NF * D].bitcast(F32)
        oZ = 0
        RRe, oZ = carveF(tailF, oZ, [8, NC + 1]); RIm, oZ = carveF(tailF, oZ, [8, NC + 1])
        assert oZ <= (NF * D - ROFF) // 2, oZ
        smallT = sb("smallT", [128, 4, 16], F32)
        SRb = sb("SRb", [128, 16, NC + 1], BF16)
        SIb = sb("SIb", [128, 16, NC + 1], BF16)
        zhist = sb("zhist", [128, 4, 32], BF16)
        rzh = res("zhist")
        czf = hn[:].rearrange("p s d -> p (s d)").bitcast(F32).rearrange("p (s d) -> p s d", s=2)
        rZB = res("zbuf"); rZBh = [res("zbuf_h0"), res("zbuf_h1")]; rZREP = res("zrep")
        rZREPh = [res("zrep_h0"), res("zrep_h1")]; ds_reph = [P.dsem("ds_rep0"), P.dsem("ds_rep1")]; rUS5 = res("us5"); rYG = res("yg"); rYC = [res("ycat%d" % t) for t in range(NTT)]
        rW = res("wrot"); rRR = res("rr"); rS = res("sfull"); rTM = res("tm"); rczf = rhn
        rcar = res("carry")

        def mixer(st_i, rs5, rcv):
            for fp in range(6):
                P.dma("gpsimd", wgu[:, fp % 3, fp // 3, :, :], winv[:, :, fp * 256:(fp + 1) * 256], ds_slot[fp % 3],
                      writes=[rslot[fp % 3]])
            P.dma("gpsimd", WOUT, w_out[0].rearrange("(k p) n -> p k n", p=128), ds_wd, writes=[rwd, rB])
            if stage == "full" and st_i == 0:
                convert_ffn(1, w2g, w2u, w2d)
            if st_i == 0:
                P.ms(zhist[:], 0.0, writes=[rzh])
            P.cp(ZB[:, :, 0:32], zhist[:], reads=[rzh], writes=[rZB, rZBh[0], rZBh[1], rA])
            P.ms(ZB[:, :, 32 + TT:ZW], 0.0, writes=[rZB, rZBh[0], rZBh[1], rA])
            it = 0
            for q in range(4):
                for t in range(NTT):
                    b1 = (it % 2) * 2
                    it += 1
                    tsl = slice(t * 512, (t + 1) * 512)
                    for hh, fp in enumerate((2 + q // 2, 4 + q // 2)):
                        sl, hf, fi = fp % 3, fp // 3, q % 2
                        for k in range(KD):
                            P.mm(bank(b1 + hh), wgu[:, sl, hf, k, fi * 128:(fi + 1) * 128], hT[:, k, tsl], k == 0, k == KD - 1,
                                 reads=[rslot[sl], rhT[t]], writes=[rb[b1 + hh]], inc=(k == KD - 1))
                    ss = it % 2
                    P.act(sg[:, ss, :], bank(b1 + 1), AF.Sigmoid, reads=[rb[b1 + 1]], writes=[rb[b1 + 1], rsg[ss]])
                    P.tt(ZB[:, q, 32 + t * 512:32 + (t + 1) * 512], bank(b1), sg[:, ss, :], ALU.mult,
                         reads=[rb[b1], rsg[ss]], writes=[rb[b1], rZBh[q // 2], rA])
                if q % 2 == 1:
                    hq = q // 2
                    for s in range(4):
                        for g4 in range(4):
                            g0 = 8 * hq + g4
                            P.dma("sync", ZREP[32 * s:32 * s + 32, g0:g0 + 5:4, :],
                                  ZB[32 * g4:32 * g4 + 32, 2 * hq:2 * hq + 2, s:s + RW], ds_reph[hq],
                                  reads=[rZBh[hq]], writes=[rZREPh[hq], rwdt])
            rUS5h = [res("us5_h0"), res("us5_h1")]

            def win_s5(cq):
                fp, fi = cq // 2, cq % 2
                sl, hf = fp % 3, fp // 3
                t = 0
                bk = 4 + cq
                tsl = slice(t * 512, (t + 1) * 512)
                for k in range(KD):
                    P.mm(bank(bk), wgu[:, sl, hf, k, fi * 128:(fi + 1) * 128], hT[:, k, tsl], k == 0, k == KD - 1,
                         reads=[rslot[sl], rhT[t]], writes=[rb[bk]], inc=(k == KD - 1))
                P.act(US5[:, cq, tsl], bank(bk), AF.Copy, reads=[rb[bk]], writes=[rb[bk], rUS5h[cq // 2], rUS5, rA])

            def z_half(h):
                for part, BZ in enumerate((BZR, BZI)):
                    for ql in range(2):
                        q = 2 * h + ql
                        c0 = part * 2 * NC + ql * NC
                        for i in range(T):
                            for g4 in range(4):
                                bk = 4 * h + g4
                                P.mm(bank(bk)[:, c0:c0 + NC], BZ[32 * g4:32 * g4 + 32, q, i, :],
                                     US5[32 * g4:32 * g4 + 32, q, i::T], i == 0, i == T - 1,
                                     reads=[rUS5h[h], rs5], writes=[rb[bk]], tp=(32 * g4, 0),
                                     inc=(i == T - 1 and part == 1 and ql == 1))

            win_s5(0)
            win_s5(1)
            z_half(0)
            win_s5(2)
            win_s5(3)
            z_half(1)
            rSh = [res("sfull_h0"), res("sfull_h1")]

            def s5_half(h):
                ECv = EC[:, 8 * h:8 * h + 8, :].rearrange("p (q g) k -> p g q k", g=4)
                ESv = ES[:, 8 * h:8 * h + 8, :].rearrange("p (q g) k -> p g q k", g=4)
                WRv = WRe.rearrange("p (q g) k -> p g q k", g=4)
                WIv = WIm.rearrange("p (q g) k -> p g q k", g=4)
                RRv = RRe.rearrange("p (q g) k -> p g q k", g=4)
                RIv = RIm.rearrange("p (q g) k -> p g q k", g=4)
                for a in range(2):
                    b0 = 4 * h + 2 * a
                    zz = pbig[b0 // 2][:, :].rearrange("p (b x) -> p b x", b=2)
                    zr = zz[:, :, 0:2 * NC].rearrange("p b (q c) -> p b q c", q=2)
                    zi = zz[:, :, 2 * NC:4 * NC].rearrange("p b (q c) -> p b q c", q=2)
                    ec = ECv[:, 2 * a:2 * a + 2, :, 1:NC + 1]
                    es = ESv[:, 2 * a:2 * a + 2, :, 1:NC + 1]
                    wr = WRv[:, 2 * a:2 * a + 2, :, :]
                    wi = WIv[:, 2 * a:2 * a + 2, :, :]
                    t1 = RRv[:, 2 * a:2 * a + 2, :, 1:NC + 1]
                    t2 = RIv[:, 2 * a:2 * a + 2, :, 1:NC + 1]
                    dep = dict(reads=[rb[b0], rb[b0 + 1], rs5, rS], writes=[rb[b0], rb[b0 + 1], rW, rRR, rhT[0]])
                    P.tt(wr, zr, ec, ALU.mult, **dep)
                    P.tt(t1, zi, es, ALU.mult, **dep)
                    P.tt(wr, wr, t1, ALU.add, **dep)
                    P.tt(wi, zi, ec, ALU.mult, **dep)
                    P.tt(t2, zr, es, ALU.mult, **dep)
                    P.tt(wi, wi, t2, ALU.subtract, **dep)
                gs = slice(8 * h, 8 * h + 8)
                P.cp(RRe[:, :, 0], carR[:, gs], reads=[rcar, rW], writes=[rRR])
                P.cp(RIm[:, :, 0], carI[:, gs], reads=[rcar, rW], writes=[rRR])
                for gl in range(8):
                    gp = 8 * h + gl
                    mg = MAGT[:, gp:gp + 1].to_broadcast([128, NC])
                    for (RX, WX, CAR) in ((RRe, WRe, carR), (RIm, WIm, carI)):
                        P.op("vector", lambda e, RX=RX, WX=WX, CAR=CAR, gp=gp, gl=gl, mg=mg: e.tensor_tensor_scan(
                            out=RX[:, gl, 1:NC + 1], data0=mg, data1=WX[:, gl, :], initial=CAR[:, gp:gp + 1],
                            op0=ALU.mult, op1=ALU.add), reads=[rW, rRR, rcar, rs5, rhT[0]], writes=[rRR])
                dep = dict(reads=[rRR, rs5, rW, rhT[0]], writes=[rSh[h], rS, rW])
                TM1 = WRe
                TM2 = WIm
                P.tt(TM1, EC[:, gs, 0:NC], RRe[:, :, 0:NC], ALU.mult, **dep)
                P.tt(TM2, ES[:, gs, 0:NC], RIm[:, :, 0:NC], ALU.mult, **dep)
                P.tt(SRb[:, gs, 0:NC], TM1, TM2, ALU.subtract, **dep)
                P.tt(TM1, ES[:, gs, 0:NC], RRe[:, :, 0:NC], ALU.mult, **dep)
                P.tt(TM2, EC[:, gs, 0:NC], RIm[:, :, 0:NC], ALU.mult, **dep)
                P.tt(SIb[:, gs, 0:NC], TM1, TM2, ALU.add, **dep)
                depc = dict(reads=[rRR, rs5], writes=[rTM])
                P.tt(smallT[:, 0, gs], EC[:, gs, NC], RRe[:, :, NC], ALU.mult, **depc)
                P.tt(smallT[:, 1, gs], ES[:, gs, NC], RIm[:, :, NC], ALU.mult, **depc)
                P.tt(smallT[:, 2, gs], ES[:, gs, NC], RRe[:, :, NC], ALU.mult, **depc)
                P.tt(smallT[:, 3, gs], EC[:, gs, NC], RIm[:, :, NC], ALU.mult, **depc)
                P.tt(carR[:, gs], smallT[:, 0, gs], smallT[:, 1, gs], ALU.subtract, reads=[rTM], writes=[rcar])
                P.tt(carI[:, gs], smallT[:, 2, gs], smallT[:, 3, gs], ALU.add, reads=[rTM], writes=[rcar])

            def conv_a(q):
                t = 0
                bk = q % 2
                cs = q % 2
                bm, bv = 2, 3
                for r in range(8):
                    for g4 in range(4):
                        grp = 4 * q + g4
                        c0 = 2 + 4 * r + t * 512
                        P.mm(bank(bk)[32 * g4:32 * g4 + 32, :], wdiag[:, grp, r, :], ZREP[:, grp, c0:c0 + 512],
                             r == 0, r == 7, reads=[rZREPh[q // 2], rcv], writes=[rb[bk]], tp=(0, 32 * g4),
                             inc=(r == 7 and g4 == 3))
                tsl = slice(t * 512, (t + 1) * 512)
                P.act(czf[:, cs, :], bank(bk), AF.Identity, bias=cvec[:, 0, q:q + 1],
                      reads=[rb[bk], rconst], writes=[rb[bk], rczf[cs]])
                P.act(sgf[:, cs, :], czf[:, cs, :], AF.Square, reads=[rczf[cs]], writes=[rsgf[cs]])
                P.mm(bank(bm), m64[:], czf[:, cs, :], True, True, reads=[rczf[cs], rconst], writes=[rb[bm]], inc=True)
                P.mm(bank(bv), m64[:], sgf[:, cs, :], True, True, reads=[rsgf[cs], rconst], writes=[rb[bv]], inc=True)
                P.tt(czf[:, cs, :], czf[:, cs, :], bank(bm), ALU.subtract, reads=[rb[bm], rczf[cs]],
                     writes=[rb[bm], rczf[cs]])
                P.act(sgf[:, cs, :], bank(bm), AF.Square, reads=[rb[bm]], writes=[rb[bm], rsgf[cs]])
                P.stt(sgf[:, cs, :], bank(bv), EPS, sgf[:, cs, :], ALU.add, ALU.subtract, reads=[rb[bv], rsgf[cs]],
                      writes=[rb[bv], rsgf[cs]])

            def conv_sqrt(q):
                cs = q % 2
                P.act(sgf[:, cs, :], sgf[:, cs, :], AF.Sqrt, reads=[rsgf[cs]], writes=[rsgf[cs]])

            def conv_norm(q):
                cs = q % 2
                P.op("vector", lambda e, cs=cs: e.reciprocal(out=sgf[:, cs, :], in_=sgf[:, cs, :]),
                     reads=[rsgf[cs]], writes=[rsgf[cs]])
                P.tt(czf[:, cs, :], czf[:, cs, :], sgf[:, cs, :], ALU.mult, reads=[rczf[cs], rsgf[cs]],
                     writes=[rczf[cs]])

            def conv_silu(q):
                cs = q % 2
                t = 0
                tsl = slice(t * 512, (t + 1) * 512)
                P.act(YCAT[:, 4 + q, tsl], czf[:, cs, :], AF.Silu, scale=cvec[:, 1, q:q + 1], bias=cvec[:, 2, q:q + 1],
                      reads=[rczf[cs], rconst], writes=[rYC[t], rA])

            def conv_pair(q0):
                conv_a(q0)
                conv_a(q0 + 1)
                conv_sqrt(q0)
                conv_sqrt(q0 + 1)
                conv_norm(q0)
                conv_norm(q0 + 1)
                conv_silu(q0)
                conv_silu(q0 + 1)

            def y_q(q):
                rSq = rSh[q // 2]
                for g4 in range(4):
                    gp = 4 * q + g4
                    bk = g4
                    o_ = bank(bk)[:, q * NC:(q + 1) * NC]
                    P.mm(o_, CZR[:, gp].rearrange("p j c -> p (j c)"), SRb[:, gp, 0:NC], True, False,
                         reads=[rSq, rs5], writes=[rb[bk]])
                    P.mm(o_, CZI[:, gp].rearrange("p j c -> p (j c)"), SIb[:, gp, 0:NC], False, False,
                         reads=[rSq, rs5], writes=[rb[bk]])
                for j in range(T):
                    for i in range(j + 1):
                        for g4 in range(4):
                            bk = g4
                            o_ = bank(bk)[32 * j:32 * j + 32, q * NC:(q + 1) * NC]
                            P.mm(o_, KDS[32 * g4:32 * g4 + 32, q, j - i, :], US5[32 * g4:32 * g4 + 32, q, i::T],
                                 False, (i == j), reads=[rUS5h[q // 2], rs5], writes=[rb[bk]], tp=(32 * g4, 32 * j),
                                 inc=(i == j and j == T - 1))

            def y_evac(h):
                for g4 in range(4):
                    for j in range(T):
                        src = bank(g4)[32 * j:32 * j + 32, 2 * h * NC:(2 * h + 2) * NC].rearrange("p (q c) -> p q c", q=2)
                        dst = YG[32 * g4:32 * g4 + 32, 2 * h:2 * h + 2, j::T]
                        P.act(dst, src, AF.Gelu_apprx_tanh, reads=[rb[g4]], writes=[rb[g4], rYG, rA])

            s5_half(0)
            conv_pair(0)
            s5_half(1)
            y_q(0)
            y_q(1)
            y_evac(0)
            conv_pair(2)
            y_q(2)
            y_q(3)
            y_evac(1)
            P.cp(zhist[:], ZB[:, :, TT:TT + 32], reads=[rZB, rZBh[0], rZBh[1]], writes=[rzh])
            it = 0
            for cq in range(4):
                for t in range(NTT):
                    bk = it % 2
                    ss = it % 2
                    it += 1
                    tsl = slice(t * 512, (t + 1) * 512)
                    for k in range(4):
                        P.mm(bank(bk), wglu_sb[:, k, cq * 128:(cq + 1) * 128], YG[:, k, tsl], k == 0, k == 3,
                             reads=[rYG, rconst, rwglu], writes=[rb[bk]], inc=(k == 3))
                    P.act(sg[:, ss, :], bank(bk), AF.Sigmoid, bias=bglu[:, cq:cq + 1],
                          reads=[rb[bk], rconst], writes=[rb[bk], rsg[ss]])
                    P.tt(YCAT[:, cq, tsl], YG[:, cq, tsl], sg[:, ss, :], ALU.mult, reads=[rYG, rsg[ss]],
                         writes=[rYC[t], rA])
            for b in range(NB):
                t = b // 4
                pa = 4 + (b % 2) * 2
                for dh in range(2):
                    for k in range(8):
                        P.mm(bank(pa + dh), YCAT[:, k, b * 128:(b + 1) * 128], WOUT[:, k, dh * 512:(dh + 1) * 512],
                             k == 0, k == 7, reads=[rYC[t], rwd], writes=[rb[pa + dh]], inc=(k == 7))
                P.tt(x_sb[:, b, :], pbig[pa // 2][:, :], x_sb[:, b, :], ALU.add,
                     reads=[rb[pa], rb[pa + 1], rx[b]], writes=[rb[pa], rb[pa + 1], rx[b]])

        defer = Deferred()
        rs5 = setup_s5(defer) if stage != "ffn1" else None
        defer.replay(P, only_dma=True)
        rcv = setup_conv() if stage != "ffn1" else None
        for st_i in range(NST):
            t0 = st_i * TT
            x_sb = xbuf[:, st_i % 2]
            rx = rxs[st_i % 2]
            late_x = (st_i == 0 and rs5 is not None)
            if st_i + 1 < NST and not late_x:
                load_x(st_i + 1)
            if st_i == 0 or stage != "full":
                norm_to_hT(0)
            if st_i == 0 and rs5 is not None:
                nsl = (len(defer.ops) - defer.pos) // NF + 1
                ffn(w1g, w1u, w1d, [rA, rB, rs5], 0, st_i, hid_dep=[rA],
                    interleave=lambda: defer.replay(P, n=nsl),
                    mid_hook=lambda: defer.replay(P), wd_late=True)
                load_x(1)
            else:
                ffn(w1g, w1u, w1d, [rA, rB], 0, st_i, hid_dep=[rA, rB])
            if stage != "ffn1":
                if st_i == 0:
                    setup_conv_late(rcv)
                norm_to_hT(1)
                mixer(st_i, rs5, rcv)
            if stage == "full":
                norm_to_hT(2)
                hook = None
                if st_i + 1 < NST:
                    nxt = (st_i + 1) % 2
                    hook = (lambda nxt=nxt: norm_to_hT(0, xbuf[:, nxt], rxs[nxt], bank0=0, stt_=stat2, rst_=rstat2))
                ffn(w2g, w2u, w2d, [rA, rB], 1, st_i, mid_hook=hook, hid_dep=[rA, rB])
                for b in range(NB):
                    s = b % 2
                    P.act(sgf[:, s, :].bitcast(BF16), x_sb[:, b, :], AF.Square, accum_out=stat[:, 2 * NB + b:2 * NB + b + 1],
                          reads=[rx[b]], writes=[rsgf[s], rstat])
                rf_all = stat[:, 3 * NB:4 * NB]
                P.ts(rf_all, stat[:, 2 * NB:3 * NB], 1.0 / D, EPS, ALU.mult, ALU.add, reads=[rstat], writes=[rstat])
                P.act(rf_all, rf_all, AF.Sqrt, reads=[rstat], writes=[rstat])
                P.op("vector", lambda e: e.reciprocal(out=rf_all, in_=rf_all), reads=[rstat], writes=[rstat])
                for b in range(NB):
                    rstd = stat[:, 3 * NB + b:3 * NB + b + 1]
                    P.stt(x_sb[:, b, :], x_sb[:, b, :], rstd, gfb[:], ALU.mult, ALU.mult,
                          reads=[rx[b], rstat, rconst], writes=[rx[b]])
            for b in range(NB):
                P.dma("sync", y[t0 + b * 128:t0 + (b + 1) * 128, :], x_sb[:, b, :], ds_y, reads=[rx[b]])
        fin = (ds_y.sem, ds_y.count, "dma", ds_y)
        P._wait("sync", fin)

        with nc.Block() as block:
            @block.sync
            def _(e):
                for f in P.q["sync"]:
                    f(e)

            @block.scalar
            def _(e):
                for f in P.q["scalar"]:
                    f(e)

            @block.vector
            def _(e):
                for f in P.q["vector"]:
                    f(e)

            @block.gpsimd
            def _(e):
                for f in P.q["gpsimd"]:
                    f(e)

            @block.tensor
            def _(e):
                for f in P.q["tensor"]:
                    f(e)
    return nc


def make_consts():
    c = {}
    c["c_idb"] = np.eye(128, dtype=np.float32).astype(ml_dtypes.bfloat16)
    c["c_idf"] = np.eye(128, dtype=np.float32)
    p = np.arange(128)
    c["c_i32"] = (p[:, None] % 32 == np.arange(32)[None, :]).astype(np.float32)
    par = ((p // 16) % 2)
    c["c_par"] = np.stack([(par == 0), (par == 1), -1.0 * (par == 0), -1.0 * (par == 1)], 1).astype(np.float32)
    c["c_ramp"] = np.broadcast_to(np.arange(144, dtype=np.float32)[None, :], (128, 144)).copy()
    c["c_m64"] = ((p[:, None] // 64) == (p[None, :] // 64)).astype(np.float32) / 64.0
    return c


_NC_CACHE = {}


def kernel(**inputs):
    stage = "full"
    if stage not in _NC_CACHE:
        _NC_CACHE[stage] = build(stage)
    nc = _NC_CACHE[stage]
    consts = make_consts()
    shared = {k: np.ascontiguousarray(np.asarray(v)) for k, v in inputs.items() if k != "x"}
    x = np.asarray(inputs["x"])
    in_maps = []
    for b in range(8):
        m = dict(shared)
        m.update(consts)
        m["x"] = np.ascontiguousarray(x[b])
        in_maps.append(m)
    res = run_bass_kernel_spmd(nc, in_maps, core_ids=list(range(8)))
    return np.stack([r["y"] for r in res.results], 0).astype(np.float32)
```
